# Optimizing a Trainium2 kernel written in Bass

```python
import math
import jax, jax.numpy as jnp
from jax import lax
import numpy as np

D_MODEL = 2048
BATCH = 4
SEQ = 2048
DEPTH = 4
DEC_BATCH = 128
DEC_SEQ = 8
PAST_LEN = 16384
PAGE_SIZE = 128

D_MIX = D_MODEL
S5_WIDTH = D_MIX // 4
S5_GROUP = 16
S5_GROUPS = S5_WIDTH // S5_GROUP
S5_STATE = 64
RWKV_WIDTH = (3 * D_MIX) // 8
RWKV_HEAD = 64
RWKV_HEADS = RWKV_WIDTH // RWKV_HEAD
RWKV_DECAY_LORA = 64
RWKV_A_LORA = 64
RWKV_GATE_LORA = 128
RWKV_SHIFT_WIDTH = 3 * RWKV_WIDTH + RWKV_DECAY_LORA + RWKV_A_LORA + RWKV_GATE_LORA
RWKV_GN_EPS = 64e-5
HGRN_WIDTH = D_MIX - S5_WIDTH - RWKV_WIDTH
HGRN_EXPAND = 128
HGRN_HEADS = HGRN_WIDTH // HGRN_EXPAND
HGRN_HEAD_V = HGRN_WIDTH // HGRN_HEADS
HGRN_CHUNK = 64
HGRN_EPS = 1e-5
N_IN = S5_WIDTH + RWKV_SHIFT_WIDTH + 4 * HGRN_WIDTH
D_FF = ((8 * D_MODEL // 3 + 255) // 256) * 256
NORM_EPS = 1e-6

kernel_name = 'hybrid_s5_rwkv7_hgrn2_step'


def rms_norm(x, w, eps=NORM_EPS):
    x32 = x.astype(jnp.float32)
    y = x32 * lax.rsqrt(jnp.mean(x32 * x32, axis=-1, keepdims=True) + eps)
    return (y * w.astype(jnp.float32)).astype(x.dtype)


def swiglu_ffn(x, w_gate, w_up, w_down):
    return (jax.nn.silu(x @ w_gate) * (x @ w_up)) @ w_down


def s5_mixer(u, x0_re, x0_im, a_re, a_im, log_dt, b_re, b_im, c_re, c_im, d, w_glu, b_glu):
    n, t, _ = u.shape
    f32 = jnp.float32
    lam = lax.complex(a_re.astype(f32), a_im.astype(f32))
    dt = jnp.exp(log_dt.astype(f32))[:, None]
    a_bar = jnp.exp(lam * dt)
    b = lax.complex(b_re.astype(f32), b_im.astype(f32))
    b_bar = ((a_bar - 1.0) / lam)[..., None] * b
    c = lax.complex(c_re.astype(f32), c_im.astype(f32))
    u32 = u.astype(f32)
    ug = u32.reshape(n, t, S5_GROUPS, S5_GROUP).astype(jnp.complex64)
    bu = jnp.einsum('gph,ntgh->ntgp', b_bar, ug)
    x0 = lax.complex(x0_re.astype(f32), x0_im.astype(f32))
    bu = bu.at[:, 0].add(a_bar * x0)
    a_seq = jnp.broadcast_to(a_bar, bu.shape)

    def combine(e1, e2):
        a1, b1 = e1
        a2, b2 = e2
        return a2 * a1, a2 * b1 + b2

    _, xs = lax.associative_scan(combine, (a_seq, bu), axis=1)
    y = jnp.real(jnp.einsum('ghp,ntgp->ntgh', c, xs)).reshape(n, t, S5_WIDTH)
    y = jax.nn.gelu(y + d.astype(f32) * u32)
    out = y * jax.nn.sigmoid(y @ w_glu.astype(f32) + b_glu.astype(f32))
    x_last = xs[:, -1]
    return out, jnp.real(x_last), jnp.imag(x_last)


def rwkv7_mixer(z, shift0, s0, mu, w0, w2, a0, a2, g2, k_k, k_a, r_k, ln_w, ln_b):
    n, t, _ = z.shape
    f32 = jnp.float32
    z = z.astype(f32)
    prev = jnp.concatenate([shift0.astype(f32)[:, None], z[:, :-1]], axis=1)
    zm = z + (prev - z) * mu.astype(f32)
    o1 = 3 * RWKV_WIDTH
    o2 = o1 + RWKV_DECAY_LORA
    o3 = o2 + RWKV_A_LORA
    r = zm[..., :RWKV_WIDTH]
    k = zm[..., RWKV_WIDTH:2 * RWKV_WIDTH]
    v = zm[..., 2 * RWKV_WIDTH:o1]
    wi, ai, gi = zm[..., o1:o2], zm[..., o2:o3], zm[..., o3:]
    w = -jax.nn.softplus(-(w0 + jnp.tanh(wi) @ w2)) - 0.5
    decay = jnp.exp(-jnp.exp(w))
    a = jax.nn.sigmoid(a0 + ai @ a2)
    g = jax.nn.sigmoid(gi) @ g2

    def heads(y):
        return y.reshape(n, t, RWKV_HEADS, RWKV_HEAD)

    kk = heads(k * k_k)
    kk = kk * lax.rsqrt(jnp.maximum(jnp.sum(kk * kk, axis=-1, keepdims=True), 1e-24))
    k = k * (1.0 + (a - 1.0) * k_a)
    rh, kh, vh, dh, ah = heads(r), heads(k), heads(v), heads(decay), heads(a)
    bh = kk * ah

    def step(S, inp):
        r_t, k_t, v_t, d_t, kk_t, b_t = inp
        sa = jnp.einsum('nhvk,nhk->nhv', S, -kk_t)
        S = S * d_t[:, :, None, :] + sa[..., None] * b_t[:, :, None, :] + v_t[..., None] * k_t[:, :, None, :]
        return S, jnp.einsum('nhvk,nhk->nhv', S, r_t)

    seq = tuple(jnp.moveaxis(y_, 1, 0) for y_ in (rh, kh, vh, dh, kk, bh))
    s_last, ys = lax.scan(step, s0.astype(f32), seq)
    y = jnp.moveaxis(ys, 0, 1)
    mean = jnp.mean(y, axis=-1, keepdims=True)
    var = jnp.mean(jnp.square(y - mean), axis=-1, keepdims=True)
    y = ((y - mean) * lax.rsqrt(var + RWKV_GN_EPS)).reshape(n, t, RWKV_WIDTH) * ln_w + ln_b
    bonus = jnp.sum(rh * kh * r_k, axis=-1, keepdims=True) * vh
    y = (y + bonus.reshape(n, t, RWKV_WIDTH)) * g
    return y, z[:, -1], s_last


def hgrn2_mixer(q, f, i, g, s0, lb, norm_w):
    n, t, _ = q.shape
    f32 = jnp.float32

    def heads(y):
        return y.astype(f32).reshape(n, t, HGRN_HEADS, -1)

    fg = lb + (1.0 - lb) * jax.nn.sigmoid(f.astype(f32))
    qh = jax.nn.silu(heads(q))
    kh = heads(1.0 - fg)
    lfh = heads(jnp.log(fg))
    vh = heads(i)
    c = math.gcd(t, HGRN_CHUNK)
    nc = t // c

    def chunks(y):
        return jnp.transpose(y.reshape(n, nc, c, HGRN_HEADS, -1), (1, 0, 3, 2, 4))

    mask = jnp.tril(jnp.ones((c, c), bool))[:, :, None]

    def step(S, inp):
        q_c, k_c, v_c, lf_c = inp
        b = jnp.cumsum(lf_c, axis=2)
        o_inter = jnp.einsum('nhtk,nhkv->nhtv', q_c * jnp.exp(b), S)
        diff = b[:, :, :, None, :] - b[:, :, None, :, :]
        dec = jnp.where(mask, jnp.exp(jnp.where(mask, diff, 0.0)), 0.0)
        att = jnp.einsum('nhtk,nhtsk,nhsk->nhts', q_c, dec, k_c)
        o = o_inter + jnp.einsum('nhts,nhsv->nhtv', att, v_c)
        b_last = b[:, :, -1]
        S = jnp.exp(b_last)[..., None] * S + jnp.einsum('nhsk,nhsv->nhkv', k_c * jnp.exp(b_last[:, :, None] - b), v_c)
        return S, o

    s_last, os_ = lax.scan(step, s0.astype(f32), tuple(chunks(y) for y in (qh, kh, vh, lfh)))
    o = jnp.transpose(os_, (1, 0, 3, 2, 4)).reshape(n, t, HGRN_HEADS, HGRN_HEAD_V)
    o = o * lax.rsqrt(jnp.mean(o * o, axis=-1, keepdims=True) + HGRN_EPS)
    o = o.reshape(n, t, HGRN_WIDTH) * norm_w * jax.nn.silu(g.astype(f32))
    return o, s_last


def _normal(k, shape, scale):
    return jax.random.normal(k, shape, jnp.float32) * scale


def setup_inputs(seed: int = 0) -> dict:
    key = jax.random.key(seed)
    ks = iter(jax.random.split(key, 48))
    L = DEPTH
    G, P, H = S5_GROUPS, S5_STATE, S5_GROUP
    n_idx = jnp.arange(P, dtype=jnp.float32)
    inp = {}
    inp['x_prompt'] = _normal(next(ks), (BATCH, SEQ, D_MODEL), 1.0)
    inp['x_sample'] = _normal(next(ks), (DEC_BATCH, DEC_SEQ, D_MODEL), 1.0)
    inp['state_s5_re'] = _normal(next(ks), (L, DEC_BATCH, G, P), 0.5)
    inp['state_s5_im'] = _normal(next(ks), (L, DEC_BATCH, G, P), 0.5)
    inp['state_rwkv_shift'] = _normal(next(ks), (L, DEC_BATCH, RWKV_SHIFT_WIDTH), 1.0)
    inp['state_rwkv_wkv'] = _normal(next(ks), (L, DEC_BATCH, RWKV_HEADS, RWKV_HEAD, RWKV_HEAD), 1.0)
    inp['state_hgrn'] = _normal(next(ks), (L, DEC_BATCH, HGRN_HEADS, HGRN_EXPAND, HGRN_HEAD_V), 0.5)
    inp['norm_ffn1'] = 1.0 + _normal(next(ks), (L, D_MODEL), 0.02)
    inp['ffn1_w_gate'] = _normal(next(ks), (L, D_MODEL, D_FF), D_MODEL ** -0.5)
    inp['ffn1_w_up'] = _normal(next(ks), (L, D_MODEL, D_FF), D_MODEL ** -0.5)
    inp['ffn1_w_down'] = _normal(next(ks), (L, D_FF, D_MODEL), D_FF ** -0.5)
    inp['norm_mix'] = 1.0 + _normal(next(ks), (L, D_MODEL), 0.02)
    inp['w_in'] = _normal(next(ks), (L, D_MODEL, N_IN), D_MODEL ** -0.5)
    inp['s5_a_re'] = -0.5 + _normal(next(ks), (L, G, P), 0.01)
    inp['s5_a_im'] = math.pi * n_idx + _normal(next(ks), (L, G, P), 0.01)
    inp['s5_log_dt'] = jax.random.uniform(next(ks), (L, G), jnp.float32, math.log(1e-3), math.log(1e-1))
    inp['s5_b_re'] = _normal(next(ks), (L, G, P, H), (2.0 * H) ** -0.5)
    inp['s5_b_im'] = _normal(next(ks), (L, G, P, H), (2.0 * H) ** -0.5)
    inp['s5_c_re'] = _normal(next(ks), (L, G, H, P), (2.0 * P) ** -0.5)
    inp['s5_c_im'] = _normal(next(ks), (L, G, H, P), (2.0 * P) ** -0.5)
    inp['s5_d'] = _normal(next(ks), (L, S5_WIDTH), 1.0)
    inp['s5_w_glu'] = _normal(next(ks), (L, S5_WIDTH, S5_WIDTH), S5_WIDTH ** -0.5)
    inp['s5_b_glu'] = _normal(next(ks), (L, S5_WIDTH), 0.01)
    inp['rwkv_mu'] = jax.random.uniform(next(ks), (L, RWKV_SHIFT_WIDTH), jnp.float32, 0.0, 1.0)
    inp['rwkv_w0'] = jax.random.uniform(next(ks), (L, RWKV_WIDTH), jnp.float32, -6.0, 0.0)
    inp['rwkv_w2'] = _normal(next(ks), (L, RWKV_DECAY_LORA, RWKV_WIDTH), RWKV_DECAY_LORA ** -0.5)
    inp['rwkv_a0'] = _normal(next(ks), (L, RWKV_WIDTH), 0.5)
    inp['rwkv_a2'] = _normal(next(ks), (L, RWKV_A_LORA, RWKV_WIDTH), RWKV_A_LORA ** -0.5)
    inp['rwkv_g2'] = _normal(next(ks), (L, RWKV_GATE_LORA, RWKV_WIDTH), RWKV_GATE_LORA ** -0.5)
    inp['rwkv_k_k'] = 0.85 + _normal(next(ks), (L, RWKV_WIDTH), 0.05)
    inp['rwkv_k_a'] = 1.0 + _normal(next(ks), (L, RWKV_WIDTH), 0.05)
    inp['rwkv_r_k'] = _normal(next(ks), (L, RWKV_HEADS, RWKV_HEAD), 0.1)
    inp['rwkv_ln_w'] = 1.0 + _normal(next(ks), (L, RWKV_WIDTH), 0.02)
    inp['rwkv_ln_b'] = _normal(next(ks), (L, RWKV_WIDTH), 0.01)
    inp['hgrn_lb_raw'] = _normal(next(ks), (L, HGRN_WIDTH), 0.1)
    inp['hgrn_norm_w'] = 1.0 + _normal(next(ks), (L, HGRN_WIDTH), 0.02)
    inp['w_out'] = _normal(next(ks), (L, D_MIX, D_MODEL), D_MIX ** -0.5)
    inp['norm_ffn2'] = 1.0 + _normal(next(ks), (L, D_MODEL), 0.02)
    inp['ffn2_w_gate'] = _normal(next(ks), (L, D_MODEL, D_FF), D_MODEL ** -0.5)
    inp['ffn2_w_up'] = _normal(next(ks), (L, D_MODEL, D_FF), D_MODEL ** -0.5)
    inp['ffn2_w_down'] = _normal(next(ks), (L, D_FF, D_MODEL), D_FF ** -0.5)
    inp['norm_final'] = 1.0 + _normal(next(ks), (D_MODEL,), 0.02)
    return inp


def reference(x_prompt, x_sample, state_s5_re, state_s5_im, state_rwkv_shift, state_rwkv_wkv, state_hgrn,
              norm_ffn1, ffn1_w_gate, ffn1_w_up, ffn1_w_down, norm_mix, w_in,
              s5_a_re, s5_a_im, s5_log_dt, s5_b_re, s5_b_im, s5_c_re, s5_c_im, s5_d, s5_w_glu, s5_b_glu,
              rwkv_mu, rwkv_w0, rwkv_w2, rwkv_a0, rwkv_a2, rwkv_g2, rwkv_k_k, rwkv_k_a, rwkv_r_k,
              rwkv_ln_w, rwkv_ln_b, hgrn_lb_raw, hgrn_norm_w, w_out,
              norm_ffn2, ffn2_w_gate, ffn2_w_up, ffn2_w_down, norm_final):
    p_lb = jax.nn.softmax(hgrn_lb_raw.astype(jnp.float32), axis=0)
    lower_bounds = jnp.cumsum(p_lb, axis=0) - p_lb[0]
    o_rw = S5_WIDTH
    o_hg = S5_WIDTH + RWKV_SHIFT_WIDTH

    def run(x, s5_re0, s5_im0, shift0, wkv0, hgrn0):
        s5_re_l, s5_im_l, shift_l, wkv_l, hgrn_l = [], [], [], [], []
        for l in range(DEPTH):
            h = rms_norm(x, norm_ffn1[l])
            x = x + 0.5 * swiglu_ffn(h, ffn1_w_gate[l], ffn1_w_up[l], ffn1_w_down[l])
            h = rms_norm(x, norm_mix[l])
            p = h @ w_in[l]
            y_s5, s_re, s_im = s5_mixer(p[..., :o_rw], s5_re0[l], s5_im0[l], s5_a_re[l], s5_a_im[l],
                                        s5_log_dt[l], s5_b_re[l], s5_b_im[l], s5_c_re[l], s5_c_im[l],
                                        s5_d[l], s5_w_glu[l], s5_b_glu[l])
            y_rw, sh, wkv = rwkv7_mixer(p[..., o_rw:o_hg], shift0[l], wkv0[l], rwkv_mu[l], rwkv_w0[l],
                                        rwkv_w2[l], rwkv_a0[l], rwkv_a2[l], rwkv_g2[l], rwkv_k_k[l],
                                        rwkv_k_a[l], rwkv_r_k[l], rwkv_ln_w[l], rwkv_ln_b[l])
            hq = p[..., o_hg:o_hg + HGRN_WIDTH]
            hf = p[..., o_hg + HGRN_WIDTH:o_hg + 2 * HGRN_WIDTH]
            hi = p[..., o_hg + 2 * HGRN_WIDTH:o_hg + 3 * HGRN_WIDTH]
            hgt = p[..., o_hg + 3 * HGRN_WIDTH:]
            y_hg, hs = hgrn2_mixer(hq, hf, hi, hgt, hgrn0[l], lower_bounds[l], hgrn_norm_w[l])
            mix = jnp.concatenate([y_s5, y_rw, y_hg], axis=-1).astype(x.dtype)
            x = x + mix @ w_out[l]
            h = rms_norm(x, norm_ffn2[l])
            x = x + 0.5 * swiglu_ffn(h, ffn2_w_gate[l], ffn2_w_up[l], ffn2_w_down[l])
            s5_re_l.append(s_re)
            s5_im_l.append(s_im)
            shift_l.append(sh)
            wkv_l.append(wkv)
            hgrn_l.append(hs)
        y = rms_norm(x, norm_final)
        return y, jnp.stack(s5_re_l), jnp.stack(s5_im_l), jnp.stack(shift_l), jnp.stack(wkv_l), jnp.stack(hgrn_l)

    nb = x_prompt.shape[0]

    def zeros_like_state(s):
        return jnp.zeros((DEPTH, nb) + s.shape[2:], jnp.float32)

    y_p, s5re_p, s5im_p, shift_p, wkv_p, hgrn_p = run(
        x_prompt, zeros_like_state(state_s5_re), zeros_like_state(state_s5_im),
        zeros_like_state(state_rwkv_shift), zeros_like_state(state_rwkv_wkv), zeros_like_state(state_hgrn))
    y_s, s5re_s, s5im_s, shift_s, wkv_s, hgrn_s = run(
        x_sample, state_s5_re, state_s5_im, state_rwkv_shift, state_rwkv_wkv, state_hgrn)
    return (y_p, y_s, s5re_p, s5im_p, shift_p, wkv_p, hgrn_p, s5re_s, s5im_s, shift_s, wkv_s, hgrn_s)
```

```python
import os
import numpy as np
import concourse.bass as bass
import concourse.mybir as mybir
from concourse.bass_utils import run_bass_kernel_spmd
from contextlib import ExitStack

F32 = mybir.dt.float32
BF16 = mybir.dt.bfloat16
I32 = mybir.dt.int32
AF = mybir.ActivationFunctionType
ALU = mybir.AluOpType

D = 2048
DFF = 5632
NIN = 6144
KT = D // 128
FT = DFF // 128
NS = 16
TS = 8
NORM_EPS = 1e-6

EPOCH = 8192
COMPUTE = ('pe', 'act', 'dve', 'pool')
NDMASEM = 24
NSWSEM = 8


class Buf:
    __slots__ = ('name', 'w', 'r')

    def __init__(self, name=''):
        self.name = name
        self.w = None
        self.r = {}


class Prog:
    def __init__(self, nc):
        self.nc = nc
        self.q = {e: [] for e in ('pe', 'act', 'dve', 'pool', 'sp')}
        self.src_n = {}
        self.signalled = {}
        self.seen = {e: {} for e in self.q}
        self.dma_rr = 0
        self.dma_rr2 = 0

    def _new_event(self, src):
        i = self.src_n.get(src, 0)
        self.src_n[src] = i + 1
        return i

    def _deps(self, eng, src, reads, writes):
        deps = {}

        def add(d, raw):
            if d is None:
                return
            s, i = d
            if s == src and src == 'pe':
                return
            if deps.get(s, -1) < i:
                deps[s] = i
        for b in reads:
            add(b.w, True)
        for b in writes:
            add(b.w, False)
            for s, i in b.r.items():
                if (s != src or src != 'pe') and deps.get(s, -1) < i:
                    deps[s] = i
        out = []
        seen = self.seen[eng]
        for s, i in deps.items():
            if seen.get(s, -1) >= i:
                continue
            seen[s] = i
            self.signalled[(s, i)] = True
            out.append((s, i))
        return out

    def op(self, eng, fn, reads=(), writes=()):
        waits = self._deps(eng, eng, reads, writes)
        idx = self._new_event(eng)
        for b in reads:
            b.r[eng] = idx
        for b in writes:
            b.w = (eng, idx)
            b.r = {}
        self.q[eng].append(('op', waits, fn, (eng, idx)))

    def dma(self, eng, fn, reads=(), writes=()):
        if eng == 'pool':
            src = 'dma%d' % (NDMASEM + self.dma_rr2)
            self.dma_rr2 = (self.dma_rr2 + 1) % NSWSEM
        else:
            src = 'dma%d' % self.dma_rr
            self.dma_rr = (self.dma_rr + 1) % NDMASEM
        waits = self._deps(eng, src, reads, writes)
        idx = self._new_event(src)
        for b in reads:
            b.r[src] = idx
        for b in writes:
            b.w = (src, idx)
            b.r = {}
        self.q[eng].append(('dma', waits, fn, (src, idx)))

    def barrier(self):
        for eng in self.q:
            waits = []
            seen = self.seen[eng]
            for src, n in self.src_n.items():
                if src == eng or n == 0:
                    continue
                i = n - 1
                if seen.get(src, -1) >= i:
                    continue
                seen[src] = i
                self.signalled[(src, i)] = True
                waits.append((src, i))
            self.q[eng].append(('bar', waits, None, None))

    def run(self, es):
        nc = self.nc
        sems = {}
        val = {}
        for e in COMPUTE:
            n = self.src_n.get(e, 0)
            c = 0
            ep = 0
            for i in range(n):
                if self.signalled.get((e, i)):
                    if c == EPOCH:
                        ep += 1
                        c = 0
                    c += 1
                    val[(e, i)] = (e, ep, c)
                    if (e, ep) not in sems:
                        sems[(e, ep)] = es.enter_context(nc.semaphore('s_%s_%d' % (e, ep)))
        for k in range(NDMASEM + NSWSEM):
            s = 'dma%d' % k
            if self.src_n.get(s, 0):
                sems[(s, 0)] = es.enter_context(nc.semaphore('s_' + s))
        final_dma = {('dma%d' % k): 16 * self.src_n.get('dma%d' % k, 0) for k in range(NDMASEM + NSWSEM)}

        def waitspec(s, i):
            if s.startswith('dma'):
                return sems[(s, 0)], 16 * (i + 1)
            _, ep, c = val[(s, i)]
            return sems[(s, ep)], c

        block = es.enter_context(nc.Block())
        prog = self

        def replay(engname):
            def body(eng):
                for kind, waits, fn, ev in prog.q[engname]:
                    for (s, i) in waits:
                        sem, v = waitspec(s, i)
                        eng.wait_ge(sem, v)
                    if fn is None:
                        continue
                    ins = fn(eng)
                    if kind == 'dma':
                        ins.then_inc(sems[(ev[0], 0)], 16)
                    elif ev in val:
                        _, ep, c = val[ev]
                        ins.then_inc(sems[(ev[0], ep)], 1)
                if engname == 'sp':
                    for s, v in final_dma.items():
                        if v:
                            eng.wait_ge(sems[(s, 0)], v)
            return body

        block.tensor(replay('pe'))
        block.scalar(replay('act'))
        block.vector(replay('dve'))
        block.gpsimd(replay('pool'))
        block.sync(replay('sp'))


class Cfg:
    def __init__(self, depth=4, tp=2048, tb=512, mixers=True, debug=False, mode='full'):
        self.mode = mode
        self.mixl = 1
        self.stop = 99
        self.only = None
        self.depth = depth
        self.debug = debug
        self.tp = tp
        self.tb = tb
        self.tok = tp + NS * TS
        self.mixers = mixers
        self.blocks = [(i * tb, tb) for i in range(tp // tb)] + [(tp, NS * TS)]


NB = 4


def build(cfg):
    nc = bass.Bass("TRN2", target_bir_lowering=False)
    L = cfg.depth
    TOK = cfg.tok
    TB = cfg.tb

    def din(name, shape, dt=F32):
        return nc.dram_tensor(name, list(shape), dt, kind="ExternalInput").ap()

    def dout(name, shape, dt=F32):
        return nc.dram_tensor(name, list(shape), dt, kind="ExternalOutput").ap()

    def dscr(name, shape, dt=F32):
        return nc.dram_tensor(name, list(shape), dt, kind="Internal").ap()

    xT = din("xT", [D, TOK])
    wts = {}
    for nm, ncb, nkt in (("wg1", 11, 16), ("wu1", 11, 16), ("wd1", 4, 44), ("win", 12, 16),
                         ("wout", 4, 16), ("wg2", 11, 16), ("wu2", 11, 16), ("wd2", 4, 44)):
        wts[nm] = din(nm, [L if cfg.mode != 'mix' else 1, ncb if cfg.mode != 'mix' else 1, 128, nkt, 512])
    nrm = din("nrm", [128, 3 * L + 1, KT])
    yT = dout("yT", [D, TOK])
    xsc = dscr("xsc", [D, TOK])
    psc = (dout if cfg.debug else dscr)("psc", [NIN, TOK])
    msc = dscr("msc", [D, TOK], BF16)
    if cfg.debug:
        dbg_h = dout("dbg_h", [D, TOK], BF16)
        dbg_x = dout("dbg_x", [D, TOK])
    if cfg.mode == 'mix':
        psc = din("psc_in", [NIN, TOK])
        msc = dout("msc_out", [D, TOK], BF16)
    ident_d = din("ident", [128, 128])
    m64_d = din("m64", [128, 128])
    m8_d = din("m8", [128, 128])
    bm16_d = din("bm16", [128, NS, 128], BF16)
    cm_d = din("cm", [128, 3, 512], BF16)
    lbraw_d = din("lbraw", [128, 6, L])
    hnw_d = din("hnw", [128, L, 6])
    hst_d = din("hst", [L, NS, 6, 128, 128])
    s5fm_d = din("s5fm", [128, L, 3, 16])
    s5row_d = din("s5row", [128, L, 12, 128])
    s5b_d = din("s5b", [128, L, 2, 4, 128])
    s5c_d = din("s5c", [128, L, 2, 16, 128])
    s5db_d = din("s5db", [128, L, 2, 4])
    s5w_d = din("s5w", [128, L, 4, 512])
    s5x0_d = din("s5x0", [128, L, 2, 16, NS])
    s5p_d = dout("s5p", [128, L, 2, 16])
    s5s_d = dout("s5s", [128, L, 2, 16, NS])
    iota_d = din("iota", [128, 128])
    rwp_d = din("rwp", [128, L, 62])
    rwl_d = din("rwl", [128, L, 2, 768])
    rwsh0_d = din("rwsh0", [128, L, 20, NS])
    rwshp_d = dout("rwshp", [128, L, 20])
    rwshs_d = dout("rwshs", [128, L, 20, NS])
    rwh0_d = din("rwh0", [128, L, 6, NS, 128])
    rwhp_d = dout("rwhp", [128, L, 6, 128])
    rwhs_d = dout("rwhs", [128, L, 6, NS, 128])
    rwm_d = din("rwm", [128, 2, 384], BF16)
    blk64_d = din("blk64", [128, 128])
    cm6_d = din("cm6", [128, 2, 768], BF16)
    rmask2_d = din("rmask2", [128, 2])
    hgp_d = dout("hgp", [L, 6, 128, 128])
    hgs_d = dout("hgs", [L, NS, 6, 128, 128])
    bxsc = [Buf() for _ in cfg.blocks]
    bpsc = [Buf() for _ in cfg.blocks]
    bmsc = [Buf() for _ in cfg.blocks]

    es = ExitStack()
    with es:
        P = Prog(nc)

        def sb(name, shape, dt=F32):
            return es.enter_context(nc.sbuf_tensor(name, list(shape), dt))

        AW = KT * TB + KT * TB // 2 + FT * TB // 2
        arena = sb("arena", [128, AW])
        ast = {'p': 0}

        def a_reset():
            ast['p'] = 0

        def a_alloc(shape, dt=F32):
            n = 1
            for d_ in shape[1:]:
                n *= d_
            words = n if dt == F32 else (n + 1) // 2
            p0 = ast['p']
            ast['p'] = p0 + words
            assert ast['p'] <= AW, ("arena overflow", ast['p'], AW)
            v = arena[:, p0:p0 + words]
            if dt != F32:
                v = v.bitcast(dt)
                if dt == I32:
                    pass
            if len(shape) == 3:
                v = v.rearrange("p (a b) -> p a b", a=shape[1])
            return v

        xs = a_alloc([128, KT, TB])
        bx = [Buf('x%d' % i) for i in range(KT)]
        hb = a_alloc([128, KT, TB], BF16)
        bh = Buf('h')
        act = a_alloc([128, FT, TB], BF16)
        bact = [Buf('act%d' % i) for i in range(FT)]
        sq = [sb("sq%d" % i, [128, TB]) for i in range(2)]
        bsq = [Buf() for _ in range(2)]
        rstd = sb("rstd", [128, TB])
        brstd = Buf('rstd')
        sg = [sb("sg%d" % i, [128, TB]) for i in range(2)]
        bsg = [Buf() for _ in range(2)]
        stg = [sb("stg%d" % i, [128, TB]) for i in range(2)]
        bstg = [Buf() for _ in range(2)]
        ones = sb("ones", [128, 128])
        bones = Buf('ones')
        nrm_t = sb("nrm_t", [128, 3 * L + 1, KT])
        bnrm = Buf('nrm')
        wring = [sb("wr%d" % i, [128, 8, 512], BF16) for i in range(NB)]
        bwr = [Buf('wr%d' % i) for i in range(NB)]
        psall = es.enter_context(nc.psum_tensor("psall", [128, 8, 512], F32))
        psb = [psall[:, i, :] for i in range(8)]
        bps = [Buf('ps%d' % i) for i in range(8)]
        st = {'ps': 0, 'sq': 0, 'sg': 0, 'stg': 0}

        def ps_next():
            i = st['ps']
            st['ps'] = (i + 1) % 8
            return psb[i], bps[i]

        def rr(key, n):
            i = st[key]
            st[key] = (i + 1) % n
            return i

        def wseq():
            seq = []

            def ffn(l, g, u, d):
                for cb in range(11):
                    for nm in (g, u):
                        for kb in range(2):
                            seq.append((nm, l, cb, kb * 8, 8))
                for cb in range(4):
                    for rb in range(6):
                        seq.append((d, l, cb, rb * 8, 8 if rb < 5 else 4))

            def front(l):
                ffn(l, "wg1", "wu1", "wd1")
                for cb in range(12):
                    for kb in range(2):
                        seq.append(("win", l, cb, kb * 8, 8))

            def back(l):
                for cb in range(4):
                    for kb in range(2):
                        seq.append(("wout", l, cb, kb * 8, 8))
                ffn(l, "wg2", "wu2", "wd2")

            for _ in cfg.blocks:
                front(0)
            for l in range(1, L):
                for _ in cfg.blocks:
                    back(l - 1)
                    front(l)
            for _ in cfg.blocks:
                back(L - 1)
            return seq

        WSEQ = wseq()
        wst = {'issued': 0, 'used': 0}

        def w_issue():
            j = wst['issued']
            nm, l, cb, k0, n = WSEQ[j]
            slot = j % NB
            src = wts[nm][l, cb, :, k0:k0 + n, :]
            P.dma('pool', lambda e, slot=slot, src=src, n=n: e.dma_start(out=wring[slot][:, 0:n, :], in_=src),
                  writes=[bwr[slot]])
            wst['issued'] = j + 1

        def w_get(expect):
            j = wst['used']
            assert WSEQ[j] == expect, (WSEQ[j], expect)
            while wst['issued'] < min(len(WSEQ), j + NB - 1):
                w_issue()
            wst['used'] = j + 1
            slot = j % NB
            return wring[slot], bwr[slot]

        P.op('dve', lambda e: e.memset(ones[:], 1.0), writes=[bones])
        P.dma('sp', lambda e: e.dma_start(out=nrm_t[:], in_=nrm), writes=[bnrm])

        def rmsnorm(n, widx, out_fn):
            ps, bp = ps_next()
            for kt in range(KT):
                i = rr('sq', 2)
                P.op('act', lambda e, kt=kt, i=i: e.activation(out=sq[i][:, :n], in_=xs[:, kt, :n], func=AF.Square),
                     reads=[bx[kt]], writes=[bsq[i]])
                P.op('pe', lambda e, kt=kt, i=i, ps=ps: e.matmul(ps[:, :n], lhsT=ones[:], rhs=sq[i][:, :n],
                                                                   start=(kt == 0), stop=(kt == KT - 1)),
                     reads=[bones, bsq[i]], writes=[bp])
            P.op('dve', lambda e, ps=ps: e.tensor_scalar(out=rstd[:, :n], in0=ps[:, :n], scalar1=1.0 / D, scalar2=NORM_EPS,
                                                          op0=ALU.mult, op1=ALU.add), reads=[bp], writes=[brstd])
            P.op('act', lambda e: e.activation(out=rstd[:, :n], in_=rstd[:, :n], func=AF.Sqrt), reads=[brstd], writes=[brstd])
            P.op('dve', lambda e: e.reciprocal(out=rstd[:, :n], in_=rstd[:, :n]), reads=[brstd], writes=[brstd])
            for kt in range(KT):
                o, wb = out_fn(kt)
                P.op('dve', lambda e, kt=kt, o=o: e.scalar_tensor_tensor(out=o, in0=xs[:, kt, :n],
                                                                         scalar=nrm_t[:, widx, kt:kt + 1],
                                                                         in1=rstd[:, :n], op0=ALU.mult, op1=ALU.mult),
                     reads=[bx[kt], bnrm, brstd], writes=wb)

        def norm_to_h(n, widx):
            rmsnorm(n, widx, lambda kt: (hb[:, kt, :n], [bh]))

        def proj16(n, l, nm, cb, rhs_t, rhs_b):
            w0, b0 = w_get((nm, l, cb, 0, 8))
            w1, b1 = w_get((nm, l, cb, 8, 8))
            outs = []
            for ft in range(4):
                ps, bp = ps_next()
                for kt in range(KT):
                    w, bw_ = (w0, b0) if kt < 8 else (w1, b1)
                    P.op('pe', lambda e, ps=ps, w=w, kt=kt, ft=ft: e.matmul(
                        ps[:, :n], lhsT=w[:, kt % 8, ft * 128:(ft + 1) * 128], rhs=rhs_t[:, kt, :n],
                        start=(kt == 0), stop=(kt == KT - 1)), reads=[bw_, rhs_b], writes=[bp])
                outs.append((ps, bp))
            return outs

        def ffn(n, l, g, u, d):
            for cb in range(11):
                go = proj16(n, l, g, cb, hb, bh)
                uo = proj16(n, l, u, cb, hb, bh)
                for ft in range(4):
                    f = cb * 4 + ft
                    i = rr('sg', 2)
                    gps, gb = go[ft]
                    ups, ub = uo[ft]
                    P.op('act', lambda e, i=i, gps=gps: e.activation(out=sg[i][:, :n], in_=gps[:, :n], func=AF.Silu),
                         reads=[gb], writes=[bsg[i]])
                    P.op('dve', lambda e, i=i, ups=ups, f=f: e.tensor_tensor(out=act[:, f, :n], in0=sg[i][:, :n],
                                                                            in1=ups[:, :n], op=ALU.mult),
                         reads=[bsg[i], ub], writes=[bact[f]])
            for cb in range(4):
                pss = [ps_next() for _ in range(4)]
                for rb in range(6):
                    nk = 8 if rb < 5 else 4
                    w, bw_ = w_get((d, l, cb, rb * 8, nk))
                    for dt in range(4):
                        ps, bp = pss[dt]
                        for k in range(nk):
                            f = rb * 8 + k
                            P.op('pe', lambda e, ps=ps, w=w, k=k, dt=dt, f=f: e.matmul(
                                ps[:, :n], lhsT=w[:, k, dt * 128:(dt + 1) * 128], rhs=act[:, f, :n],
                                start=(f == 0), stop=(f == FT - 1)), reads=[bw_, bact[f]], writes=[bp])
                for dt in range(4):
                    ps, bp = pss[dt]
                    kt = cb * 4 + dt
                    P.op('dve', lambda e, ps=ps, kt=kt: e.scalar_tensor_tensor(
                        out=xs[:, kt, :n], in0=ps[:, :n], scalar=0.5, in1=xs[:, kt, :n], op0=ALU.mult, op1=ALU.add),
                        reads=[bp, bx[kt]], writes=[bx[kt]])

        def seg_front(l, bi):
            t0, n = cfg.blocks[bi]
            norm_to_h(n, 3 * l + 0)
            if cfg.debug and l == 0:
                P.dma('sp', lambda e: e.dma_start(out=dbg_h[:, t0:t0 + n].rearrange("(kt p) t -> p kt t", p=128), in_=hb[:, :, :n]),
                      reads=[bh])
            ffn(n, l, "wg1", "wu1", "wd1")
            if cfg.debug and l == 0:
                P.dma('sp', lambda e: e.dma_start(out=dbg_x[:, t0:t0 + n].rearrange("(kt p) t -> p kt t", p=128), in_=xs[:, :, :n]),
                      reads=bx)
            P.dma('sp', lambda e: e.dma_start(out=xsc[:, t0:t0 + n].rearrange("(kt p) t -> p kt t", p=128), in_=xs[:, :, :n]),
                  reads=bx, writes=[bxsc[bi]])
            norm_to_h(n, 3 * l + 1)
            for cb in range(12):
                po = proj16(n, l, "win", cb, hb, bh)
                for ft in range(4):
                    ps, bp = po[ft]
                    i = rr('stg', 2)
                    c0 = cb * 512 + ft * 128
                    P.op('act', lambda e, i=i, ps=ps: e.activation(out=stg[i][:, :n], in_=ps[:, :n], func=AF.Copy),
                         reads=[bp], writes=[bstg[i]])
                    P.dma('sp', lambda e, i=i, c0=c0: e.dma_start(out=psc[c0:c0 + 128, t0:t0 + n], in_=stg[i][:, :n]),
                          reads=[bstg[i]], writes=[bpsc[bi]])

        def seg_back(l, bi):
            t0, n = cfg.blocks[bi]
            P.dma('sp', lambda e: e.dma_start(out=xs[:, :, :n], in_=xsc[:, t0:t0 + n].rearrange("(kt p) t -> p kt t", p=128)),
                  reads=[bxsc[bi]], writes=bx)
            P.dma('sp', lambda e: e.dma_start(out=hb[:, :, :n], in_=msc[:, t0:t0 + n].rearrange("(kt p) t -> p kt t", p=128)),
                  reads=[bmsc[bi]], writes=[bh])
            for cb in range(4):
                po = proj16(n, l, "wout", cb, hb, bh)
                for ft in range(4):
                    ps, bp = po[ft]
                    kt = cb * 4 + ft
                    P.op('dve', lambda e, ps=ps, kt=kt: e.tensor_tensor(out=xs[:, kt, :n], in0=ps[:, :n], in1=xs[:, kt, :n],
                                                                       op=ALU.add), reads=[bp, bx[kt]], writes=[bx[kt]])
            norm_to_h(n, 3 * l + 2)
            ffn(n, l, "wg2", "wu2", "wd2")

        def load_x0(bi):
            t0, n = cfg.blocks[bi]
            P.dma('sp', lambda e: e.dma_start(out=xs[:, :, :n], in_=xT[:, t0:t0 + n].rearrange("(kt p) t -> p kt t", p=128)),
                  writes=bx)

        def final(bi):
            t0, n = cfg.blocks[bi]

            def of(kt):
                return act[:, 0:2, :].bitcast(F32)[:, 0, :n] if False else None
            ps, bp = ps_next()
            for kt in range(KT):
                i = rr('sq', 2)
                P.op('act', lambda e, kt=kt, i=i: e.activation(out=sq[i][:, :n], in_=xs[:, kt, :n], func=AF.Square),
                     reads=[bx[kt]], writes=[bsq[i]])
                P.op('pe', lambda e, kt=kt, i=i, ps=ps: e.matmul(ps[:, :n], lhsT=ones[:], rhs=sq[i][:, :n],
                                                                   start=(kt == 0), stop=(kt == KT - 1)),
                     reads=[bones, bsq[i]], writes=[bp])
            P.op('dve', lambda e, ps=ps: e.tensor_scalar(out=rstd[:, :n], in0=ps[:, :n], scalar1=1.0 / D, scalar2=NORM_EPS,
                                                          op0=ALU.mult, op1=ALU.add), reads=[bp], writes=[brstd])
            P.op('act', lambda e: e.activation(out=rstd[:, :n], in_=rstd[:, :n], func=AF.Sqrt), reads=[brstd], writes=[brstd])
            P.op('dve', lambda e: e.reciprocal(out=rstd[:, :n], in_=rstd[:, :n]), reads=[brstd], writes=[brstd])
            for kt in range(KT):
                i = rr('stg', 2)
                P.op('dve', lambda e, kt=kt, i=i: e.scalar_tensor_tensor(out=stg[i][:, :n], in0=xs[:, kt, :n],
                                                                         scalar=nrm_t[:, 3 * L, kt:kt + 1],
                                                                         in1=rstd[:, :n], op0=ALU.mult, op1=ALU.mult),
                     reads=[bx[kt], bnrm, brstd], writes=[bstg[i]])
                P.dma('sp', lambda e, kt=kt, i=i: e.dma_start(out=yT[kt * 128:(kt + 1) * 128, t0:t0 + n], in_=stg[i][:, :n]),
                      reads=[bstg[i]])


        a_reset()
        MT = [a_alloc([128, 512]) for i in range(10)]
        bMT = [Buf('mt%d' % i) for i in range(10)]
        qtb = a_alloc([128, 512], BF16); bqtb = Buf()
        ktb = a_alloc([128, 512], BF16); bktb = Buf()
        vtm = a_alloc([128, 128], BF16); bvtm = Buf()
        ktm = a_alloc([128, 128], BF16); bktm = Buf()
        attm = a_alloc([128, 128], BF16); battm = Buf()
        Sst = a_alloc([128, 128]); bSst = Buf()
        Stmp = a_alloc([128, 128]); bStmp = Buf()
        Sbf = a_alloc([128, 128], BF16); bSbf = Buf()
        S0 = a_alloc([128, NS, 128]); bS0 = Buf()
        S0b = a_alloc([128, NS, 128], BF16); bS0b = Buf()
        vmask = a_alloc([128, NS, 128], BF16); bvmask = Buf()
        mixo = sb("mixo", [128, 512], BF16); bmixo = Buf()
        ident = sb("identt", [128, 128]); m64 = sb("m64t", [128, 128]); m8 = sb("m8t", [128, 128])
        bm16 = sb("bm16t", [128, NS, 128], BF16); cm = sb("cmt", [128, 3, 512], BF16); iota = sb("iotat", [128, 128])
        bconst = Buf('const')
        lbraw = sb("lbraw_t", [128, 6, L]); lbt = sb("lbt", [128, 6, L]); omlt = sb("omlt", [128, 6, L])
        lsum = sb("lsum", [128, 6]); hnw = sb("hnw_t", [128, L, 6])
        blb = Buf('lb')
        for t_, d_ in ((ident, ident_d), (m64, m64_d), (m8, m8_d), (bm16, bm16_d), (cm, cm_d), (hnw, hnw_d), (iota, iota_d)):
            P.dma('sp', lambda e, t_=t_, d_=d_: e.dma_start(out=t_[:], in_=d_), writes=[bconst])
        P.dma('sp', lambda e: e.dma_start(out=lbraw[:], in_=lbraw_d), writes=[blb])
        P.op('act', lambda e: e.activation(out=lbraw[:], in_=lbraw[:], func=AF.Exp), reads=[blb], writes=[blb])
        P.op('dve', lambda e: e.tensor_reduce(out=lsum[:], in_=lbraw[:], axis=mybir.AxisListType.X, op=ALU.add), reads=[blb], writes=[blb])
        P.op('dve', lambda e: e.reciprocal(out=lsum[:], in_=lsum[:]), reads=[blb], writes=[blb])
        P.op('dve', lambda e: e.memset(lbt[:], 0.0), reads=[blb], writes=[blb])
        for l_ in range(1, L):
            P.op('dve', lambda e, l_=l_: e.tensor_tensor(out=lbt[:, :, l_], in0=lbraw[:, :, l_], in1=lsum[:], op=ALU.mult), reads=[blb], writes=[blb])
            if l_ > 1:
                P.op('dve', lambda e, l_=l_: e.tensor_tensor(out=lbt[:, :, l_], in0=lbt[:, :, l_], in1=lbt[:, :, l_ - 1], op=ALU.add), reads=[blb], writes=[blb])
        P.op('dve', lambda e: e.tensor_scalar(out=omlt[:], in0=lbt[:], scalar1=-1.0, scalar2=1.0, op0=ALU.mult, op1=ALU.add), reads=[blb], writes=[blb])

        pieces = list(cfg.blocks)
        NPB = len(pieces) - 1
        bpsc_all = bpsc
        bmsc_all = bmsc

        def hgrn(l):
            O_Q, O_F, O_I, O_G = 3072, 3840, 4608, 5376
            tF, tQ, tG, tI, tK, tB, tE, tEn, oacc, rst = MT
            bF, bQ, bG, bI, bK, bB, bE, bEn, boacc, brst = bMT
            for h in range(6):
                P.op('dve', lambda e: e.memset(Sst[:], 0.0), writes=[bSst])
                P.op('dve', lambda e: e.memset(Sbf[:], 0.0), writes=[bSbf])
                for pi, (t0, n) in enumerate(pieces):
                    samp = (pi == NPB)
                    for tt_, bb_, off in ((tF, bF, O_F), (tQ, bQ, O_Q), (tI, bI, O_I), (tG, bG, O_G)):
                        P.dma('sp', lambda e, tt_=tt_, off=off, t0=t0, n=n, h=h: e.dma_start(
                            out=tt_[:, :n], in_=psc[off + 128 * h: off + 128 * h + 128, t0:t0 + n]),
                            reads=[bpsc_all[pi]], writes=[bb_])
                    P.op('act', lambda e, n=n: e.activation(out=tF[:, :n], in_=tF[:, :n], func=AF.Sigmoid), reads=[bF], writes=[bF])
                    P.op('dve', lambda e, n=n, h=h: e.tensor_scalar(out=tF[:, :n], in0=tF[:, :n], scalar1=omlt[:, h, l:l + 1],
                                                                      scalar2=lbt[:, h, l:l + 1], op0=ALU.mult, op1=ALU.add),
                         reads=[bF, blb], writes=[bF])
                    P.op('dve', lambda e, n=n: e.tensor_scalar(out=tK[:, :n], in0=tF[:, :n], scalar1=-1.0, scalar2=1.0,
                                                                op0=ALU.mult, op1=ALU.add), reads=[bF], writes=[bK])
                    P.op('act', lambda e, n=n: e.activation(out=tF[:, :n], in_=tF[:, :n], func=AF.Ln), reads=[bF], writes=[bF])
                    ci = 1 if samp else 0
                    P.op('dve', lambda e, n=n, ci=ci: e.tensor_tensor_scan(out=tB[:, :n], data0=cm[:, ci, :n], data1=tF[:, :n],
                                                                           initial=0.0, op0=ALU.mult, op1=ALU.add),
                         reads=[bF, bconst], writes=[bB])
                    P.op('act', lambda e, n=n: e.activation(out=tE[:, :n], in_=tB[:, :n], func=AF.Exp), reads=[bB], writes=[bE])
                    P.op('act', lambda e, n=n: e.activation(out=tEn[:, :n], in_=tB[:, :n], func=AF.Exp, scale=-1.0), reads=[bB], writes=[bEn])
                    P.op('act', lambda e, n=n: e.activation(out=tQ[:, :n], in_=tQ[:, :n], func=AF.Silu), reads=[bQ], writes=[bQ])
                    P.op('dve', lambda e, n=n: e.tensor_tensor(out=qtb[:, :n], in0=tQ[:, :n], in1=tE[:, :n], op=ALU.mult),
                         reads=[bQ, bE], writes=[bqtb])
                    P.op('dve', lambda e, n=n: e.tensor_tensor(out=tK[:, :n], in0=tK[:, :n], in1=tEn[:, :n], op=ALU.mult),
                         reads=[bK, bEn], writes=[bK])
                    P.op('act', lambda e, n=n: e.activation(out=ktb[:, :n], in_=tK[:, :n], func=AF.Copy), reads=[bK], writes=[bktb])
                    if samp:
                        P.dma('sp', lambda e, h=h: e.dma_start(out=S0[:], in_=hst_d[l, :, h, :, :].rearrange("n k v -> k n v")),
                              writes=[bS0])
                        P.op('act', lambda e: e.activation(out=S0b[:], in_=S0[:], func=AF.Copy), reads=[bS0], writes=[bS0b])
                    for tt in range(n // 128):
                        c0 = tt * 128
                        ps1, bp1 = ps_next()
                        P.op('pe', lambda e, ps1=ps1, c0=c0: e.transpose(ps1[:, 0:128], tI[:, c0:c0 + 128], ident[:]),
                             reads=[bI, bconst], writes=[bp1])
                        P.op('act', lambda e, ps1=ps1: e.activation(out=vtm[:], in_=ps1[:, 0:128], func=AF.Copy), reads=[bp1], writes=[bvtm])
                        ps2, bp2 = ps_next()
                        P.op('pe', lambda e, ps2=ps2, c0=c0: e.transpose(ps2[:, 0:128], tK[:, c0:c0 + 128], ident[:]),
                             reads=[bK, bconst], writes=[bp2])
                        P.op('act', lambda e, ps2=ps2: e.activation(out=ktm[:], in_=ps2[:, 0:128], func=AF.Copy), reads=[bp2], writes=[bktm])
                        ps3, bp3 = ps_next()
                        P.op('pe', lambda e, ps3=ps3, c0=c0: e.matmul(ps3[:, 0:128], lhsT=ktb[:, c0:c0 + 128], rhs=qtb[:, c0:c0 + 128],
                                                                      start=True, stop=True), reads=[bktb, bqtb], writes=[bp3])
                        msk = m8 if samp else m64
                        P.op('dve', lambda e, ps3=ps3, msk=msk: e.tensor_tensor(out=attm[:], in0=ps3[:, 0:128], in1=msk[:], op=ALU.mult),
                             reads=[bp3, bconst], writes=[battm])
                        pso, bpo = ps_next()
                        P.op('pe', lambda e, pso=pso: e.matmul(pso[:, 0:128], lhsT=vtm[:], rhs=attm[:], start=True, stop=False),
                             reads=[bvtm, battm], writes=[bpo])
                        if not samp:
                            for cc in range(2):
                                a0 = cc * 64
                                P.op('pe', lambda e, pso=pso, a0=a0, c0=c0, cc=cc: e.matmul(
                                    pso[:, a0:a0 + 64], lhsT=Sbf[:], rhs=qtb[:, c0 + a0:c0 + a0 + 64], start=False, stop=(cc == 1)),
                                    reads=[bSbf, bqtb], writes=[bpo])
                                psd, bpd = ps_next()
                                P.op('pe', lambda e, psd=psd, a0=a0: e.matmul(psd[:, 0:128], lhsT=ktm[a0:a0 + 64, :], rhs=vtm[a0:a0 + 64, :],
                                                                                start=True, stop=True), reads=[bktm, bvtm], writes=[bpd])
                                ecol = c0 + a0 + 63
                                P.op('dve', lambda e, ecol=ecol: e.tensor_scalar(out=Stmp[:], in0=Sst[:], scalar1=tE[:, ecol:ecol + 1],
                                                                                 scalar2=None, op0=ALU.mult), reads=[bSst, bE], writes=[bStmp])
                                P.op('dve', lambda e, psd=psd, ecol=ecol: e.scalar_tensor_tensor(
                                    out=Sst[:], in0=psd[:, 0:128], scalar=tE[:, ecol:ecol + 1], in1=Stmp[:], op0=ALU.mult, op1=ALU.add),
                                    reads=[bpd, bE, bStmp], writes=[bSst])
                                P.op('act', lambda e: e.activation(out=Sbf[:], in_=Sst[:], func=AF.Copy), reads=[bSst], writes=[bSbf])
                        else:
                            for sn in range(NS):
                                P.op('pe', lambda e, pso=pso, sn=sn: e.matmul(
                                    pso[:, sn * 8:sn * 8 + 8], lhsT=S0b[:, sn, :], rhs=qtb[:, sn * 8:sn * 8 + 8], start=False, stop=(sn == NS - 1)),
                                    reads=[bS0b, bqtb], writes=[bpo])
                            for sn in range(NS):
                                P.op('dve', lambda e, sn=sn: e.tensor_tensor(out=vmask[:, sn, :], in0=vtm[:], in1=bm16[:, sn, :], op=ALU.mult),
                                     reads=[bvtm, bconst], writes=[bvmask])
                            for g4 in range(4):
                                psd, bpd = ps_next()
                                P.op('pe', lambda e, psd=psd, g4=g4: e.matmul(
                                    psd[:, :], lhsT=ktm[:], rhs=vmask[:, g4 * 4:g4 * 4 + 4, :].rearrange("p a b -> p (a b)"), start=True, stop=True),
                                    reads=[bktm, bvmask], writes=[bpd])
                                for j in range(4):
                                    sn = g4 * 4 + j
                                    ecol = sn * 8 + 7
                                    P.op('dve', lambda e, psd=psd, j=j, sn=sn: e.tensor_tensor(
                                        out=S0[:, sn, :], in0=psd[:, j * 128:(j + 1) * 128], in1=S0[:, sn, :], op=ALU.add),
                                        reads=[bpd, bS0, bS0b], writes=[bS0])
                                    P.op('dve', lambda e, sn=sn, ecol=ecol: e.tensor_scalar(
                                        out=S0[:, sn, :], in0=S0[:, sn, :], scalar1=tE[:, ecol:ecol + 1], scalar2=None, op0=ALU.mult),
                                        reads=[bS0, bE], writes=[bS0])
                            P.dma('sp', lambda e, h=h: e.dma_start(out=hgs_d[l, :, h, :, :].rearrange("n k v -> k n v"), in_=S0[:]),
                                  reads=[bS0])
                        P.op('act', lambda e, pso=pso, c0=c0: e.activation(out=oacc[:, c0:c0 + 128], in_=pso[:, 0:128], func=AF.Copy),
                             reads=[bpo], writes=[boacc])
                    if pi == NPB - 1:
                        P.dma('sp', lambda e, h=h: e.dma_start(out=hgp_d[l, h, :, :], in_=Sst[:]), reads=[bSst])
                    P.op('act', lambda e, n=n: e.activation(out=rst[:, :n], in_=oacc[:, :n], func=AF.Square), reads=[boacc], writes=[brst])
                    psn, bpn = ps_next()
                    P.op('pe', lambda e, psn=psn, n=n: e.matmul(psn[:, :n], lhsT=ones[:], rhs=rst[:, :n], start=True, stop=True),
                         reads=[bones, brst], writes=[bpn])
                    P.op('dve', lambda e, psn=psn, n=n: e.tensor_scalar(out=rst[:, :n], in0=psn[:, :n], scalar1=1.0 / 128, scalar2=1e-5,
                                                                        op0=ALU.mult, op1=ALU.add), reads=[bpn], writes=[brst])
                    P.op('act', lambda e, n=n: e.activation(out=rst[:, :n], in_=rst[:, :n], func=AF.Sqrt), reads=[brst], writes=[brst])
                    P.op('dve', lambda e, n=n: e.reciprocal(out=rst[:, :n], in_=rst[:, :n]), reads=[brst], writes=[brst])
                    P.op('dve', lambda e, n=n, h=h: e.scalar_tensor_tensor(out=oacc[:, :n], in0=oacc[:, :n], scalar=hnw[:, l, h:h + 1],
                                                                            in1=rst[:, :n], op0=ALU.mult, op1=ALU.mult),
                         reads=[boacc, brst, bconst], writes=[boacc])
                    P.op('act', lambda e, n=n: e.activation(out=tG[:, :n], in_=tG[:, :n], func=AF.Silu), reads=[bG], writes=[bG])
                    P.op('dve', lambda e, n=n: e.tensor_tensor(out=mixo[:, :n], in0=oacc[:, :n], in1=tG[:, :n], op=ALU.mult),
                         reads=[boacc, bG], writes=[bmixo])
                    P.dma('sp', lambda e, h=h, t0=t0, n=n: e.dma_start(out=msc[1280 + 128 * h:1280 + 128 * h + 128, t0:t0 + n], in_=mixo[:, :n]),
                          reads=[bmixo], writes=[bmsc_all[pi]])


        a_reset()
        cosT = a_alloc([128, 16, 128]); sinT = a_alloc([128, 16, 128]); RMp = a_alloc([128, 16, 128]); RMs = a_alloc([128, 16, 128])
        bTab = Buf('s5tab')
        tA = a_alloc([128, 16, 128]); tBt = a_alloc([128, 16, 128]); zr = a_alloc([128, 16, 128]); zi = a_alloc([128, 16, 128])
        bA, bBt, bzr, bzi = Buf('A'), Buf('Bt'), Buf('zr'), Buf('zi')
        cblk = a_alloc([128, 32, 128]); bcblk = Buf('cblk')
        s5fm = sb("s5fm_t", [128, 3, 16]); fmw = sb("fmw", [128, 6, 16]); bfm = Buf('fm')
        Bp = sb("Bp", [128, 2, 2, 4, 128], BF16); bBbar = Buf('Bbar')
        utb = sb("utb", [128, 4, 128], BF16); butb = Buf('utb')
        rmask2 = sb("rmask2_t", [128, 2])
        zer128 = sb("zer128", [128, 128], BF16)
        P.op('dve', lambda e: e.memset(zer128[:], 0.0), writes=[bconst])
        P.dma('sp', lambda e: e.dma_start(out=rmask2[:], in_=rmask2_d), writes=[bconst])
        s5db = sb("s5db_t", [128, 2, 4]); wglu = sb("wglu", [128, 4, 512], BF16); bs5p = Buf('s5p')
        ut = sb("ut", [128, 4, 128]); but = Buf('u')
        xc = sb("xc", [128, 2, 16, NS]); bxc = Buf('xc')
        wk = sb("wk", [128, 6, 16, NS]); bwk = Buf('wk')
        ysb = sb("ysb", [128, 4, 128]); bysb = Buf('ysb')
        yb = sb("yb", [128, 4, 128], BF16); byb = Buf('yb')
        yt = [sb("yt%d" % i, [128, 128]) for i in range(2)]; byt = [Buf() for _ in range(2)]
        TWO_PI = 2.0 * np.pi

        def flat(v):
            return v.rearrange("p a b -> p (a b)")

        def sin_reduce(out, src, tmpf, tmpi, bufs_r, bufs_w):
            rw = list(bufs_r) + list(bufs_w)
            P.op('dve', lambda e: e.tensor_scalar(out=tmpf, in0=src, scalar1=1.0 / TWO_PI, scalar2=None, op0=ALU.mult), reads=rw, writes=bufs_w)
            P.op('dve', lambda e: e.tensor_copy(out=tmpi, in_=tmpf), reads=rw, writes=bufs_w)
            P.op('dve', lambda e: e.tensor_copy(out=tmpf, in_=tmpi), reads=rw, writes=bufs_w)
            P.op('dve', lambda e: e.scalar_tensor_tensor(out=out, in0=tmpf, scalar=-TWO_PI, in1=src, op0=ALU.mult, op1=ALU.add), reads=rw, writes=bufs_w)
            P.op('dve', lambda e: e.tensor_scalar(out=tmpf, in0=out, scalar1=float(np.pi), scalar2=-TWO_PI, op0=ALU.is_gt, op1=ALU.mult), reads=rw, writes=bufs_w)
            P.op('dve', lambda e: e.tensor_tensor(out=out, in0=out, in1=tmpf, op=ALU.add), reads=rw, writes=bufs_w)
            P.op('dve', lambda e: e.tensor_scalar(out=tmpf, in0=out, scalar1=-float(np.pi), scalar2=TWO_PI, op0=ALU.is_lt, op1=ALU.mult), reads=rw, writes=bufs_w)
            P.op('dve', lambda e: e.tensor_tensor(out=out, in0=out, in1=tmpf, op=ALU.add), reads=rw, writes=bufs_w)
            P.op('dve', lambda e: e.tensor_scalar(out=out, in0=out, scalar1=3.1415925, scalar2=-3.1415925, op0=ALU.min, op1=ALU.max), reads=rw, writes=bufs_w)
            P.op('act', lambda e: e.activation(out=out, in_=out, func=AF.Sin), reads=rw, writes=bufs_w)

        def s5_setup(l):
            allb = [bTab, bA, bBt, bzr, bzi]
            P.dma('sp', lambda e: e.dma_start(out=s5fm[:], in_=s5fm_d[:, l]), writes=[bfm])
            P.dma('sp', lambda e: e.dma_start(out=s5db[:], in_=s5db_d[:, l]), writes=[bs5p])
            P.dma('pool', lambda e: e.dma_start(out=wglu[:], in_=s5w_d[:, l]), writes=[bs5p])
            P.dma('sp', lambda e: e.dma_start(out=cblk[:], in_=s5c_d[:, l].rearrange("p a j m -> p (a j) m")), writes=[bcblk])
            P.dma('sp', lambda e: e.dma_start(out=tBt[:, 0:12, :], in_=s5row_d[:, l]), writes=[bBt])
            P.dma('sp', lambda e: e.dma_start(out=tBt[:, 12:16, :], in_=s5b_d[:, l, 0]), writes=[bBt])
            P.dma('sp', lambda e: e.dma_start(out=tA[:, 12:16, :], in_=s5b_d[:, l, 1]), writes=[bA])
            are, aim, ldt = s5fm[:, 0, :], s5fm[:, 1, :], s5fm[:, 2, :]
            dt_, th, lr, rho, abr, abi = (fmw[:, i, :] for i in range(6))
            P.op('act', lambda e: e.activation(out=dt_, in_=ldt, func=AF.Exp), reads=[bfm], writes=[bfm])
            P.op('dve', lambda e: e.tensor_tensor(out=th, in0=aim, in1=dt_, op=ALU.mult), reads=[bfm], writes=[bfm])
            P.op('dve', lambda e: e.tensor_tensor(out=lr, in0=are, in1=dt_, op=ALU.mult), reads=[bfm], writes=[bfm])
            P.op('act', lambda e: e.activation(out=rho, in_=lr, func=AF.Exp), reads=[bfm], writes=[bfm])
            for j in range(16):
                P.op('dve', lambda e, j=j: e.tensor_scalar(out=zr[:, j, :], in0=iota[:], scalar1=fmw[:, 1, j:j + 1], scalar2=None, op0=ALU.mult),
                     reads=[bfm, bconst], writes=[bzr])
                P.op('dve', lambda e, j=j: e.tensor_scalar(out=RMp[:, j, :], in0=cm[:, 2, 0:128], scalar1=fmw[:, 3, j:j + 1], scalar2=None, op0=ALU.mult),
                     reads=[bfm, bconst], writes=[bTab])
                P.op('dve', lambda e, j=j: e.tensor_tensor(out=RMs[:, j, :], in0=RMp[:, j, :], in1=cm[:, 1, 0:128], op=ALU.mult),
                     reads=[bTab, bconst], writes=[bTab])
            tAf = flat(tA[:, 0:12, :])
            zif = flat(zi)
            sin_reduce(flat(sinT), flat(zr), flat(cosT), flat(zi).bitcast(I32), [bzr], [bTab, bzi])
            P.op('dve', lambda e: e.tensor_scalar(out=flat(zr), in0=flat(zr), scalar1=float(np.pi / 2), scalar2=None, op0=ALU.add), reads=[bzr, bTab], writes=[bzr])
            sin_reduce_cos(l)
            P.op('dve', lambda e: e.tensor_tensor(out=abr, in0=rho, in1=cosT[:, :, 1], op=ALU.mult), reads=[bfm, bTab], writes=[bfm])
            P.op('dve', lambda e: e.tensor_tensor(out=abi, in0=rho, in1=sinT[:, :, 1], op=ALU.mult), reads=[bfm, bTab], writes=[bfm])
            R = [flat(tBt[:, 4 * i:4 * i + 4, :]) for i in range(4)]
            ar, ai, ld, bre = R
            bim = flat(tA[:, 12:16, :])
            T = [flat(tA[:, 4 * i:4 * i + 4, :]) for i in range(3)] + [flat(zr[:, 4 * i:4 * i + 4, :]) for i in range(4)]
            tmpi = flat(zi[:, 0:4, :]).bitcast(I32)
            bb = [bA, bBt, bzr, bzi]

            def dv(fn):
                P.op('dve', fn, reads=bb, writes=bb)

            def ac(fn):
                P.op('act', fn, reads=bb, writes=bb)
            ac(lambda e: e.activation(out=ld, in_=ld, func=AF.Exp))
            dv(lambda e: e.tensor_tensor(out=T[0], in0=ai, in1=ld, op=ALU.mult))
            dv(lambda e: e.tensor_tensor(out=T[1], in0=ar, in1=ld, op=ALU.mult))
            ac(lambda e: e.activation(out=T[1], in_=T[1], func=AF.Exp))
            sin_reduce(T[3], T[0], T[5], tmpi, bb, bb)
            dv(lambda e: e.tensor_scalar(out=T[0], in0=T[0], scalar1=float(np.pi / 2), scalar2=None, op0=ALU.add))
            sin_reduce(T[4], T[0], T[5], tmpi, bb, bb)
            dv(lambda e: e.tensor_tensor(out=T[4], in0=T[1], in1=T[4], op=ALU.mult))
            dv(lambda e: e.tensor_scalar(out=T[4], in0=T[4], scalar1=-1.0, scalar2=None, op0=ALU.add))
            dv(lambda e: e.tensor_tensor(out=T[3], in0=T[1], in1=T[3], op=ALU.mult))
            dv(lambda e: e.tensor_tensor(out=T[5], in0=ar, in1=ar, op=ALU.mult))
            dv(lambda e: e.tensor_tensor(out=T[6], in0=ai, in1=ai, op=ALU.mult))
            dv(lambda e: e.tensor_tensor(out=T[5], in0=T[5], in1=T[6], op=ALU.add))
            dv(lambda e: e.reciprocal(out=T[5], in_=T[5]))
            dv(lambda e: e.tensor_tensor(out=T[6], in0=T[4], in1=ar, op=ALU.mult))
            dv(lambda e: e.tensor_tensor(out=T[2], in0=T[3], in1=ai, op=ALU.mult))
            dv(lambda e: e.tensor_tensor(out=T[6], in0=T[6], in1=T[2], op=ALU.add))
            dv(lambda e: e.tensor_tensor(out=T[6], in0=T[6], in1=T[5], op=ALU.mult))
            dv(lambda e: e.tensor_tensor(out=T[2], in0=T[3], in1=ar, op=ALU.mult))
            dv(lambda e: e.tensor_tensor(out=T[0], in0=T[4], in1=ai, op=ALU.mult))
            dv(lambda e: e.tensor_tensor(out=T[2], in0=T[2], in1=T[0], op=ALU.subtract))
            dv(lambda e: e.tensor_tensor(out=T[2], in0=T[2], in1=T[5], op=ALU.mult))
            dv(lambda e: e.tensor_tensor(out=T[0], in0=T[6], in1=bre, op=ALU.mult))
            dv(lambda e: e.tensor_tensor(out=T[1], in0=T[2], in1=bim, op=ALU.mult))
            dv(lambda e: e.tensor_tensor(out=T[0], in0=T[0], in1=T[1], op=ALU.subtract))
            for sl in range(2):
                P.op('dve', lambda e, sl=sl: e.tensor_scalar(out=flat(Bp[:, 0, sl]), in0=T[0], scalar1=rmask2[:, sl:sl + 1], scalar2=None, op0=ALU.mult),
                     reads=bb + [bconst], writes=[bBbar])
            dv(lambda e: e.tensor_tensor(out=T[0], in0=T[6], in1=bim, op=ALU.mult))
            dv(lambda e: e.tensor_tensor(out=T[1], in0=T[2], in1=bre, op=ALU.mult))
            dv(lambda e: e.tensor_tensor(out=T[0], in0=T[0], in1=T[1], op=ALU.add))
            for sl in range(2):
                P.op('dve', lambda e, sl=sl: e.tensor_scalar(out=flat(Bp[:, 1, sl]), in0=T[0], scalar1=rmask2[:, sl:sl + 1], scalar2=None, op0=ALU.mult),
                     reads=bb + [bconst], writes=[bBbar])

        def sin_reduce_cos(l):
            sin_reduce(flat(cosT), flat(zr), flat(tA), flat(zi).bitcast(I32), [bzr], [bTab, bzi, bA])
            P.dma('sp', lambda e: e.dma_start(out=tA[:, 12:16, :], in_=s5b_d[:, l, 1]), writes=[bA])

        pieces128 = [(i * 128, 128, False) for i in range(cfg.tp // 128)] + [(cfg.tp, 128, True)]

        def s5(l):
            s5_setup(l)
            if cfg.stop <= 1:
                return
            cflat, sflat = flat(cosT), flat(sinT)
            PRv = psall[:, 0:4, :].rearrange("p b c -> p (b c)")
            PIv = psall[:, 4:8, :].rearrange("p b c -> p (b c)")
            bPR, bPI = bps[0:4], bps[4:8]
            SK = int(os.environ.get('S5SKIP', 0))
            if not SK & 1:
                P.op('dve', lambda e: e.memset(xc[:].rearrange("p a b c -> p (a b c)"), 0.0), writes=[bxc])
            for (t0, n, samp) in pieces128:
                bi = min(t0 // TB, len(cfg.blocks) - 1)
                P.dma('sp', lambda e, t0=t0: e.dma_start(out=ut[:], in_=psc[0:512, t0:t0 + 128].rearrange("(q p) t -> p q t", p=128)),
                      reads=[bpsc_all[bi]], writes=[but])
                if samp:
                    P.dma('sp', lambda e: e.dma_start(out=xc[:], in_=s5x0_d[:, l]), writes=[bxc])
                if not SK & 2:
                    P.op('act', lambda e: e.activation(out=utb[:], in_=ut[:], func=AF.Copy), reads=[but], writes=[butb])
                for j in range(16):
                    q, r, sl = j // 4, 64 * ((j % 4) // 2), j % 2
                    psR, bpR = ps_next()
                    psI, bpI = ps_next()
                    for c_, pso_, bpo_ in ((0, psR, bpR), (1, psI, bpI)):
                        P.op('pe', lambda e, q=q, r=r, sl=sl, c_=c_, pso_=pso_: e.matmul(
                            pso_[:, 0:128], lhsT=Bp[r:r + 64, c_, sl, q, :], rhs=utb[r:r + 64, q, :],
                            start=True, stop=True), reads=[bBbar, butb], writes=[bpo_])
                    PRb, PIb = psR[:, 0:128], psI[:, 0:128]
                    cf_, sf_ = cosT[:, j, :], sinT[:, j, :]
                    fA, fB, fzr, fzi = tA[:, j, :], tBt[:, j, :], zr[:, j, :], zi[:, j, :]
                    P.op('dve', lambda e, PRb=PRb, cf_=cf_, fA=fA: e.tensor_tensor(out=fA, in0=PRb, in1=cf_, op=ALU.mult), reads=[bpR, bTab], writes=[bA])
                    P.op('dve', lambda e, PIb=PIb, sf_=sf_, fB=fB: e.tensor_tensor(out=fB, in0=PIb, in1=sf_, op=ALU.mult), reads=[bpI, bTab], writes=[bBt])
                    P.op('dve', lambda e, fA=fA, fB=fB, fzr=fzr: e.tensor_tensor(out=fzr, in0=fA, in1=fB, op=ALU.add), reads=[bA, bBt], writes=[bzr])
                    P.op('dve', lambda e, PIb=PIb, cf_=cf_, fA=fA: e.tensor_tensor(out=fA, in0=PIb, in1=cf_, op=ALU.mult), reads=[bpI, bTab, bzr], writes=[bA])
                    P.op('dve', lambda e, PRb=PRb, sf_=sf_, fB=fB: e.tensor_tensor(out=fB, in0=PRb, in1=sf_, op=ALU.mult), reads=[bpR, bTab, bzr], writes=[bBt])
                    P.op('dve', lambda e, fA=fA, fB=fB, fzi=fzi: e.tensor_tensor(out=fzi, in0=fA, in1=fB, op=ALU.subtract), reads=[bA, bBt], writes=[bzi])
                if cfg.stop <= 2:
                    return
                nc_ = NS if samp else 1
                Xr, Xi = xc[:, 0, :, 0:nc_], xc[:, 1, :, 0:nc_]
                w = [wk[:, i, :, 0:nc_] for i in range(6)]
                if samp:
                    abr_b = fmw[:, 4, :].unsqueeze(2).to_broadcast([128, 16, NS])
                    abi_b = fmw[:, 5, :].unsqueeze(2).to_broadcast([128, 16, NS])
                    csel = lambda tbl: tbl[:, :, 0:128:8]
                else:
                    abr_b = fmw[:, 4, :].unsqueeze(2)
                    abi_b = fmw[:, 5, :].unsqueeze(2)
                    csel = lambda tbl: tbl[:, :, 0:1]
                rb_ = [bxc, bwk, bfm, bTab]

                def dw(fn):
                    P.op('dve', fn, reads=rb_, writes=[bwk])
                dw(lambda e, Xr=Xr, abr_b=abr_b, w=w: e.tensor_tensor(out=w[0], in0=Xr, in1=abr_b, op=ALU.mult))
                dw(lambda e, Xi=Xi, abi_b=abi_b, w=w: e.tensor_tensor(out=w[1], in0=Xi, in1=abi_b, op=ALU.mult))
                dw(lambda e, w=w: e.tensor_tensor(out=w[0], in0=w[0], in1=w[1], op=ALU.subtract))
                dw(lambda e, Xi=Xi, abr_b=abr_b, w=w: e.tensor_tensor(out=w[1], in0=Xi, in1=abr_b, op=ALU.mult))
                dw(lambda e, Xr=Xr, abi_b=abi_b, w=w: e.tensor_tensor(out=w[2], in0=Xr, in1=abi_b, op=ALU.mult))
                dw(lambda e, w=w: e.tensor_tensor(out=w[1], in0=w[1], in1=w[2], op=ALU.add))
                cs, ss = csel(cosT), csel(sinT)
                dw(lambda e, w=w, cs=cs: e.tensor_tensor(out=w[2], in0=w[0], in1=cs, op=ALU.mult))
                dw(lambda e, w=w, ss=ss: e.tensor_tensor(out=w[3], in0=w[1], in1=ss, op=ALU.mult))
                dw(lambda e, w=w: e.tensor_tensor(out=w[2], in0=w[2], in1=w[3], op=ALU.add))
                dw(lambda e, w=w, cs=cs: e.tensor_tensor(out=w[3], in0=w[1], in1=cs, op=ALU.mult))
                dw(lambda e, w=w, ss=ss: e.tensor_tensor(out=w[4], in0=w[0], in1=ss, op=ALU.mult))
                dw(lambda e, w=w: e.tensor_tensor(out=w[3], in0=w[3], in1=w[4], op=ALU.subtract))
                zrc, zic = csel(zr), csel(zi)
                P.op('dve', lambda e, zrc=zrc, w=w: e.tensor_tensor(out=zrc, in0=zrc, in1=w[2], op=ALU.add), reads=[bwk, bzr], writes=[bzr])
                P.op('dve', lambda e, zic=zic, w=w: e.tensor_tensor(out=zic, in0=zic, in1=w[3], op=ALU.add), reads=[bwk, bzi], writes=[bzi])
                if cfg.stop <= 3:
                    return
                RM = flat(RMs) if samp else flat(RMp)
                P.op('dve', lambda e, RM=RM: e.tensor_tensor_scan(out=flat(tA), data0=RM, data1=flat(zr), initial=0.0, op0=ALU.mult, op1=ALU.add),
                     reads=[bTab, bzr], writes=[bA])
                P.op('dve', lambda e, RM=RM: e.tensor_tensor_scan(out=flat(tBt), data0=RM, data1=flat(zi), initial=0.0, op0=ALU.mult, op1=ALU.add),
                     reads=[bTab, bzi], writes=[bBt])
                P.op('dve', lambda e: e.tensor_tensor(out=flat(zr), in0=flat(tA), in1=cflat, op=ALU.mult), reads=[bA, bTab], writes=[bzr])
                P.op('dve', lambda e: e.tensor_tensor(out=flat(zi), in0=flat(tBt), in1=sflat, op=ALU.mult), reads=[bBt, bTab], writes=[bzi])
                P.op('dve', lambda e: e.tensor_tensor(out=flat(zr), in0=flat(zr), in1=flat(zi), op=ALU.subtract), reads=[bzr, bzi], writes=[bzr])
                P.op('dve', lambda e: e.tensor_tensor(out=flat(zi), in0=flat(tA), in1=sflat, op=ALU.mult), reads=[bA, bTab, bzr], writes=[bzi])
                P.op('dve', lambda e: e.tensor_tensor(out=flat(tA), in0=flat(tBt), in1=cflat, op=ALU.mult), reads=[bBt, bTab, bzi], writes=[bA])
                P.op('dve', lambda e: e.tensor_tensor(out=flat(zi), in0=flat(zi), in1=flat(tA), op=ALU.add), reads=[bzi, bA], writes=[bzi])
                if cfg.stop <= 4:
                    return
                if samp:
                    P.op('act', lambda e: e.activation(out=xc[:, 0], in_=zr[:, :, 7:128:8], func=AF.Copy), reads=[bzr, bwk], writes=[bxc])
                    P.op('act', lambda e: e.activation(out=xc[:, 1], in_=zi[:, :, 7:128:8], func=AF.Copy), reads=[bzi, bwk], writes=[bxc])
                    P.dma('sp', lambda e: e.dma_start(out=s5s_d[:, l], in_=xc[:]), reads=[bxc])
                else:
                    P.op('act', lambda e: e.activation(out=xc[:, 0, :, 0:1], in_=zr[:, :, 127:128], func=AF.Copy), reads=[bzr, bwk], writes=[bxc])
                    P.op('act', lambda e: e.activation(out=xc[:, 1, :, 0:1], in_=zi[:, :, 127:128], func=AF.Copy), reads=[bzi, bwk], writes=[bxc])
                    if t0 + 128 == cfg.tp:
                        P.dma('sp', lambda e: e.dma_start(out=s5p_d[:, l], in_=xc[:, :, :, 0], allow_slow_non_contiguous=True), reads=[bxc])
                for q in range(4):
                    psA, bpA = ps_next()
                    psB, bpB = ps_next()
                    for jj in range(4):
                        j = 4 * q + jj
                        P.op('pe', lambda e, psA=psA, j=j, jj=jj: e.matmul(psA[:, 0:128], lhsT=cblk[:, j, :], rhs=zr[:, j, :], start=(jj == 0), stop=(jj == 3)),
                             reads=[bcblk, bzr], writes=[bpA])
                        P.op('pe', lambda e, psB=psB, j=j, jj=jj: e.matmul(psB[:, 0:128], lhsT=cblk[:, 16 + j, :], rhs=zi[:, j, :], start=(jj == 0), stop=(jj == 3)),
                             reads=[bcblk, bzi], writes=[bpB])
                    P.op('act', lambda e, psB=psB: e.activation(out=yt[0][:], in_=psB[:, 0:128], func=AF.Copy), reads=[bpB], writes=[byt[0]])
                    P.op('dve', lambda e, psA=psA: e.tensor_tensor(out=yt[0][:], in0=psA[:, 0:128], in1=yt[0][:], op=ALU.subtract), reads=[bpA, byt[0]], writes=[byt[0]])
                    P.op('dve', lambda e, q=q: e.scalar_tensor_tensor(out=yt[0][:], in0=ut[:, q, :], scalar=s5db[:, 0, q:q + 1], in1=yt[0][:], op0=ALU.mult, op1=ALU.add),
                         reads=[but, bs5p, byt[0]], writes=[byt[0]])
                    P.op('dve', lambda e: e.tensor_tensor(out=yt[1][:], in0=yt[0][:], in1=yt[0][:], op=ALU.mult), reads=[byt[0]], writes=[byt[1]])
                    P.op('dve', lambda e: e.tensor_scalar(out=yt[1][:], in0=yt[1][:], scalar1=0.044715, scalar2=1.0, op0=ALU.mult, op1=ALU.add), reads=[byt[1]], writes=[byt[1]])
                    P.op('dve', lambda e: e.tensor_tensor(out=yt[1][:], in0=yt[1][:], in1=yt[0][:], op=ALU.mult), reads=[byt[0], byt[1]], writes=[byt[1]])
                    P.op('act', lambda e: e.activation(out=yt[1][:], in_=yt[1][:], func=AF.Sigmoid, scale=1.5957691216057308), reads=[byt[1]], writes=[byt[1]])
                    P.op('dve', lambda e, q=q: e.tensor_tensor(out=ysb[:, q, :], in0=yt[0][:], in1=yt[1][:], op=ALU.mult), reads=[byt[0], byt[1]], writes=[bysb])
                    P.op('act', lambda e, q=q: e.activation(out=yb[:, q, :], in_=ysb[:, q, :], func=AF.Copy), reads=[bysb], writes=[byb])
                for q2 in range(4):
                    ps, bp = ps_next()
                    for q in range(4):
                        P.op('pe', lambda e, ps=ps, q=q, q2=q2: e.matmul(ps[:, 0:128], lhsT=wglu[:, q, q2 * 128:(q2 + 1) * 128], rhs=yb[:, q, :],
                                                                         start=(q == 0), stop=(q == 3)), reads=[bs5p, byb], writes=[bp])
                    P.op('act', lambda e, ps=ps, q2=q2: e.activation(out=yt[1][:], in_=ps[:, 0:128], func=AF.Sigmoid, bias=s5db[:, 1, q2:q2 + 1]),
                         reads=[bp, bs5p], writes=[byt[1]])
                    P.op('dve', lambda e, q2=q2: e.tensor_tensor(out=mixo[:, 0:128], in0=ysb[:, q2, :], in1=yt[1][:], op=ALU.mult), reads=[bysb, byt[1]], writes=[bmixo])
                    P.dma('sp', lambda e, q2=q2, t0=t0: e.dma_start(out=msc[q2 * 128:(q2 + 1) * 128, t0:t0 + 128], in_=mixo[:, 0:128]),
                          reads=[bmixo], writes=[bmsc_all[bi]])


        a_reset()
        zt = a_alloc([128, 20, 128]); zp = a_alloc([128, 20, 128]); bzt, bzp = Buf('zt'), Buf('zp')
        F6 = [a_alloc([128, 6, 128]) for _ in range(11)]; bF6 = [Buf('f6_%d' % i) for i in range(11)]
        F6.append(F6[1]); bF6.append(bF6[1])
        ARt = a_alloc([128, 6, 256], BF16); bAR = Buf('AR')
        ktl = a_alloc([128, 6, 128], BF16); btl = a_alloc([128, 6, 128], BF16); bktl, bbtl = Buf('ktl'), Buf('btl')
        twa = a_alloc([128, 128], BF16); sgi = a_alloc([128, 128], BF16); btwa, bsgi = Buf(), Buf()
        vA = a_alloc([128, 128], BF16); vB = a_alloc([128, 128], BF16); bvv = Buf('vAB')
        ktm_ = a_alloc([128, 128], BF16); btm_ = a_alloc([128, 128], BF16); bktm_, bbtm_ = Buf(), Buf()
        NBm = [a_alloc([128, 256], BF16) for _ in range(2)]; KAm = [a_alloc([128, 256], BF16) for _ in range(2)]
        bNB = [Buf() for _ in range(2)]; bKA = [Buf() for _ in range(2)]
        Xm = [[a_alloc([128, 128], BF16) for _ in range(2)] for _ in range(2)]
        Ym = [[a_alloc([128, 128], BF16) for _ in range(2)] for _ in range(2)]
        Ttm = [a_alloc([128, 128], BF16) for _ in range(2)]; Tnm = [a_alloc([128, 128], BF16) for _ in range(2)]
        bXm = [[Buf() for _ in range(2)] for _ in range(2)]; bYm = [[Buf() for _ in range(2)] for _ in range(2)]
        bTt = [Buf() for _ in range(2)]; bTn = [Buf() for _ in range(2)]
        Ttf = [a_alloc([128, 128]) for _ in range(2)]; Tnf = [a_alloc([128, 128]) for _ in range(2)]
        Hblk = a_alloc([128, 6, 128]); Hbb = a_alloc([128, 6, 128], BF16); bH, bHb = Buf('H'), Buf('Hb')
        Hs = a_alloc([128, NS, 128]); Hsb = a_alloc([128, NS, 128], BF16); bHs, bHsb = Buf('Hs'), Buf('Hsb')
        ztf = zt.rearrange("p a b -> p (a b)")
        UmA = ztf[:, 0:1024].bitcast(BF16).rearrange("p (a b) -> p a b", a=NS); bUm = bzt
        VmA = ztf[:, 1024:2048].bitcast(BF16).rearrange("p (a b) -> p a b", a=NS); bVm = bzt
        WpA = a_alloc([128, 128], BF16); WpB = a_alloc([128, 128], BF16); bWp = Buf('Wp')
        UpA = a_alloc([128, 128], BF16); UpB = a_alloc([128, 128], BF16); bUp = Buf('Up')
        t128 = [a_alloc([128, 128]) for _ in range(3)]; bt128 = [Buf() for _ in range(3)]
        yfm = a_alloc([128, 6, 128]); byfm = Buf('yfm')
        shc = a_alloc([128, 20, NS]); bshc = Buf('shc')
        rwp = sb("rwp_t", [128, 62]); brwp = Buf('rwp')
        rwl = sb("rwl_t", [128, 2, 768], BF16); brwl = Buf('rwl')
        rwm = sb("rwm_t", [128, 2, 384], BF16); blk64 = sb("blk64_t", [128, 128]); cm6 = sb("cm6_t", [128, 2, 768], BF16)
        for t_, d_ in ((rwm, rwm_d), (blk64, blk64_d), (cm6, cm6_d)):
            P.dma('sp', lambda e, t_=t_, d_=d_: e.dma_start(out=t_[:], in_=d_), writes=[bconst])

        def rwkv(l):
            Dv = lambda fn, r, w: P.op('dve', fn, reads=r, writes=w)
            Ac = lambda fn, r, w: P.op('act', fn, reads=r, writes=w)
            Pe = lambda fn, r, w: P.op('pe', fn, reads=r, writes=w)
            P.dma('sp', lambda e: e.dma_start(out=rwp[:], in_=rwp_d[:, l]), writes=[brwp])
            P.dma('pool', lambda e: e.dma_start(out=rwl[:], in_=rwl_d[:, l]), writes=[brwl])
            mu = rwp[:, 0:20]
            w0, a0, k_k, k_a, r_k, ln_w, ln_b = (rwp[:, 20 + 6 * i:26 + 6 * i] for i in range(7))
            sg_a, lw, cc, ec, enc, ecm, kk, kp, gg, tm1, tm2, tm3 = F6
            b_a, b_lw, b_cc, b_ec, b_enc, b_ecm, b_kk, b_kp, b_gg, b_tm1, b_tm2, b_tm3 = bF6
            Dv(lambda e: e.memset(Hblk[:].rearrange("p a b -> p (a b)"), 0.0), [], [bH])
            Dv(lambda e: e.memset(Hbb[:].rearrange("p a b -> p (a b)"), 0.0), [], [bHb])
            for z_ in (WpA, WpB, UpA, UpB, vA, vB):
                Dv(lambda e, z_=z_: e.memset(z_[:], 0.0), [], [bWp, bUp, bvv])
            for (t0, n, samp) in pieces128:
                bi = min(t0 // TB, len(cfg.blocks) - 1)
                mi = 1 if samp else 0
                NLEV = 2 if samp else 6
                rd = [bpsc_all[bi]] + ([bpsc_all[bi - 1]] if bi > 0 else [])
                P.dma('sp', lambda e, t0=t0: e.dma_start(out=zt[:], in_=psc[512:3072, t0:t0 + 128].rearrange("(c p) t -> p c t", p=128)),
                      reads=rd, writes=[bzt])
                if t0 == 0:
                    Dv(lambda e: e.memset(zp[:, :, 0:1], 0.0), [], [bzp])
                    P.dma('sp', lambda e: e.dma_start(out=zp[:, :, 1:128], in_=psc[512:3072, 0:127].rearrange("(c p) t -> p c t", p=128)),
                          reads=rd, writes=[bzp])
                else:
                    P.dma('sp', lambda e, t0=t0: e.dma_start(out=zp[:], in_=psc[512:3072, t0 - 1:t0 + 127].rearrange("(c p) t -> p c t", p=128)),
                          reads=rd, writes=[bzp])
                if samp:
                    P.dma('sp', lambda e: e.dma_start(out=shc[:], in_=rwsh0_d[:, l]), writes=[bshc])
                    Dv(lambda e: e.tensor_copy(out=zp[:, :, 0:128:8], in_=shc[:]), [bshc, bzp], [bzp])
                    Ac(lambda e: e.activation(out=shc[:], in_=zt[:, :, 7:128:8], func=AF.Copy), [bzt, bzp], [bshc])
                    P.dma('sp', lambda e: e.dma_start(out=rwshs_d[:, l], in_=shc[:]), reads=[bshc])
                elif t0 + 128 == cfg.tp:
                    P.dma('sp', lambda e: e.dma_start(out=rwshp_d[:, l], in_=zt[:, :, 127], allow_slow_non_contiguous=True), reads=[bzt])
                Dv(lambda e: e.tensor_tensor(out=zp[:], in0=zp[:], in1=zt[:], op=ALU.subtract), [bzp, bzt], [bzp])
                Dv(lambda e: e.tensor_tensor(out=zp[:], in0=zp[:], in1=mu.unsqueeze(2).to_broadcast([128, 20, 128]), op=ALU.mult), [bzp, brwp], [bzp])
                Dv(lambda e: e.tensor_tensor(out=zp[:], in0=zp[:], in1=zt[:], op=ALU.add), [bzp, bzt], [bzp])
                zr_, zk_, zv_ = zp[:, 0:6, :], zp[:, 6:12, :], zp[:, 12:18, :]
                Ac(lambda e: e.activation(out=twa[0:64, :], in_=zp[0:64, 18, :], func=AF.Tanh), [bzp], [btwa])
                Ac(lambda e: e.activation(out=twa[64:128, :], in_=zp[64:128, 18, :], func=AF.Copy), [bzp], [btwa])
                Ac(lambda e: e.activation(out=sgi[:], in_=zp[:, 19, :], func=AF.Sigmoid), [bzp], [bsgi])
                for c in range(6):
                    ps, bp = ps_next()
                    Pe(lambda e, ps=ps, c=c: e.matmul(ps[:, 0:128], lhsT=rwl[0:64, 0, c * 128:(c + 1) * 128], rhs=twa[0:64, :], start=True, stop=True),
                       [brwl, btwa], [bp])
                    Ac(lambda e, ps=ps, c=c: e.activation(out=lw[:, c, :], in_=ps[:, 0:128], func=AF.Sigmoid, bias=w0[:, c:c + 1]), [bp, brwp], [b_lw])
                    ps, bp = ps_next()
                    Pe(lambda e, ps=ps, c=c: e.matmul(ps[:, 0:128], lhsT=rwl[64:128, 0, c * 128:(c + 1) * 128], rhs=twa[64:128, :], start=True, stop=True),
                       [brwl, btwa], [bp])
                    Ac(lambda e, ps=ps, c=c: e.activation(out=sg_a[:, c, :], in_=ps[:, 0:128], func=AF.Sigmoid, bias=a0[:, c:c + 1]), [bp, brwp], [b_a])
                    ps, bp = ps_next()
                    Pe(lambda e, ps=ps, c=c: e.matmul(ps[:, 0:128], lhsT=rwl[:, 1, c * 128:(c + 1) * 128], rhs=sgi[:], start=True, stop=True),
                       [brwl, bsgi], [bp])
                    Ac(lambda e, ps=ps, c=c: e.activation(out=gg[:, c, :], in_=ps[:, 0:128], func=AF.Copy), [bp], [b_gg])
                Dv(lambda e: e.tensor_scalar(out=lw[:], in0=lw[:], scalar1=-0.6065306597126334, scalar2=None, op0=ALU.mult), [b_lw], [b_lw])
                Dv(lambda e: e.tensor_tensor(out=kk[:], in0=zk_, in1=k_k.unsqueeze(2).to_broadcast([128, 6, 128]), op=ALU.mult), [bzp, brwp], [b_kk])
                Dv(lambda e: e.tensor_tensor(out=tm1[:], in0=kk[:], in1=kk[:], op=ALU.mult), [b_kk], [b_tm1])
                for c in range(6):
                    ps, bp = ps_next()
                    Pe(lambda e, ps=ps, c=c: e.matmul(ps[:, 0:128], lhsT=blk64[:], rhs=tm1[:, c, :], start=True, stop=True), [bconst, b_tm1], [bp])
                    Dv(lambda e, ps=ps, c=c: e.tensor_scalar(out=tm2[:, c, :], in0=ps[:, 0:128], scalar1=1e-24, scalar2=None, op0=ALU.max), [bp], [b_tm2])
                Ac(lambda e: e.activation(out=tm2[:], in_=tm2[:], func=AF.Sqrt), [b_tm2], [b_tm2])
                Dv(lambda e: e.reciprocal(out=tm2[:], in_=tm2[:]), [b_tm2], [b_tm2])
                Dv(lambda e: e.tensor_tensor(out=kk[:], in0=kk[:], in1=tm2[:], op=ALU.mult), [b_kk, b_tm2], [b_kk])
                Dv(lambda e: e.tensor_scalar(out=tm1[:], in0=sg_a[:], scalar1=-1.0, scalar2=None, op0=ALU.add), [b_a, b_tm1], [b_tm1])
                Dv(lambda e: e.tensor_tensor(out=tm1[:], in0=tm1[:], in1=k_a.unsqueeze(2).to_broadcast([128, 6, 128]), op=ALU.mult), [b_tm1, brwp], [b_tm1])
                Dv(lambda e: e.tensor_scalar(out=tm1[:], in0=tm1[:], scalar1=1.0, scalar2=None, op0=ALU.add), [b_tm1], [b_tm1])
                Dv(lambda e: e.tensor_tensor(out=kp[:], in0=zk_, in1=tm1[:], op=ALU.mult), [bzp, b_tm1], [b_kp])
                Dv(lambda e, mi=mi: e.tensor_tensor_scan(out=cc[:].rearrange("p a b -> p (a b)"), data0=cm6[:, mi, :], data1=lw[:].rearrange("p a b -> p (a b)"),
                                                         initial=0.0, op0=ALU.mult, op1=ALU.add), [b_lw, bconst], [b_cc])
                Ac(lambda e: e.activation(out=ec[:], in_=cc[:], func=AF.Exp), [b_cc], [b_ec])
                Ac(lambda e: e.activation(out=enc[:], in_=cc[:], func=AF.Exp, scale=-1.0), [b_cc], [b_enc])
                Dv(lambda e: e.tensor_tensor(out=ecm[:], in0=cc[:], in1=lw[:], op=ALU.subtract), [b_cc, b_lw], [b_ecm])
                Ac(lambda e: e.activation(out=ecm[:], in_=ecm[:], func=AF.Exp), [b_ecm], [b_ecm])
                Dv(lambda e: e.tensor_tensor(out=tm1[:], in0=kk[:], in1=ecm[:], op=ALU.mult), [b_kk, b_ecm, b_tm1], [b_tm1])
                Dv(lambda e: e.tensor_scalar(out=ARt[:, :, 0:128], in0=tm1[:], scalar1=-1.0, scalar2=None, op0=ALU.mult), [b_tm1], [bAR])
                Dv(lambda e: e.tensor_tensor(out=ARt[:, :, 128:256], in0=zr_, in1=ec[:], op=ALU.mult), [bzp, b_ec], [bAR])
                Dv(lambda e: e.tensor_tensor(out=tm2[:], in0=kp[:], in1=enc[:], op=ALU.mult), [b_kp, b_enc, b_tm2], [b_tm2])
                Dv(lambda e: e.tensor_tensor(out=tm3[:], in0=kk[:], in1=sg_a[:], op=ALU.mult), [b_kk, b_a], [b_tm3])
                Dv(lambda e: e.tensor_tensor(out=tm3[:], in0=tm3[:], in1=enc[:], op=ALU.mult), [b_tm3, b_enc], [b_tm3])
                Ac(lambda e: e.activation(out=ktl[:], in_=tm2[:], func=AF.Copy), [b_tm2], [bktl])
                Ac(lambda e: e.activation(out=btl[:], in_=tm3[:], func=AF.Copy), [b_tm3], [bbtl])
                for c in range(6):
                    ps, bp = ps_next()
                    Pe(lambda e, ps=ps, c=c: e.transpose(ps[:, 0:128], zp[:, 12 + c, :], ident[:]), [bzp, bconst], [bp])
                    Ac(lambda e, ps=ps: e.activation(out=vA[:, 0:64], in_=ps[:, 0:64], func=AF.Copy), [bp], [bvv])
                    Ac(lambda e, ps=ps: e.activation(out=vB[:, 64:128], in_=ps[:, 64:128], func=AF.Copy), [bp], [bvv])
                    ps, bp = ps_next()
                    Pe(lambda e, ps=ps, c=c: e.transpose(ps[:, 0:128], tm2[:, c, :], ident[:]), [b_tm2, bconst], [bp])
                    Ac(lambda e, ps=ps: e.activation(out=ktm_[:], in_=ps[:, 0:128], func=AF.Copy), [bp], [bktm_])
                    ps, bp = ps_next()
                    Pe(lambda e, ps=ps, c=c: e.transpose(ps[:, 0:128], tm3[:, c, :], ident[:]), [b_tm3, bconst], [bp])
                    Ac(lambda e, ps=ps: e.activation(out=btm_[:], in_=ps[:, 0:128], func=AF.Copy), [bp], [bbtm_])
                    for hp in range(2):
                        r0 = 64 * hp
                        ps, bp = ps_next()
                        Pe(lambda e, ps=ps, c=c, r0=r0: e.matmul(ps[:, 0:256], lhsT=btl[r0:r0 + 64, c, :], rhs=ARt[r0:r0 + 64, c, :], start=True, stop=True),
                           [bbtl, bAR], [bp])
                        Dv(lambda e, ps=ps, hp=hp, mi=mi: e.tensor_tensor(out=NBm[hp][:], in0=ps[:, 0:256], in1=rwm[:, mi, 0:256], op=ALU.mult), [bp, bconst], [bNB[hp]])
                        ps, bp = ps_next()
                        Pe(lambda e, ps=ps, c=c, r0=r0: e.matmul(ps[:, 0:256], lhsT=ktl[r0:r0 + 64, c, :], rhs=ARt[r0:r0 + 64, c, :], start=True, stop=True),
                           [bktl, bAR], [bp])
                        Dv(lambda e, ps=ps, hp=hp, mi=mi: e.tensor_tensor(out=KAm[hp][:], in0=ps[:, 0:256], in1=rwm[:, mi, 0:256], op=ALU.mult), [bp, bconst], [bKA[hp]])
                        ps, bp = ps_next()
                        Pe(lambda e, ps=ps, c=c, r0=r0: e.matmul(ps[:, 0:128], lhsT=ARt[r0:r0 + 64, c, 0:128], rhs=btl[r0:r0 + 64, c, :], start=True, stop=True),
                           [bbtl, bAR], [bp])
                        Dv(lambda e, ps=ps, hp=hp, mi=mi: e.tensor_tensor(out=Ym[hp][0][:], in0=ps[:, 0:128], in1=rwm[:, mi, 256:384], op=ALU.mult), [bp, bconst], [bYm[hp][0]])
                        X0 = NBm[hp][:, 0:128]
                        Dv(lambda e, hp=hp, X0=X0: e.tensor_tensor(out=Ttf[hp][:], in0=X0, in1=ident[:], op=ALU.add), [bNB[hp], bconst], [bTt[hp]])
                        Dv(lambda e, hp=hp: e.tensor_tensor(out=Tnf[hp][:], in0=Ym[hp][0][:], in1=ident[:], op=ALU.add), [bYm[hp][0], bconst], [bTn[hp]])
                        Ac(lambda e, hp=hp: e.activation(out=Ttm[hp][:], in_=Ttf[hp][:], func=AF.Copy), [bTt[hp]], [bTt[hp]])
                        Ac(lambda e, hp=hp: e.activation(out=Tnm[hp][:], in_=Tnf[hp][:], func=AF.Copy), [bTn[hp]], [bTn[hp]])
                        Xp, bXp = X0, bNB[hp]
                        Yp, bYp = Ym[hp][0][:], bYm[hp][0]
                        for lev in range(1, NLEV + 1):
                            cur = lev % 2
                            last = (lev == NLEV)
                            ps, bp = ps_next()
                            Pe(lambda e, ps=ps, Xp=Xp, Yp=Yp: e.matmul(ps[:, 0:128], lhsT=Yp, rhs=Xp, start=True, stop=True), [bXp, bYp], [bp])
                            Xc, bXc = Xm[hp][cur][:], bXm[hp][cur]
                            Ac(lambda e, ps=ps, Xc=Xc: e.activation(out=Xc, in_=ps[:, 0:128], func=AF.Copy), [bp], [bXc])
                            if not last:
                                ps2, bp2 = ps_next()
                                Pe(lambda e, ps2=ps2, Xp=Xp, Yp=Yp: e.matmul(ps2[:, 0:128], lhsT=Xp, rhs=Yp, start=True, stop=True), [bXp, bYp], [bp2])
                                Yc, bYc = Ym[hp][cur][:] if cur == 1 else Xm[hp][0][:], None
                            ps3, bp3 = ps_next()
                            Pe(lambda e, ps3=ps3, hp=hp, Xc=Xc: e.matmul(ps3[:, 0:128], lhsT=Tnm[hp][:], rhs=Xc, start=True, stop=True), [bTn[hp], bXc], [bp3])
                            if not last:
                                ps4, bp4 = ps_next()
                                Pe(lambda e, ps4=ps4, hp=hp, Xc=Xc: e.matmul(ps4[:, 0:128], lhsT=Xc, rhs=Tnm[hp][:], start=True, stop=True), [bTn[hp], bXc], [bp4])
                            Dv(lambda e, ps3=ps3, hp=hp: e.tensor_tensor(out=Ttf[hp][:], in0=ps3[:, 0:128], in1=Ttf[hp][:], op=ALU.add), [bp3, bTt[hp]], [bTt[hp]])
                            Ac(lambda e, hp=hp: e.activation(out=Ttm[hp][:], in_=Ttf[hp][:], func=AF.Copy), [bTt[hp]], [bTt[hp]])
                            if not last:
                                Yc, bYc = Ym[hp][cur][:], bYm[hp][cur]
                                Ac(lambda e, ps2=ps2, Yc=Yc: e.activation(out=Yc, in_=ps2[:, 0:128], func=AF.Copy), [bp2], [bYc])
                                Dv(lambda e, ps4=ps4, hp=hp: e.tensor_tensor(out=Tnf[hp][:], in0=ps4[:, 0:128], in1=Tnf[hp][:], op=ALU.add), [bp4, bTn[hp]], [bTn[hp]])
                                Ac(lambda e, hp=hp: e.activation(out=Tnm[hp][:], in_=Tnf[hp][:], func=AF.Copy), [bTn[hp]], [bTn[hp]])
                                Xp, bXp, Yp, bYp = Xc, bXc, Yc, bYc
                    if samp:
                        P.dma('sp', lambda e, c=c: e.dma_start(out=Hs[:], in_=rwh0_d[:, l, c]), writes=[bHs])
                        Ac(lambda e: e.activation(out=Hsb[:], in_=Hs[:], func=AF.Copy), [bHs], [bHsb])
                    psW, bpW = ps_next()
                    if not samp:
                        Pe(lambda e, psW=psW, c=c: e.matmul(psW[:, 0:128], lhsT=ARt[:, c, 0:128], rhs=Hbb[:, c, :], start=True, stop=False), [bAR, bHb], [bpW])
                    else:
                        wsel = t128[0]
                        for g4 in range(4):
                            psq, bpq = ps_next()
                            Pe(lambda e, psq=psq, c=c, g4=g4: e.matmul(psq[:, :], lhsT=ARt[:, c, 0:128], rhs=Hsb[:, 4 * g4:4 * g4 + 4, :].rearrange("p a b -> p (a b)"),
                                                                      start=True, stop=True), [bAR, bHsb], [bpq])
                            Dv(lambda e, psq=psq, g4=g4: e.tensor_tensor(out=UmA[:, 4 * g4:4 * g4 + 4, :].rearrange("p a b -> p (a b)"), in0=psq[:, :],
                                                                       in1=bm16[:, 4 * g4:4 * g4 + 4, :].rearrange("p a b -> p (a b)"), op=ALU.mult), [bpq, bconst], [bUm])
                        Dv(lambda e: e.tensor_reduce(out=t128[0][:], in_=UmA[:].rearrange("p n v -> p v n"), axis=mybir.AxisListType.X, op=ALU.add), [bUm], [bt128[0]])
                        Ac(lambda e: e.activation(out=WpA[:, 0:64], in_=t128[0][:, 0:64], func=AF.Copy), [bt128[0]], [bWp])
                        Ac(lambda e: e.activation(out=WpB[:, 64:128], in_=t128[0][:, 64:128], func=AF.Copy), [bt128[0]], [bWp])
                    for hp, vv in ((0, vA), (1, vB)):
                        Pe(lambda e, psW=psW, hp=hp, vv=vv, st_=(samp and hp == 0): e.matmul(psW[:, 0:128], lhsT=KAm[hp][:, 0:128], rhs=vv[:], start=st_, stop=(hp == 1)),
                           [bKA[hp], bvv], [bpW])
                    if not samp:
                        Ac(lambda e, psW=psW: e.activation(out=WpA[:, 0:64], in_=psW[:, 0:64], func=AF.Copy), [bpW], [bWp])
                        Ac(lambda e, psW=psW: e.activation(out=WpB[:, 64:128], in_=psW[:, 64:128], func=AF.Copy), [bpW], [bWp])
                    else:
                        Dv(lambda e, psW=psW: e.tensor_tensor(out=t128[0][:], in0=psW[:, 0:128], in1=t128[0][:], op=ALU.add), [bpW, bt128[0], bWp], [bt128[0]])
                        Ac(lambda e: e.activation(out=WpA[:, 0:64], in_=t128[0][:, 0:64], func=AF.Copy), [bt128[0]], [bWp])
                        Ac(lambda e: e.activation(out=WpB[:, 64:128], in_=t128[0][:, 64:128], func=AF.Copy), [bt128[0]], [bWp])
                    psU, bpU = ps_next()
                    for hp, ww in ((0, WpA), (1, WpB)):
                        Pe(lambda e, psU=psU, hp=hp, ww=ww: e.matmul(psU[:, 0:128], lhsT=Ttm[hp][:], rhs=ww[:], start=(hp == 0), stop=(hp == 1)), [bTt[hp], bWp], [bpU])
                    Ac(lambda e, psU=psU: e.activation(out=UpA[:, 0:64], in_=psU[:, 0:64], func=AF.Copy), [bpU], [bUp])
                    Ac(lambda e, psU=psU: e.activation(out=UpB[:, 64:128], in_=psU[:, 64:128], func=AF.Copy), [bpU], [bUp])
                    psY, bpY = ps_next()
                    first = True
                    if not samp:
                        Pe(lambda e, psY=psY, c=c: e.matmul(psY[:, 0:128], lhsT=Hbb[:, c, :], rhs=ARt[:, c, 128:256], start=True, stop=False), [bHb, bAR], [bpY])
                        first = False
                    for hp, uu, vv in ((0, UpA, vA), (1, UpB, vB)):
                        Pe(lambda e, psY=psY, hp=hp, uu=uu, first=first: e.matmul(psY[:, 0:128], lhsT=uu[:], rhs=NBm[hp][:, 128:256], start=first, stop=False), [bUp, bNB[hp]], [bpY])
                        first = False
                        Pe(lambda e, psY=psY, hp=hp, vv=vv, sp_=(hp == 1 and not samp): e.matmul(psY[:, 0:128], lhsT=vv[:], rhs=KAm[hp][:, 128:256], start=False, stop=sp_), [bvv, bKA[hp]], [bpY])
                    if samp:
                        for sn in range(NS):
                            Pe(lambda e, psY=psY, sn=sn, c=c: e.matmul(psY[:, sn * 8:sn * 8 + 8], lhsT=Hsb[:, sn, :], rhs=ARt[:, c, 128 + sn * 8:128 + sn * 8 + 8],
                                                                      start=False, stop=(sn == NS - 1)), [bHsb, bAR], [bpY])
                    Ac(lambda e, psY=psY, c=c: e.activation(out=yfm[:, c, :], in_=psY[:, 0:128], func=AF.Copy), [bpY], [byfm])
                    if not samp:
                        psD, bpD = ps_next()
                        Pe(lambda e, psD=psD: e.matmul(psD[:, 0:128], lhsT=btm_[:], rhs=UpA[:], start=True, stop=False), [bbtm_, bUp], [bpD])
                        Pe(lambda e, psD=psD: e.matmul(psD[:, 0:128], lhsT=btm_[:], rhs=UpB[:], start=False, stop=False), [bbtm_, bUp], [bpD])
                        Pe(lambda e, psD=psD: e.matmul(psD[:, 0:128], lhsT=ktm_[:], rhs=vA[:], start=False, stop=False), [bktm_, bvv], [bpD])
                        Pe(lambda e, psD=psD: e.matmul(psD[:, 0:128], lhsT=ktm_[:], rhs=vB[:], start=False, stop=True), [bktm_, bvv], [bpD])
                        Dv(lambda e, psD=psD: e.tensor_tensor(out=t128[1][:], in0=psD[:, 0:128], in1=blk64[:], op=ALU.mult), [bpD, bconst], [bt128[1]])
                        Dv(lambda e, c=c: e.tensor_tensor(out=t128[1][:], in0=t128[1][:], in1=Hblk[:, c, :], op=ALU.add), [bt128[1], bH], [bt128[1]])
                        Dv(lambda e, c=c: e.tensor_scalar(out=Hblk[:, c, :], in0=t128[1][:], scalar1=ec[:, c, 127:128], scalar2=None, op0=ALU.mult), [bt128[1], b_ec], [bH])
                        Ac(lambda e, c=c: e.activation(out=Hbb[:, c, :], in_=Hblk[:, c, :], func=AF.Copy), [bH], [bHb])
                        if t0 + 128 == cfg.tp:
                            P.dma('sp', lambda e, c=c: e.dma_start(out=rwhp_d[:, l, c], in_=Hblk[:, c, :]), reads=[bH])
                    else:
                        for sn in range(NS):
                            Dv(lambda e, sn=sn: e.tensor_tensor(out=UmA[:, sn, :], in0=UpA[:], in1=bm16[:, sn, :], op=ALU.mult), [bUp, bconst, bUm], [bUm])
                            Dv(lambda e, sn=sn: e.tensor_tensor(out=VmA[:, sn, :], in0=UpB[:], in1=bm16[:, sn, :], op=ALU.mult), [bUp, bconst, bVm], [bVm])
                        Dv(lambda e: e.tensor_tensor(out=UmA[:], in0=UmA[:], in1=VmA[:], op=ALU.add), [bUm, bVm], [bUm])
                        for sn in range(NS):
                            Dv(lambda e, sn=sn: e.tensor_tensor(out=VmA[:, sn, :], in0=vA[:], in1=bm16[:, sn, :], op=ALU.mult), [bvv, bconst, bVm], [bVm])
                        for sn in range(NS):
                            Dv(lambda e, sn=sn: e.tensor_tensor(out=Hsb[:, sn, :], in0=vB[:], in1=bm16[:, sn, :], op=ALU.mult), [bvv, bconst, bHsb], [bHsb])
                        Dv(lambda e: e.tensor_tensor(out=VmA[:], in0=VmA[:], in1=Hsb[:], op=ALU.add), [bVm, bHsb], [bVm])
                        for g4 in range(4):
                            psD, bpD = ps_next()
                            Pe(lambda e, psD=psD, g4=g4: e.matmul(psD[:, :], lhsT=btm_[:], rhs=UmA[:, 4 * g4:4 * g4 + 4, :].rearrange("p a b -> p (a b)"), start=True, stop=False),
                               [bbtm_, bUm], [bpD])
                            Pe(lambda e, psD=psD, g4=g4: e.matmul(psD[:, :], lhsT=ktm_[:], rhs=VmA[:, 4 * g4:4 * g4 + 4, :].rearrange("p a b -> p (a b)"), start=False, stop=True),
                               [bktm_, bVm], [bpD])
                            for j4 in range(4):
                                sn = 4 * g4 + j4
                                Dv(lambda e, psD=psD, j4=j4: e.tensor_tensor(out=t128[1][:], in0=psD[:, j4 * 128:(j4 + 1) * 128], in1=blk64[:], op=ALU.mult), [bpD, bconst], [bt128[1]])
                                Dv(lambda e, sn=sn: e.tensor_tensor(out=t128[1][:], in0=t128[1][:], in1=Hs[:, sn, :], op=ALU.add), [bt128[1], bHs], [bt128[1]])
                                Dv(lambda e, sn=sn, c=c: e.tensor_scalar(out=Hs[:, sn, :], in0=t128[1][:], scalar1=ec[:, c, sn * 8 + 7:sn * 8 + 8], scalar2=None, op0=ALU.mult),
                                   [bt128[1], b_ec], [bHs])
                        P.dma('sp', lambda e, c=c: e.dma_start(out=rwhs_d[:, l, c], in_=Hs[:]), reads=[bHs])
                for c in range(6):
                    psm, bpm = ps_next()
                    Pe(lambda e, psm=psm, c=c: e.matmul(psm[:, 0:128], lhsT=blk64[:], rhs=yfm[:, c, :], start=True, stop=True), [bconst, byfm], [bpm])
                    Dv(lambda e, psm=psm, c=c: e.scalar_tensor_tensor(out=t128[0][:], in0=psm[:, 0:128], scalar=-1.0 / 64, in1=yfm[:, c, :], op0=ALU.mult, op1=ALU.add),
                       [bpm, byfm], [bt128[0]])
                    Dv(lambda e: e.tensor_tensor(out=t128[1][:], in0=t128[0][:], in1=t128[0][:], op=ALU.mult), [bt128[0]], [bt128[1]])
                    psv, bpv = ps_next()
                    Pe(lambda e, psv=psv: e.matmul(psv[:, 0:128], lhsT=blk64[:], rhs=t128[1][:], start=True, stop=True), [bconst, bt128[1]], [bpv])
                    Dv(lambda e, psv=psv: e.tensor_scalar(out=t128[1][:], in0=psv[:, 0:128], scalar1=1.0 / 64, scalar2=64e-5, op0=ALU.mult, op1=ALU.add), [bpv], [bt128[1]])
                    Ac(lambda e: e.activation(out=t128[1][:], in_=t128[1][:], func=AF.Sqrt), [bt128[1]], [bt128[1]])
                    Dv(lambda e: e.reciprocal(out=t128[1][:], in_=t128[1][:]), [bt128[1]], [bt128[1]])
                    Dv(lambda e, c=c: e.scalar_tensor_tensor(out=t128[0][:], in0=t128[0][:], scalar=ln_w[:, c:c + 1], in1=t128[1][:], op0=ALU.mult, op1=ALU.mult),
                       [bt128[0], bt128[1], brwp], [bt128[0]])
                    Dv(lambda e, c=c: e.tensor_scalar(out=t128[0][:], in0=t128[0][:], scalar1=ln_b[:, c:c + 1], scalar2=None, op0=ALU.add), [bt128[0], brwp], [bt128[0]])
                    Dv(lambda e, c=c: e.scalar_tensor_tensor(out=t128[1][:], in0=zp[:, c, :], scalar=r_k[:, c:c + 1], in1=kp[:, c, :], op0=ALU.mult, op1=ALU.mult),
                       [bzp, b_kp, brwp, bt128[1]], [bt128[1]])
                    psb_, bpb = ps_next()
                    Pe(lambda e, psb_=psb_: e.matmul(psb_[:, 0:128], lhsT=blk64[:], rhs=t128[1][:], start=True, stop=True), [bconst, bt128[1]], [bpb])
                    Dv(lambda e, psb_=psb_, c=c: e.tensor_tensor(out=t128[1][:], in0=psb_[:, 0:128], in1=zp[:, 12 + c, :], op=ALU.mult), [bpb, bzp, bt128[1]], [bt128[1]])
                    Dv(lambda e: e.tensor_tensor(out=t128[0][:], in0=t128[0][:], in1=t128[1][:], op=ALU.add), [bt128[0], bt128[1]], [bt128[0]])
                    Dv(lambda e, c=c: e.tensor_tensor(out=mixo[:, 0:128], in0=t128[0][:], in1=gg[:, c, :], op=ALU.mult), [bt128[0], b_gg], [bmixo])
                    P.dma('sp', lambda e, c=c, t0=t0: e.dma_start(out=msc[512 + 128 * c:512 + 128 * c + 128, t0:t0 + 128], in_=mixo[:, 0:128]),
                          reads=[bmixo], writes=[bmsc_all[bi]])

        def zero_mix(l, c0, c1):
            P.op('dve', lambda e: e.memset(mixo[:], 0.0), writes=[bmixo])
            for pi, (t0, n) in enumerate(pieces):
                for c in range(c0, c1, 128):
                    P.dma('sp', lambda e, c=c, t0=t0, n=n: e.dma_start(out=msc[c:c + 128, t0:t0 + n], in_=mixo[:, :n]),
                          reads=[bmixo], writes=[bmsc_all[pi]])

        def mixers(l):
            P.barrier()
            if cfg.only is not None:
                zero_mix(l, 0, 2048)
            if cfg.only in (None, 's5'):
                s5(l)
            P.barrier()
            if cfg.only in (None, 'hgrn'):
                hgrn(l)
            P.barrier()
            if cfg.only in (None, 'rwkv'):
                rwkv(l)
            P.barrier()

        nblk = len(cfg.blocks)
        if cfg.mode == 'mix':
            mixers(cfg.mixl)
            nblk = 0
            wst['used'] = len(WSEQ)
        for bi in range(nblk):
            load_x0(bi)
            seg_front(0, bi)
        if cfg.mode != 'mix':
            mixers(0)
        for l in range(1, L if cfg.mode != 'mix' else 1):
            for bi in range(nblk):
                seg_back(l - 1, bi)
                seg_front(l, bi)
            mixers(l)
        for bi in range(nblk):
            seg_back(L - 1, bi)
            final(bi)
        assert wst['used'] == len(WSEQ)
        P.run(es)
    return nc


def host_consts():
    s = np.arange(128)
    m64 = ((s[:, None] // 64 == s[None, :] // 64) & (s[:, None] <= s[None, :])).astype(np.float32)
    m8 = ((s[:, None] // 8 == s[None, :] // 8) & (s[:, None] <= s[None, :])).astype(np.float32)
    bm16 = np.zeros((128, NS, 128), np.float32)
    for n in range(NS):
        bm16[n * 8:(n + 1) * 8, n, :] = 1
    t = np.arange(512)
    cm = np.ones((128, 3, 512), np.float32)
    cm[:, 0, t % 64 == 0] = 0
    cm[:, 1, t % 8 == 0] = 0
    cm[:, 2, t % 128 == 0] = 0
    iota = np.broadcast_to(np.arange(128, dtype=np.float32)[None, :], (128, 128)).copy()
    rwm = np.zeros((128, 2, 384), np.float32)
    for mi, blk in ((0, 128), (1, 8)):
        same = (s[:, None] // blk == s[None, :] // blk)
        rwm[:, mi, 0:128] = same & (s[:, None] < s[None, :])
        rwm[:, mi, 128:256] = same & (s[:, None] <= s[None, :])
        rwm[:, mi, 256:384] = same & (s[:, None] > s[None, :])
    blk64 = (s[:, None] // 64 == s[None, :] // 64).astype(np.float32)
    i6 = np.arange(768)
    cm6 = np.ones((128, 2, 768), np.float32)
    cm6[:, 0, i6 % 128 == 0] = 0
    cm6[:, 1, i6 % 8 == 0] = 0
    rmask2 = np.zeros((128, 2), np.float32)
    pp = np.arange(128)
    rmask2[:, 0] = ((pp % 64) // 32 == 0)
    rmask2[:, 1] = ((pp % 64) // 32 == 1)
    import ml_dtypes
    bf = ml_dtypes.bfloat16
    return dict(ident=np.eye(128, dtype=np.float32), m64=m64, m8=m8, bm16=bm16.astype(bf), cm=cm.astype(bf), iota=iota, rmask2=rmask2, rwm=rwm.astype(bf), blk64=blk64, cm6=cm6.astype(bf))


def prep_hgrn(lb_raw, norm_w):
    L = lb_raw.shape[0]
    return dict(lbraw=np.ascontiguousarray(lb_raw.reshape(L, 6, 128).transpose(2, 1, 0)),
                hnw=np.ascontiguousarray(norm_w.reshape(L, 6, 128).transpose(2, 0, 1)))


def prep_s5(a_re, a_im, log_dt, b_re, b_im, c_re, c_im, d, w_glu, b_glu):
    L = a_re.shape[0]
    f32 = np.float32
    ldt_full = np.repeat(log_dt, 64, axis=1)
    fm = np.stack([a_re.reshape(L, 2048), a_im.reshape(L, 2048), ldt_full], axis=1)
    s5fm = np.ascontiguousarray(fm.reshape(L, 3, 16, 128).transpose(3, 0, 1, 2)).astype(f32)
    p = np.arange(128)
    row = np.zeros((128, L, 3, 4, 128), f32)
    for jq in range(4):
        j = 4 * jq + p // 32
        idx = j[:, None] * 128 + np.arange(128)[None, :]
        for i in range(3):
            row[:, :, i, jq, :] = fm[:, i, :][:, idx].transpose(1, 0, 2)
    s5row = row.reshape(128, L, 12, 128)
    s5b = np.zeros((128, L, 2, 4, 128), f32)
    m = np.arange(128)
    for c, bb in enumerate((b_re, b_im)):
        for jq in range(4):
            for pp_ in range(128):
                j = 4 * jq + pp_ // 32
                gl = (pp_ % 32) // 16
                h = pp_ % 16
                s5b[pp_, :, c, jq, gl * 64:(gl + 1) * 64] = bb[:, 2 * j + gl, :, h]
    s5c = np.zeros((128, L, 2, 16, 128), f32)
    for c, cc in enumerate((c_re, c_im)):
        for j in range(16):
            for gl in range(2):
                m0 = 32 * (j % 4) + 16 * gl
                s5c[gl * 64:(gl + 1) * 64, :, c, j, m0:m0 + 16] = cc[:, 2 * j + gl, :, :].transpose(2, 0, 1)
    s5db = np.ascontiguousarray(np.stack([d, b_glu], axis=1).reshape(L, 2, 4, 128).transpose(3, 0, 1, 2)).astype(f32)
    s5w = np.ascontiguousarray(w_glu.reshape(L, 4, 128, 512).transpose(2, 0, 1, 3)).astype(f32)
    return dict(s5fm=s5fm, s5row=np.ascontiguousarray(s5row), s5b=s5b, s5c=s5c, s5db=s5db, s5w=s5w)


def fm_state_s5(re, im):
    L, N = re.shape[:2]
    x = np.stack([re.reshape(L, N, 16, 128), im.reshape(L, N, 16, 128)], axis=1)
    return np.ascontiguousarray(x.transpose(4, 0, 1, 3, 2)).astype(np.float32)


def unfm_state_s5(x):
    L, N = x.shape[1], x.shape[4]
    y = x.transpose(1, 2, 4, 3, 0).reshape(L, 2, N, 32, 64)
    return np.ascontiguousarray(y[:, 0]), np.ascontiguousarray(y[:, 1])


def prep_rwkv(mu, w0, w2, a0, a2, g2, k_k, k_a, r_k, ln_w, ln_b):
    L = mu.shape[0]
    f32 = np.float32
    cols = [mu.reshape(L, 20, 128)] + [x.reshape(L, 6, 128) for x in (w0, a0, k_k, k_a, r_k.reshape(L, 768), ln_w, ln_b)]
    rwp = np.ascontiguousarray(np.concatenate(cols, axis=1).transpose(2, 0, 1)).astype(f32)
    rwl = np.zeros((128, L, 2, 768), f32)
    rwl[0:64, :, 0, :] = w2.transpose(1, 0, 2)
    rwl[64:128, :, 0, :] = a2.transpose(1, 0, 2)
    rwl[:, :, 1, :] = g2.transpose(1, 0, 2)
    return dict(rwp=rwp, rwl=rwl)


def fm_shift(sh):
    L, N = sh.shape[:2]
    return np.ascontiguousarray(sh.reshape(L, N, 20, 128).transpose(3, 0, 2, 1)).astype(np.float32)


def unfm_shift(x):
    L, N = x.shape[1], x.shape[3]
    return np.ascontiguousarray(x.transpose(1, 3, 2, 0).reshape(L, N, 2560))


def fm_wkv(wkv):
    L, N = wkv.shape[:2]
    out = np.zeros((128, L, 6, N, 128), np.float32)
    w = wkv.reshape(L, N, 6, 2, 64, 64)
    for hp in range(2):
        out[hp * 64:(hp + 1) * 64, :, :, :, hp * 64:(hp + 1) * 64] = w[:, :, :, hp].transpose(4, 0, 2, 1, 3)
    return out


def unfm_wkv(x):
    L, N = x.shape[1], x.shape[3]
    out = np.zeros((L, N, 6, 2, 64, 64), np.float32)
    for hp in range(2):
        blk = x[hp * 64:(hp + 1) * 64, :, :, :, hp * 64:(hp + 1) * 64]
        out[:, :, :, hp] = blk.transpose(1, 3, 2, 4, 0)
    return out.reshape(L, N, 12, 64, 64)


def tile_w(w):
    L, K, N = w.shape
    return np.ascontiguousarray(w.reshape(L, K // 128, 128, N // 512, 512).transpose(0, 3, 2, 1, 4))


_WNAMES = (("wg1", "ffn1_w_gate"), ("wu1", "ffn1_w_up"), ("wd1", "ffn1_w_down"), ("win", "w_in"), ("wout", "w_out"),
           ("wg2", "ffn2_w_gate"), ("wu2", "ffn2_w_up"), ("wd2", "ffn2_w_down"))


def make_inmaps(inp, cfg, ncores, nprompt):
    f32 = np.float32
    L = cfg.depth
    shared = host_consts()
    for k, nm in _WNAMES:
        shared[k] = tile_w(np.asarray(inp[nm], f32)[:L])
    norms = np.concatenate([np.stack([inp["norm_ffn1"][l], inp["norm_mix"][l], inp["norm_ffn2"][l]]) for l in range(L)] + [inp["norm_final"][None]], axis=0)
    shared["nrm"] = np.ascontiguousarray(np.asarray(norms, f32).reshape(3 * L + 1, KT, 128).transpose(2, 0, 1))
    shared.update(prep_hgrn(np.asarray(inp["hgrn_lb_raw"], f32)[:L], np.asarray(inp["hgrn_norm_w"], f32)[:L]))
    shared.update(prep_s5(*(np.asarray(inp[k], f32)[:L] for k in ("s5_a_re", "s5_a_im", "s5_log_dt", "s5_b_re", "s5_b_im", "s5_c_re", "s5_c_im",
                                                                   "s5_d", "s5_w_glu", "s5_b_glu"))))
    shared.update(prep_rwkv(*(np.asarray(inp[k], f32)[:L] for k in ("rwkv_mu", "rwkv_w0", "rwkv_w2", "rwkv_a0", "rwkv_a2", "rwkv_g2", "rwkv_k_k",
                                                                     "rwkv_k_a", "rwkv_r_k", "rwkv_ln_w", "rwkv_ln_b"))))
    maps = []
    for c in range(ncores):
        m = dict(shared)
        sl = slice(NS * c, NS * (c + 1))
        xp = np.asarray(inp["x_prompt"][c % nprompt], f32)
        xs = np.asarray(inp["x_sample"][sl], f32).reshape(NS * TS, D)
        m["xT"] = np.ascontiguousarray(np.concatenate([xp, xs], axis=0).T)
        m["hst"] = np.ascontiguousarray(np.asarray(inp["state_hgrn"], f32)[:L, sl])
        m["s5x0"] = fm_state_s5(np.asarray(inp["state_s5_re"], f32)[:L, sl], np.asarray(inp["state_s5_im"], f32)[:L, sl])
        m["rwsh0"] = fm_shift(np.asarray(inp["state_rwkv_shift"], f32)[:L, sl])
        m["rwh0"] = fm_wkv(np.asarray(inp["state_rwkv_wkv"], f32)[:L, sl])
        maps.append(m)
    return maps


def gather(results, cfg, ncores, nprompt):
    L = cfg.depth
    TP = cfg.tp
    f32 = np.float32
    y_p = np.stack([results[c]["yT"][:, :TP].T for c in range(nprompt)]).astype(f32)
    y_s = np.concatenate([results[c]["yT"][:, TP:].T.reshape(NS, TS, D) for c in range(ncores)]).astype(f32)
    s5p = [unfm_state_s5(results[c]["s5p"][..., None]) for c in range(nprompt)]
    s5s = [unfm_state_s5(results[c]["s5s"]) for c in range(ncores)]
    s5re_p = np.concatenate([a[0] for a in s5p], axis=1)
    s5im_p = np.concatenate([a[1] for a in s5p], axis=1)
    s5re_s = np.concatenate([a[0] for a in s5s], axis=1)
    s5im_s = np.concatenate([a[1] for a in s5s], axis=1)
    sh_p = np.concatenate([unfm_shift(results[c]["rwshp"][..., None]) for c in range(nprompt)], axis=1)
    sh_s = np.concatenate([unfm_shift(results[c]["rwshs"]) for c in range(ncores)], axis=1)
    wkv_p = np.concatenate([unfm_wkv(results[c]["rwhp"][:, :, :, None, :]) for c in range(nprompt)], axis=1)
    wkv_s = np.concatenate([unfm_wkv(results[c]["rwhs"]) for c in range(ncores)], axis=1)
    hg_p = np.stack([results[c]["hgp"] for c in range(nprompt)], axis=1).astype(f32)
    hg_s = np.concatenate([results[c]["hgs"] for c in range(ncores)], axis=1).astype(f32)
    outs = (y_p, y_s, s5re_p, s5im_p, sh_p, wkv_p, hg_p, s5re_s, s5im_s, sh_s, wkv_s, hg_s)
    return tuple(np.ascontiguousarray(o, dtype=f32) for o in outs)


def kernel(**inp):
    cfg = Cfg(depth=4, tp=2048)
    nc = build(cfg)
    maps = make_inmaps(inp, cfg, 8, 4)
    res = run_bass_kernel_spmd(nc, maps, core_ids=list(range(8)))
    return gather(res.results, cfg, 8, 4)
```

```python
import os
import numpy as np
import concourse.bass as bass
import concourse.mybir as mybir
from concourse.bass_utils import run_bass_kernel_spmd
from contextlib import ExitStack

F32 = mybir.dt.float32
BF16 = mybir.dt.bfloat16
I32 = mybir.dt.int32
AF = mybir.ActivationFunctionType
ALU = mybir.AluOpType

D = 2048
DFF = 5632
NIN = 6144
KT = D // 128
FT = DFF // 128
NS = 16
TS = 8
NORM_EPS = 1e-6

EPOCH = 8192
COMPUTE = ('pe', 'act', 'dve', 'pool')
NDMASEM = 24
NSWSEM = 8


class Buf:
    __slots__ = ('name', 'w', 'r')

    def __init__(self, name=''):
        self.name = name
        self.w = None
        self.r = {}


class Prog:
    def __init__(self, nc):
        self.nc = nc
        self.q = {e: [] for e in ('pe', 'act', 'dve', 'pool', 'sp')}
        self.src_n = {}
        self.signalled = {}
        self.seen = {e: {} for e in self.q}
        self.dma_rr = 0
        self.dma_rr2 = 0

    def _new_event(self, src):
        i = self.src_n.get(src, 0)
        self.src_n[src] = i + 1
        return i

    def _deps(self, eng, src, reads, writes):
        deps = {}

        def add(d, raw):
            if d is None:
                return
            s, i = d
            if s == src and src == 'pe':
                return
            if deps.get(s, -1) < i:
                deps[s] = i
        for b in reads:
            add(b.w, True)
        for b in writes:
            add(b.w, False)
            for s, i in b.r.items():
                if (s != src or src != 'pe') and deps.get(s, -1) < i:
                    deps[s] = i
        out = []
        seen = self.seen[eng]
        for s, i in deps.items():
            if seen.get(s, -1) >= i:
                continue
            seen[s] = i
            self.signalled[(s, i)] = True
            out.append((s, i))
        return out

    def op(self, eng, fn, reads=(), writes=()):
        waits = self._deps(eng, eng, reads, writes)
        idx = self._new_event(eng)
        for b in reads:
            b.r[eng] = idx
        for b in writes:
            b.w = (eng, idx)
            b.r = {}
        self.q[eng].append(('op', waits, fn, (eng, idx)))

    def dma(self, eng, fn, reads=(), writes=()):
        if eng == 'pool':
            src = 'dma%d' % (NDMASEM + self.dma_rr2)
            self.dma_rr2 = (self.dma_rr2 + 1) % NSWSEM
        else:
            src = 'dma%d' % self.dma_rr
            self.dma_rr = (self.dma_rr + 1) % NDMASEM
        waits = self._deps(eng, src, reads, writes)
        idx = self._new_event(src)
        if idx > 0 and self.seen[eng].get(src, -1) < idx - 1:
            self.seen[eng][src] = idx - 1
            self.signalled[(src, idx - 1)] = True
            waits.append((src, idx - 1))
        for b in reads:
            b.r[src] = idx
        for b in writes:
            b.w = (src, idx)
            b.r = {}
        self.q[eng].append(('dma', waits, fn, (src, idx)))

    def barrier(self):
        for eng in self.q:
            waits = []
            seen = self.seen[eng]
            for src, n in self.src_n.items():
                if src == eng or n == 0:
                    continue
                i = n - 1
                if seen.get(src, -1) >= i:
                    continue
                seen[src] = i
                self.signalled[(src, i)] = True
                waits.append((src, i))
            self.q[eng].append(('bar', waits, None, None))

    def run(self, es):
        nc = self.nc
        sems = {}
        val = {}
        for e in COMPUTE:
            n = self.src_n.get(e, 0)
            c = 0
            ep = 0
            for i in range(n):
                if self.signalled.get((e, i)):
                    if c == EPOCH:
                        ep += 1
                        c = 0
                    c += 1
                    val[(e, i)] = (e, ep, c)
                    if (e, ep) not in sems:
                        sems[(e, ep)] = es.enter_context(nc.semaphore('s_%s_%d' % (e, ep)))
        for k in range(NDMASEM + NSWSEM):
            s = 'dma%d' % k
            if self.src_n.get(s, 0):
                sems[(s, 0)] = es.enter_context(nc.semaphore('s_' + s))
        final_dma = {('dma%d' % k): 16 * self.src_n.get('dma%d' % k, 0) for k in range(NDMASEM + NSWSEM)}

        def waitspec(s, i):
            if s.startswith('dma'):
                return sems[(s, 0)], 16 * (i + 1)
            _, ep, c = val[(s, i)]
            return sems[(s, ep)], c

        block = es.enter_context(nc.Block())
        prog = self

        def replay(engname):
            def body(eng):
                for kind, waits, fn, ev in prog.q[engname]:
                    for (s, i) in waits:
                        sem, v = waitspec(s, i)
                        eng.wait_ge(sem, v)
                    if fn is None:
                        continue
                    ins = fn(eng)
                    if kind == 'dma':
                        ins.then_inc(sems[(ev[0], 0)], 16)
                    elif ev in val:
                        _, ep, c = val[ev]
                        ins.then_inc(sems[(ev[0], ep)], 1)
                if engname == 'sp':
                    for s, v in final_dma.items():
                        if v:
                            eng.wait_ge(sems[(s, 0)], v)
            return body

        block.tensor(replay('pe'))
        block.scalar(replay('act'))
        block.vector(replay('dve'))
        block.gpsimd(replay('pool'))
        block.sync(replay('sp'))


class Cfg:
    def __init__(self, depth=4, tp=2048, tb=512, mixers=True, debug=False, mode='full'):
        self.mode = mode
        self.mixl = 1
        self.stop = 99
        self.only = None
        self.depth = depth
        self.debug = debug
        self.tp = tp
        self.tb = tb
        self.tok = tp + NS * TS
        self.mixers = mixers
        self.blocks = [(i * tb, tb) for i in range(tp // tb)] + [(tp, NS * TS)]


NB = 4


def build(cfg):
    nc = bass.Bass("TRN2", target_bir_lowering=False)
    L = cfg.depth
    TOK = cfg.tok
    TB = cfg.tb

    def din(name, shape, dt=F32):
        return nc.dram_tensor(name, list(shape), dt, kind="ExternalInput").ap()

    def dout(name, shape, dt=F32):
        return nc.dram_tensor(name, list(shape), dt, kind="ExternalOutput").ap()

    def dscr(name, shape, dt=F32):
        return nc.dram_tensor(name, list(shape), dt, kind="Internal").ap()

    xT = din("xT", [D, TOK])
    wts = {}
    for nm, ncb, nkt in (("wg1", 11, 16), ("wu1", 11, 16), ("wd1", 4, 44), ("win", 12, 16),
                         ("wout", 4, 16), ("wg2", 11, 16), ("wu2", 11, 16), ("wd2", 4, 44)):
        wts[nm] = din(nm, [L if cfg.mode != 'mix' else 1, ncb if cfg.mode != 'mix' else 1, 128, nkt, 512])
    nrm = din("nrm", [128, 3 * L + 1, KT])
    yT = dout("yT", [D, TOK])
    xsc = dscr("xsc", [D, TOK])
    psc = (dout if cfg.debug else dscr)("psc", [NIN, TOK])
    msc = dscr("msc", [D, TOK], BF16)
    if cfg.debug:
        dbg_h = dout("dbg_h", [D, TOK], BF16)
        dbg_x = dout("dbg_x", [D, TOK])
    if cfg.mode == 'mix':
        psc = din("psc_in", [NIN, TOK])
        msc = dout("msc_out", [D, TOK], BF16)
    ident_d = din("ident", [128, 128])
    m64_d = din("m64", [128, 128])
    m8_d = din("m8", [128, 128])
    bm16_d = din("bm16", [128, NS, 128], BF16)
    cm_d = din("cm", [128, 3, 512], BF16)
    lbraw_d = din("lbraw", [128, 6, L])
    hnw_d = din("hnw", [128, L, 6])
    hst_d = din("hst", [L, NS, 6, 128, 128])
    s5fm_d = din("s5fm", [128, L, 3, 16])
    s5row_d = din("s5row", [128, L, 12, 128])
    s5b_d = din("s5b", [128, L, 2, 4, 128])
    s5c_d = din("s5c", [128, L, 2, 16, 128])
    s5db_d = din("s5db", [128, L, 2, 4])
    s5w_d = din("s5w", [128, L, 4, 512])
    s5x0_d = din("s5x0", [128, L, 2, 16, NS])
    s5p_d = dout("s5p", [128, L, 2, 16])
    s5s_d = dout("s5s", [128, L, 2, 16, NS])
    iota_d = din("iota", [128, 128])
    rwp_d = din("rwp", [128, L, 62])
    rwl_d = din("rwl", [128, L, 2, 768])
    rwsh0_d = din("rwsh0", [128, L, 20, NS])
    rwshp_d = dout("rwshp", [128, L, 20])
    rwshs_d = dout("rwshs", [128, L, 20, NS])
    rwh0_d = din("rwh0", [128, L, 6, NS, 128])
    rwhp_d = dout("rwhp", [128, L, 6, 128])
    rwhs_d = dout("rwhs", [128, L, 6, NS, 128])
    rwm_d = din("rwm", [128, 2, 384], BF16)
    blk64_d = din("blk64", [128, 128])
    cm6_d = din("cm6", [128, 2, 768], BF16)
    rmask2_d = din("rmask2", [128, 2])
    hgp_d = dout("hgp", [L, 6, 128, 128])
    hgs_d = dout("hgs", [L, NS, 6, 128, 128])
    bxsc = [Buf() for _ in cfg.blocks]
    bpsc = [Buf() for _ in cfg.blocks]
    bmsc = [Buf() for _ in cfg.blocks]

    es = ExitStack()
    with es:
        P = Prog(nc)

        def sb(name, shape, dt=F32):
            return es.enter_context(nc.sbuf_tensor(name, list(shape), dt))

        AW = KT * TB + KT * TB // 2 + FT * TB // 2 + 7 * TB
        arena = sb("arena", [128, AW])
        ast = {'p': 0}

        def a_reset():
            ast['p'] = 0

        def a_alloc(shape, dt=F32):
            n = 1
            for d_ in shape[1:]:
                n *= d_
            words = n if dt == F32 else (n + 1) // 2
            p0 = ast['p']
            ast['p'] = p0 + words
            assert ast['p'] <= AW, ("arena overflow", ast['p'], AW)
            v = arena[:, p0:p0 + words]
            if dt != F32:
                v = v.bitcast(dt)
                if dt == I32:
                    pass
            if len(shape) == 3:
                v = v.rearrange("p (a b) -> p a b", a=shape[1])
            return v

        xs = a_alloc([128, KT, TB])
        bx = [Buf('x%d' % i) for i in range(KT)]
        hb = a_alloc([128, KT, TB], BF16)
        bh = Buf('h')
        act = a_alloc([128, FT, TB], BF16)
        bact = [Buf('act%d' % i) for i in range(FT)]
        sq = [a_alloc([128, TB]) for i in range(2)]
        bsq = [Buf() for _ in range(2)]
        rstd = a_alloc([128, TB])
        brstd = Buf('rstd')
        sg = [a_alloc([128, TB]) for i in range(2)]
        bsg = [Buf() for _ in range(2)]
        stg = [a_alloc([128, TB]) for i in range(2)]
        bstg = [Buf() for _ in range(2)]
        ones = sb("ones", [128, 128])
        bones = Buf('ones')
        nrm_t = sb("nrm_t", [128, 3 * L + 1, KT])
        bnrm = Buf('nrm')
        wring = [sb("wr%d" % i, [128, 8, 512], BF16) for i in range(NB)]
        bwr = [Buf('wr%d' % i) for i in range(NB)]
        psall = es.enter_context(nc.psum_tensor("psall", [128, 8, 512], F32))
        psb = [psall[:, i, :] for i in range(8)]
        bps = [Buf('ps%d' % i) for i in range(8)]
        st = {'ps': 0, 'sq': 0, 'sg': 0, 'stg': 0}

        def ps_next():
            i = st['ps']
            st['ps'] = (i + 1) % 8
            return psb[i], bps[i]

        def rr(key, n):
            i = st[key]
            st[key] = (i + 1) % n
            return i


        class Rec:
            def __init__(self, banks):
                self.ops = []
                self.banks = banks
                self.i = 0

            def op(self, eng, fn, reads=(), writes=()):
                self.ops.append(('op', eng, fn, list(reads), list(writes)))

            def dma(self, eng, fn, reads=(), writes=()):
                self.ops.append(('dma', eng, fn, list(reads), list(writes)))

            def ps_next(self):
                b = self.banks[self.i % len(self.banks)]
                self.i += 1
                return psb[b], bps[b]

        def merge(recs, target=None):
            tgt = target if target is not None else P
            idx = [0] * len(recs)
            live = True
            while live:
                live = False
                for i, r in enumerate(recs):
                    if idx[i] < len(r.ops):
                        kind, eng, fn, rd, wr = r.ops[idx[i]]
                        idx[i] += 1
                        (tgt.op if kind == 'op' else tgt.dma)(eng, fn, reads=rd, writes=wr)
                        live = True

        def wseq():
            seq = []

            def ffn(l, g, u, d):
                for cb in range(11):
                    for nm in (g, u):
                        for kb in range(2):
                            seq.append((nm, l, cb, kb * 8, 8))
                for cb in range(4):
                    for rb in range(6):
                        seq.append((d, l, cb, rb * 8, 8 if rb < 5 else 4))

            def front(l):
                ffn(l, "wg1", "wu1", "wd1")
                for cb in range(12):
                    for kb in range(2):
                        seq.append(("win", l, cb, kb * 8, 8))

            def back(l):
                for cb in range(4):
                    for kb in range(2):
                        seq.append(("wout", l, cb, kb * 8, 8))
                ffn(l, "wg2", "wu2", "wd2")

            for _ in cfg.blocks:
                front(0)
            for l in range(1, L):
                for _ in cfg.blocks:
                    back(l - 1)
                    front(l)
            for _ in cfg.blocks:
                back(L - 1)
            return seq

        WSEQ = wseq()
        wst = {'issued': 0, 'used': 0}

        def w_issue():
            j = wst['issued']
            nm, l, cb, k0, n = WSEQ[j]
            slot = j % NB
            src = wts[nm][l, cb, :, k0:k0 + n, :]
            P.dma('pool', lambda e, slot=slot, src=src, n=n: e.dma_start(out=wring[slot][:, 0:n, :], in_=src),
                  writes=[bwr[slot]])
            wst['issued'] = j + 1

        def w_get(expect):
            j = wst['used']
            assert WSEQ[j] == expect, (WSEQ[j], expect)
            while wst['issued'] < min(len(WSEQ), j + NB - 1):
                w_issue()
            wst['used'] = j + 1
            slot = j % NB
            return wring[slot], bwr[slot]

        P.op('dve', lambda e: e.memset(ones[:], 1.0), writes=[bones])
        P.dma('sp', lambda e: e.dma_start(out=nrm_t[:], in_=nrm), writes=[bnrm])

        def rmsnorm(n, widx, out_fn):
            ps, bp = ps_next()
            for kt in range(KT):
                i = rr('sq', 2)
                P.op('act', lambda e, kt=kt, i=i: e.activation(out=sq[i][:, :n], in_=xs[:, kt, :n], func=AF.Square),
                     reads=[bx[kt]], writes=[bsq[i]])
                P.op('pe', lambda e, kt=kt, i=i, ps=ps: e.matmul(ps[:, :n], lhsT=ones[:], rhs=sq[i][:, :n],
                                                                   start=(kt == 0), stop=(kt == KT - 1)),
                     reads=[bones, bsq[i]], writes=[bp])
            P.op('dve', lambda e, ps=ps: e.tensor_scalar(out=rstd[:, :n], in0=ps[:, :n], scalar1=1.0 / D, scalar2=NORM_EPS,
                                                          op0=ALU.mult, op1=ALU.add), reads=[bp], writes=[brstd])
            P.op('act', lambda e: e.activation(out=rstd[:, :n], in_=rstd[:, :n], func=AF.Sqrt), reads=[brstd], writes=[brstd])
            P.op('dve', lambda e: e.reciprocal(out=rstd[:, :n], in_=rstd[:, :n]), reads=[brstd], writes=[brstd])
            for kt in range(KT):
                o, wb = out_fn(kt)
                P.op('dve', lambda e, kt=kt, o=o: e.scalar_tensor_tensor(out=o, in0=xs[:, kt, :n],
                                                                         scalar=nrm_t[:, widx, kt:kt + 1],
                                                                         in1=rstd[:, :n], op0=ALU.mult, op1=ALU.mult),
                     reads=[bx[kt], bnrm, brstd], writes=wb)

        def norm_to_h(n, widx):
            rmsnorm(n, widx, lambda kt: (hb[:, kt, :n], [bh]))

        def proj16(n, l, nm, cb, rhs_t, rhs_b):
            w0, b0 = w_get((nm, l, cb, 0, 8))
            w1, b1 = w_get((nm, l, cb, 8, 8))
            outs = []
            for ft in range(4):
                ps, bp = ps_next()
                for kt in range(KT):
                    w, bw_ = (w0, b0) if kt < 8 else (w1, b1)
                    P.op('pe', lambda e, ps=ps, w=w, kt=kt, ft=ft: e.matmul(
                        ps[:, :n], lhsT=w[:, kt % 8, ft * 128:(ft + 1) * 128], rhs=rhs_t[:, kt, :n],
                        start=(kt == 0), stop=(kt == KT - 1)), reads=[bw_, rhs_b], writes=[bp])
                outs.append((ps, bp))
            return outs

        def ffn(n, l, g, u, d):
            for cb in range(11):
                go = proj16(n, l, g, cb, hb, bh)
                uo = proj16(n, l, u, cb, hb, bh)
                for ft in range(4):
                    f = cb * 4 + ft
                    i = rr('sg', 2)
                    gps, gb = go[ft]
                    ups, ub = uo[ft]
                    P.op('act', lambda e, i=i, gps=gps: e.activation(out=sg[i][:, :n], in_=gps[:, :n], func=AF.Silu),
                         reads=[gb], writes=[bsg[i]])
                    P.op('dve', lambda e, i=i, ups=ups, f=f: e.tensor_tensor(out=act[:, f, :n], in0=sg[i][:, :n],
                                                                            in1=ups[:, :n], op=ALU.mult),
                         reads=[bsg[i], ub], writes=[bact[f]])
            for cb in range(4):
                pss = [ps_next() for _ in range(4)]
                for rb in range(6):
                    nk = 8 if rb < 5 else 4
                    w, bw_ = w_get((d, l, cb, rb * 8, nk))
                    for dt in range(4):
                        ps, bp = pss[dt]
                        for k in range(nk):
                            f = rb * 8 + k
                            P.op('pe', lambda e, ps=ps, w=w, k=k, dt=dt, f=f: e.matmul(
                                ps[:, :n], lhsT=w[:, k, dt * 128:(dt + 1) * 128], rhs=act[:, f, :n],
                                start=(f == 0), stop=(f == FT - 1)), reads=[bw_, bact[f]], writes=[bp])
                for dt in range(4):
                    ps, bp = pss[dt]
                    kt = cb * 4 + dt
                    P.op('dve', lambda e, ps=ps, kt=kt: e.scalar_tensor_tensor(
                        out=xs[:, kt, :n], in0=ps[:, :n], scalar=0.5, in1=xs[:, kt, :n], op0=ALU.mult, op1=ALU.add),
                        reads=[bp, bx[kt]], writes=[bx[kt]])

        def seg_front(l, bi):
            t0, n = cfg.blocks[bi]
            norm_to_h(n, 3 * l + 0)
            if cfg.debug and l == 0:
                P.dma('sp', lambda e: e.dma_start(out=dbg_h[:, t0:t0 + n].rearrange("(kt p) t -> p kt t", p=128), in_=hb[:, :, :n]),
                      reads=[bh])
            ffn(n, l, "wg1", "wu1", "wd1")
            if cfg.debug and l == 0:
                P.dma('sp', lambda e: e.dma_start(out=dbg_x[:, t0:t0 + n].rearrange("(kt p) t -> p kt t", p=128), in_=xs[:, :, :n]),
                      reads=bx)
            P.dma('sp', lambda e: e.dma_start(out=xsc[:, t0:t0 + n].rearrange("(kt p) t -> p kt t", p=128), in_=xs[:, :, :n]),
                  reads=bx, writes=[bxsc[bi]])
            norm_to_h(n, 3 * l + 1)
            for cb in range(12):
                po = proj16(n, l, "win", cb, hb, bh)
                for ft in range(4):
                    ps, bp = po[ft]
                    i = rr('stg', 2)
                    c0 = cb * 512 + ft * 128
                    P.op('act', lambda e, i=i, ps=ps: e.activation(out=stg[i][:, :n], in_=ps[:, :n], func=AF.Copy),
                         reads=[bp], writes=[bstg[i]])
                    P.dma('sp', lambda e, i=i, c0=c0: e.dma_start(out=psc[c0:c0 + 128, t0:t0 + n], in_=stg[i][:, :n]),
                          reads=[bstg[i]], writes=[bpsc[bi]])

        def seg_back(l, bi):
            t0, n = cfg.blocks[bi]
            P.dma('sp', lambda e: e.dma_start(out=xs[:, :, :n], in_=xsc[:, t0:t0 + n].rearrange("(kt p) t -> p kt t", p=128)),
                  reads=[bxsc[bi]], writes=bx)
            P.dma('sp', lambda e: e.dma_start(out=hb[:, :, :n], in_=msc[:, t0:t0 + n].rearrange("(kt p) t -> p kt t", p=128)),
                  reads=[bmsc[bi]], writes=[bh])
            for cb in range(4):
                po = proj16(n, l, "wout", cb, hb, bh)
                for ft in range(4):
                    ps, bp = po[ft]
                    kt = cb * 4 + ft
                    P.op('dve', lambda e, ps=ps, kt=kt: e.tensor_tensor(out=xs[:, kt, :n], in0=ps[:, :n], in1=xs[:, kt, :n],
                                                                       op=ALU.add), reads=[bp, bx[kt]], writes=[bx[kt]])
            norm_to_h(n, 3 * l + 2)
            ffn(n, l, "wg2", "wu2", "wd2")

        def load_x0(bi):
            t0, n = cfg.blocks[bi]
            P.dma('sp', lambda e: e.dma_start(out=xs[:, :, :n], in_=xT[:, t0:t0 + n].rearrange("(kt p) t -> p kt t", p=128)),
                  writes=bx)

        def final(bi):
            t0, n = cfg.blocks[bi]

            def of(kt):
                return act[:, 0:2, :].bitcast(F32)[:, 0, :n] if False else None
            ps, bp = ps_next()
            for kt in range(KT):
                i = rr('sq', 2)
                P.op('act', lambda e, kt=kt, i=i: e.activation(out=sq[i][:, :n], in_=xs[:, kt, :n], func=AF.Square),
                     reads=[bx[kt]], writes=[bsq[i]])
                P.op('pe', lambda e, kt=kt, i=i, ps=ps: e.matmul(ps[:, :n], lhsT=ones[:], rhs=sq[i][:, :n],
                                                                   start=(kt == 0), stop=(kt == KT - 1)),
                     reads=[bones, bsq[i]], writes=[bp])
            P.op('dve', lambda e, ps=ps: e.tensor_scalar(out=rstd[:, :n], in0=ps[:, :n], scalar1=1.0 / D, scalar2=NORM_EPS,
                                                          op0=ALU.mult, op1=ALU.add), reads=[bp], writes=[brstd])
            P.op('act', lambda e: e.activation(out=rstd[:, :n], in_=rstd[:, :n], func=AF.Sqrt), reads=[brstd], writes=[brstd])
            P.op('dve', lambda e: e.reciprocal(out=rstd[:, :n], in_=rstd[:, :n]), reads=[brstd], writes=[brstd])
            for kt in range(KT):
                i = rr('stg', 2)
                P.op('dve', lambda e, kt=kt, i=i: e.scalar_tensor_tensor(out=stg[i][:, :n], in0=xs[:, kt, :n],
                                                                         scalar=nrm_t[:, 3 * L, kt:kt + 1],
                                                                         in1=rstd[:, :n], op0=ALU.mult, op1=ALU.mult),
                     reads=[bx[kt], bnrm, brstd], writes=[bstg[i]])
                P.dma('sp', lambda e, kt=kt, i=i: e.dma_start(out=yT[kt * 128:(kt + 1) * 128, t0:t0 + n], in_=stg[i][:, :n]),
                      reads=[bstg[i]])


        a_reset()

        def hg_set():
            d = {}
            d['MT'] = [a_alloc([128, 512]) for i in range(10)]
            d['bMT'] = [Buf('mt%d' % i) for i in range(10)]
            for nm, shp, dt_ in (('qtb', [128, 512], BF16), ('ktb', [128, 512], BF16), ('vtm', [128, 128], BF16), ('ktm', [128, 128], BF16),
                                 ('attm', [128, 128], BF16), ('Sst', [128, 128], F32), ('Stmp', [128, 128], F32), ('Sbf', [128, 128], BF16),
                                 ('S0', [128, NS, 128], F32), ('S0b', [128, NS, 128], BF16), ('vmask', [128, NS, 128], BF16), ('mixh', [128, 512], BF16)):
                d[nm] = a_alloc(shp, dt_)
                d['b' + nm] = Buf(nm)
            return d
        HGS = [hg_set(), hg_set()]
        mixo = sb("mixo", [128, 512], BF16); bmixo = Buf()
        ident = sb("identt", [128, 128]); m64 = sb("m64t", [128, 128]); m8 = sb("m8t", [128, 128])
        bm16 = sb("bm16t", [128, NS, 128], BF16); cm = sb("cmt", [128, 3, 512], BF16); iota = sb("iotat", [128, 128])
        bconst = Buf('const')
        lbraw = sb("lbraw_t", [128, 6, L]); lbt = sb("lbt", [128, 6, L]); omlt = sb("omlt", [128, 6, L])
        lsum = sb("lsum", [128, 6]); hnw = sb("hnw_t", [128, L, 6])
        blb = Buf('lb')
        for t_, d_ in ((ident, ident_d), (m64, m64_d), (m8, m8_d), (bm16, bm16_d), (cm, cm_d), (hnw, hnw_d), (iota, iota_d)):
            P.dma('sp', lambda e, t_=t_, d_=d_: e.dma_start(out=t_[:], in_=d_), writes=[bconst])
        P.dma('sp', lambda e: e.dma_start(out=lbraw[:], in_=lbraw_d), writes=[blb])
        P.op('act', lambda e: e.activation(out=lbraw[:], in_=lbraw[:], func=AF.Exp), reads=[blb], writes=[blb])
        P.op('dve', lambda e: e.tensor_reduce(out=lsum[:], in_=lbraw[:], axis=mybir.AxisListType.X, op=ALU.add), reads=[blb], writes=[blb])
        P.op('dve', lambda e: e.reciprocal(out=lsum[:], in_=lsum[:]), reads=[blb], writes=[blb])
        P.op('dve', lambda e: e.memset(lbt[:], 0.0), reads=[blb], writes=[blb])
        for l_ in range(1, L):
            P.op('dve', lambda e, l_=l_: e.tensor_tensor(out=lbt[:, :, l_], in0=lbraw[:, :, l_], in1=lsum[:], op=ALU.mult), reads=[blb], writes=[blb])
            if l_ > 1:
                P.op('dve', lambda e, l_=l_: e.tensor_tensor(out=lbt[:, :, l_], in0=lbt[:, :, l_], in1=lbt[:, :, l_ - 1], op=ALU.add), reads=[blb], writes=[blb])
        P.op('dve', lambda e: e.tensor_scalar(out=omlt[:], in0=lbt[:], scalar1=-1.0, scalar2=1.0, op0=ALU.mult, op1=ALU.add), reads=[blb], writes=[blb])

        pieces = list(cfg.blocks)
        NPB = len(pieces) - 1
        bpsc_all = bpsc
        bmsc_all = bmsc

        def hgrn(l):
            for hA in range(0, 6, 2):
                ra, rb_ = Rec([0, 1, 2, 3]), Rec([4, 5, 6, 7])
                hgrn_head(l, hA, HGS[0], ra)
                hgrn_head(l, hA + 1, HGS[1], rb_)
                merge([ra, rb_])

        def hgrn_head(l, h, S_, R_):
            O_Q, O_F, O_I, O_G = 3072, 3840, 4608, 5376
            tF, tQ, tG, tI, tK, tB, tE, tEn, oacc, rst = S_['MT']
            bF, bQ, bG, bI, bK, bB, bE, bEn, boacc, brst = S_['bMT']
            qtb, ktb, vtm, ktm, attm, Sst, Stmp, Sbf, S0, S0b, vmask, mixo = (S_[k] for k in ('qtb', 'ktb', 'vtm', 'ktm', 'attm', 'Sst', 'Stmp', 'Sbf', 'S0', 'S0b', 'vmask', 'mixh'))
            bqtb, bktb, bvtm, bktm, battm, bSst, bStmp, bSbf, bS0, bS0b, bvmask, bmixo = (S_['b' + k] for k in ('qtb', 'ktb', 'vtm', 'ktm', 'attm', 'Sst', 'Stmp', 'Sbf', 'S0', 'S0b', 'vmask', 'mixh'))
            if True:
                R_.op('dve', lambda e: e.memset(Sst[:], 0.0), writes=[bSst])
                R_.op('dve', lambda e: e.memset(Sbf[:], 0.0), writes=[bSbf])
                for pi, (t0, n) in enumerate(pieces):
                    samp = (pi == NPB)
                    for tt_, bb_, off in ((tF, bF, O_F), (tQ, bQ, O_Q), (tI, bI, O_I), (tG, bG, O_G)):
                        R_.dma('sp', lambda e, tt_=tt_, off=off, t0=t0, n=n, h=h: e.dma_start(
                            out=tt_[:, :n], in_=psc[off + 128 * h: off + 128 * h + 128, t0:t0 + n]),
                            reads=[bpsc_all[pi]], writes=[bb_])
                    R_.op('act', lambda e, n=n: e.activation(out=tF[:, :n], in_=tF[:, :n], func=AF.Sigmoid), reads=[bF], writes=[bF])
                    R_.op('dve', lambda e, n=n, h=h: e.tensor_scalar(out=tF[:, :n], in0=tF[:, :n], scalar1=omlt[:, h, l:l + 1],
                                                                      scalar2=lbt[:, h, l:l + 1], op0=ALU.mult, op1=ALU.add),
                         reads=[bF, blb], writes=[bF])
                    R_.op('dve', lambda e, n=n: e.tensor_scalar(out=tK[:, :n], in0=tF[:, :n], scalar1=-1.0, scalar2=1.0,
                                                                op0=ALU.mult, op1=ALU.add), reads=[bF], writes=[bK])
                    R_.op('act', lambda e, n=n: e.activation(out=tF[:, :n], in_=tF[:, :n], func=AF.Ln), reads=[bF], writes=[bF])
                    ci = 1 if samp else 0
                    R_.op('dve', lambda e, n=n, ci=ci: e.tensor_tensor_scan(out=tB[:, :n], data0=cm[:, ci, :n], data1=tF[:, :n],
                                                                           initial=0.0, op0=ALU.mult, op1=ALU.add),
                         reads=[bF, bconst], writes=[bB])
                    R_.op('act', lambda e, n=n: e.activation(out=tE[:, :n], in_=tB[:, :n], func=AF.Exp), reads=[bB], writes=[bE])
                    R_.op('act', lambda e, n=n: e.activation(out=tEn[:, :n], in_=tB[:, :n], func=AF.Exp, scale=-1.0), reads=[bB], writes=[bEn])
                    R_.op('act', lambda e, n=n: e.activation(out=tQ[:, :n], in_=tQ[:, :n], func=AF.Silu), reads=[bQ], writes=[bQ])
                    R_.op('dve', lambda e, n=n: e.tensor_tensor(out=qtb[:, :n], in0=tQ[:, :n], in1=tE[:, :n], op=ALU.mult),
                         reads=[bQ, bE], writes=[bqtb])
                    R_.op('dve', lambda e, n=n: e.tensor_tensor(out=tK[:, :n], in0=tK[:, :n], in1=tEn[:, :n], op=ALU.mult),
                         reads=[bK, bEn], writes=[bK])
                    R_.op('act', lambda e, n=n: e.activation(out=ktb[:, :n], in_=tK[:, :n], func=AF.Copy), reads=[bK], writes=[bktb])
                    if samp:
                        R_.dma('sp', lambda e, h=h: e.dma_start(out=S0[:], in_=hst_d[l, :, h, :, :].rearrange("n k v -> k n v")),
                              writes=[bS0])
                        R_.op('act', lambda e: e.activation(out=S0b[:], in_=S0[:], func=AF.Copy), reads=[bS0], writes=[bS0b])
                    for tt in range(n // 128):
                        c0 = tt * 128
                        ps1, bp1 = R_.ps_next()
                        R_.op('pe', lambda e, ps1=ps1, c0=c0: e.transpose(ps1[:, 0:128], tI[:, c0:c0 + 128], ident[:]),
                             reads=[bI, bconst], writes=[bp1])
                        R_.op('act', lambda e, ps1=ps1: e.activation(out=vtm[:], in_=ps1[:, 0:128], func=AF.Copy), reads=[bp1], writes=[bvtm])
                        ps2, bp2 = R_.ps_next()
                        R_.op('pe', lambda e, ps2=ps2, c0=c0: e.transpose(ps2[:, 0:128], tK[:, c0:c0 + 128], ident[:]),
                             reads=[bK, bconst], writes=[bp2])
                        R_.op('act', lambda e, ps2=ps2: e.activation(out=ktm[:], in_=ps2[:, 0:128], func=AF.Copy), reads=[bp2], writes=[bktm])
                        ps3, bp3 = R_.ps_next()
                        R_.op('pe', lambda e, ps3=ps3, c0=c0: e.matmul(ps3[:, 0:128], lhsT=ktb[:, c0:c0 + 128], rhs=qtb[:, c0:c0 + 128],
                                                                      start=True, stop=True), reads=[bktb, bqtb], writes=[bp3])
                        msk = m8 if samp else m64
                        R_.op('dve', lambda e, ps3=ps3, msk=msk: e.tensor_tensor(out=attm[:], in0=ps3[:, 0:128], in1=msk[:], op=ALU.mult),
                             reads=[bp3, bconst], writes=[battm])
                        pso, bpo = R_.ps_next()
                        R_.op('pe', lambda e, pso=pso: e.matmul(pso[:, 0:128], lhsT=vtm[:], rhs=attm[:], start=True, stop=False),
                             reads=[bvtm, battm], writes=[bpo])
                        if not samp:
                            for cc in range(2):
                                a0 = cc * 64
                                R_.op('pe', lambda e, pso=pso, a0=a0, c0=c0, cc=cc: e.matmul(
                                    pso[:, a0:a0 + 64], lhsT=Sbf[:], rhs=qtb[:, c0 + a0:c0 + a0 + 64], start=False, stop=(cc == 1)),
                                    reads=[bSbf, bqtb], writes=[bpo])
                                psd, bpd = R_.ps_next()
                                R_.op('pe', lambda e, psd=psd, a0=a0: e.matmul(psd[:, 0:128], lhsT=ktm[a0:a0 + 64, :], rhs=vtm[a0:a0 + 64, :],
                                                                                start=True, stop=True), reads=[bktm, bvtm], writes=[bpd])
                                ecol = c0 + a0 + 63
                                R_.op('dve', lambda e, ecol=ecol: e.tensor_scalar(out=Stmp[:], in0=Sst[:], scalar1=tE[:, ecol:ecol + 1],
                                                                                 scalar2=None, op0=ALU.mult), reads=[bSst, bE], writes=[bStmp])
                                R_.op('dve', lambda e, psd=psd, ecol=ecol: e.scalar_tensor_tensor(
                                    out=Sst[:], in0=psd[:, 0:128], scalar=tE[:, ecol:ecol + 1], in1=Stmp[:], op0=ALU.mult, op1=ALU.add),
                                    reads=[bpd, bE, bStmp], writes=[bSst])
                                R_.op('act', lambda e: e.activation(out=Sbf[:], in_=Sst[:], func=AF.Copy), reads=[bSst], writes=[bSbf])
                        else:
                            for sn in range(NS):
                                R_.op('pe', lambda e, pso=pso, sn=sn: e.matmul(
                                    pso[:, sn * 8:sn * 8 + 8], lhsT=S0b[:, sn, :], rhs=qtb[:, sn * 8:sn * 8 + 8], start=False, stop=(sn == NS - 1)),
                                    reads=[bS0b, bqtb], writes=[bpo])
                            R_.op('act', lambda e, pso=pso, c0=c0: e.activation(out=oacc[:, c0:c0 + 128], in_=pso[:, 0:128], func=AF.Copy),
                                 reads=[bpo], writes=[boacc])
                            for sn in range(NS):
                                R_.op('dve', lambda e, sn=sn: e.tensor_tensor(out=vmask[:, sn, :], in0=vtm[:], in1=bm16[:, sn, :], op=ALU.mult),
                                     reads=[bvtm, bconst], writes=[bvmask])
                            for g4 in range(4):
                                psd, bpd = R_.ps_next()
                                R_.op('pe', lambda e, psd=psd, g4=g4: e.matmul(
                                    psd[:, :], lhsT=ktm[:], rhs=vmask[:, g4 * 4:g4 * 4 + 4, :].rearrange("p a b -> p (a b)"), start=True, stop=True),
                                    reads=[bktm, bvmask], writes=[bpd])
                                for j in range(4):
                                    sn = g4 * 4 + j
                                    ecol = sn * 8 + 7
                                    R_.op('dve', lambda e, psd=psd, j=j, sn=sn: e.tensor_tensor(
                                        out=S0[:, sn, :], in0=psd[:, j * 128:(j + 1) * 128], in1=S0[:, sn, :], op=ALU.add),
                                        reads=[bpd, bS0, bS0b], writes=[bS0])
                                    R_.op('dve', lambda e, sn=sn, ecol=ecol: e.tensor_scalar(
                                        out=S0[:, sn, :], in0=S0[:, sn, :], scalar1=tE[:, ecol:ecol + 1], scalar2=None, op0=ALU.mult),
                                        reads=[bS0, bE], writes=[bS0])
                            R_.dma('sp', lambda e, h=h: e.dma_start(out=hgs_d[l, :, h, :, :].rearrange("n k v -> k n v"), in_=S0[:]),
                                  reads=[bS0])
                        if not samp:
                            R_.op('act', lambda e, pso=pso, c0=c0: e.activation(out=oacc[:, c0:c0 + 128], in_=pso[:, 0:128], func=AF.Copy),
                                 reads=[bpo], writes=[boacc])
                    if pi == NPB - 1:
                        R_.dma('sp', lambda e, h=h: e.dma_start(out=hgp_d[l, h, :, :], in_=Sst[:]), reads=[bSst])
                    R_.op('act', lambda e, n=n: e.activation(out=rst[:, :n], in_=oacc[:, :n], func=AF.Square), reads=[boacc], writes=[brst])
                    psn, bpn = R_.ps_next()
                    R_.op('pe', lambda e, psn=psn, n=n: e.matmul(psn[:, :n], lhsT=ones[:], rhs=rst[:, :n], start=True, stop=True),
                         reads=[bones, brst], writes=[bpn])
                    R_.op('dve', lambda e, psn=psn, n=n: e.tensor_scalar(out=rst[:, :n], in0=psn[:, :n], scalar1=1.0 / 128, scalar2=1e-5,
                                                                        op0=ALU.mult, op1=ALU.add), reads=[bpn], writes=[brst])
                    R_.op('act', lambda e, n=n: e.activation(out=rst[:, :n], in_=rst[:, :n], func=AF.Sqrt), reads=[brst], writes=[brst])
                    R_.op('dve', lambda e, n=n: e.reciprocal(out=rst[:, :n], in_=rst[:, :n]), reads=[brst], writes=[brst])
                    R_.op('dve', lambda e, n=n, h=h: e.scalar_tensor_tensor(out=oacc[:, :n], in0=oacc[:, :n], scalar=hnw[:, l, h:h + 1],
                                                                            in1=rst[:, :n], op0=ALU.mult, op1=ALU.mult),
                         reads=[boacc, brst, bconst], writes=[boacc])
                    R_.op('act', lambda e, n=n: e.activation(out=tG[:, :n], in_=tG[:, :n], func=AF.Silu), reads=[bG], writes=[bG])
                    R_.op('dve', lambda e, n=n: e.tensor_tensor(out=mixo[:, :n], in0=oacc[:, :n], in1=tG[:, :n], op=ALU.mult),
                         reads=[boacc, bG], writes=[bmixo])
                    R_.dma('sp', lambda e, h=h, t0=t0, n=n: e.dma_start(out=msc[1280 + 128 * h:1280 + 128 * h + 128, t0:t0 + n], in_=mixo[:, :n]),
                          reads=[bmixo], writes=[bmsc_all[pi]])


        a_reset()
        cosT = a_alloc([128, 16, 128]); sinT = a_alloc([128, 16, 128]); RMp = a_alloc([128, 16, 128]); RMs = a_alloc([128, 16, 128])
        bTab = Buf('s5tab')
        tA = a_alloc([128, 16, 128]); tBt = a_alloc([128, 16, 128]); zr = a_alloc([128, 16, 128]); zi = a_alloc([128, 16, 128])
        bA, bBt, bzr, bzi = Buf('A'), Buf('Bt'), Buf('zr'), Buf('zi')
        cblk = a_alloc([128, 32, 128]); bcblk = Buf('cblk')
        s5fm = sb("s5fm_t", [128, 3, 16]); fmw = sb("fmw", [128, 6, 16]); bfm = Buf('fm')
        Bp = sb("Bp", [128, 2, 2, 4, 128], BF16); bBbar = Buf('Bbar')
        utb = sb("utb", [128, 4, 128], BF16); butb = Buf('utb')
        rmask2 = sb("rmask2_t", [128, 2])
        zer128 = sb("zer128", [128, 128], BF16)
        P.op('dve', lambda e: e.memset(zer128[:], 0.0), writes=[bconst])
        P.dma('sp', lambda e: e.dma_start(out=rmask2[:], in_=rmask2_d), writes=[bconst])
        s5db = sb("s5db_t", [128, 2, 4]); wglu = sb("wglu", [128, 4, 512], BF16); bs5p = Buf('s5p')
        ut = sb("ut", [128, 4, 128]); but = Buf('u')
        xc = sb("xc", [128, 2, 16, NS]); bxc = Buf('xc')
        wk = sb("wk", [128, 6, 16, NS]); bwk = Buf('wk')
        ysb = sb("ysb", [128, 4, 128]); bysb = Buf('ysb')
        yb = sb("yb", [128, 4, 128], BF16); byb = Buf('yb')
        yt = [sb("yt%d" % i, [128, 128]) for i in range(2)]; byt = [Buf() for _ in range(2)]
        TWO_PI = 2.0 * np.pi

        def flat(v):
            return v.rearrange("p a b -> p (a b)")

        def sin_reduce(out, src, tmpf, tmpi, bufs_r, bufs_w):
            rw = list(bufs_r) + list(bufs_w)
            P.op('dve', lambda e: e.tensor_scalar(out=tmpf, in0=src, scalar1=1.0 / TWO_PI, scalar2=None, op0=ALU.mult), reads=rw, writes=bufs_w)
            P.op('dve', lambda e: e.tensor_copy(out=tmpi, in_=tmpf), reads=rw, writes=bufs_w)
            P.op('dve', lambda e: e.tensor_copy(out=tmpf, in_=tmpi), reads=rw, writes=bufs_w)
            P.op('dve', lambda e: e.scalar_tensor_tensor(out=out, in0=tmpf, scalar=-TWO_PI, in1=src, op0=ALU.mult, op1=ALU.add), reads=rw, writes=bufs_w)
            P.op('dve', lambda e: e.tensor_scalar(out=tmpf, in0=out, scalar1=float(np.pi), scalar2=-TWO_PI, op0=ALU.is_gt, op1=ALU.mult), reads=rw, writes=bufs_w)
            P.op('dve', lambda e: e.tensor_tensor(out=out, in0=out, in1=tmpf, op=ALU.add), reads=rw, writes=bufs_w)
            P.op('dve', lambda e: e.tensor_scalar(out=tmpf, in0=out, scalar1=-float(np.pi), scalar2=TWO_PI, op0=ALU.is_lt, op1=ALU.mult), reads=rw, writes=bufs_w)
            P.op('dve', lambda e: e.tensor_tensor(out=out, in0=out, in1=tmpf, op=ALU.add), reads=rw, writes=bufs_w)
            P.op('dve', lambda e: e.tensor_scalar(out=out, in0=out, scalar1=3.1415925, scalar2=-3.1415925, op0=ALU.min, op1=ALU.max), reads=rw, writes=bufs_w)
            P.op('act', lambda e: e.activation(out=out, in_=out, func=AF.Sin), reads=rw, writes=bufs_w)

        def s5_setup(l):
            allb = [bTab, bA, bBt, bzr, bzi]
            P.dma('sp', lambda e: e.dma_start(out=s5fm[:], in_=s5fm_d[:, l]), writes=[bfm])
            P.dma('sp', lambda e: e.dma_start(out=s5db[:], in_=s5db_d[:, l]), writes=[bs5p])
            P.dma('pool', lambda e: e.dma_start(out=wglu[:], in_=s5w_d[:, l]), writes=[bs5p])
            P.dma('sp', lambda e: e.dma_start(out=cblk[:], in_=s5c_d[:, l].rearrange("p a j m -> p (a j) m")), writes=[bcblk])
            P.dma('sp', lambda e: e.dma_start(out=tBt[:, 0:12, :], in_=s5row_d[:, l]), writes=[bBt])
            P.dma('sp', lambda e: e.dma_start(out=tBt[:, 12:16, :], in_=s5b_d[:, l, 0]), writes=[bBt])
            P.dma('sp', lambda e: e.dma_start(out=tA[:, 12:16, :], in_=s5b_d[:, l, 1]), writes=[bA])
            are, aim, ldt = s5fm[:, 0, :], s5fm[:, 1, :], s5fm[:, 2, :]
            dt_, th, lr, rho, abr, abi = (fmw[:, i, :] for i in range(6))
            P.op('act', lambda e: e.activation(out=dt_, in_=ldt, func=AF.Exp), reads=[bfm], writes=[bfm])
            P.op('dve', lambda e: e.tensor_tensor(out=th, in0=aim, in1=dt_, op=ALU.mult), reads=[bfm], writes=[bfm])
            P.op('dve', lambda e: e.tensor_tensor(out=lr, in0=are, in1=dt_, op=ALU.mult), reads=[bfm], writes=[bfm])
            P.op('act', lambda e: e.activation(out=rho, in_=lr, func=AF.Exp), reads=[bfm], writes=[bfm])
            for j in range(16):
                P.op('dve', lambda e, j=j: e.tensor_scalar(out=zr[:, j, :], in0=iota[:], scalar1=fmw[:, 1, j:j + 1], scalar2=None, op0=ALU.mult),
                     reads=[bfm, bconst], writes=[bzr])
                P.op('dve', lambda e, j=j: e.tensor_scalar(out=RMp[:, j, :], in0=cm[:, 2, 0:128], scalar1=fmw[:, 3, j:j + 1], scalar2=None, op0=ALU.mult),
                     reads=[bfm, bconst], writes=[bTab])
                P.op('dve', lambda e, j=j: e.tensor_tensor(out=RMs[:, j, :], in0=RMp[:, j, :], in1=cm[:, 1, 0:128], op=ALU.mult),
                     reads=[bTab, bconst], writes=[bTab])
            tAf = flat(tA[:, 0:12, :])
            zif = flat(zi)
            sin_reduce(flat(sinT), flat(zr), flat(cosT), flat(zi).bitcast(I32), [bzr], [bTab, bzi])
            P.op('dve', lambda e: e.tensor_scalar(out=flat(zr), in0=flat(zr), scalar1=float(np.pi / 2), scalar2=None, op0=ALU.add), reads=[bzr, bTab], writes=[bzr])
            sin_reduce_cos(l)
            P.op('dve', lambda e: e.tensor_tensor(out=abr, in0=rho, in1=cosT[:, :, 1], op=ALU.mult), reads=[bfm, bTab], writes=[bfm])
            P.op('dve', lambda e: e.tensor_tensor(out=abi, in0=rho, in1=sinT[:, :, 1], op=ALU.mult), reads=[bfm, bTab], writes=[bfm])
            R = [flat(tBt[:, 4 * i:4 * i + 4, :]) for i in range(4)]
            ar, ai, ld, bre = R
            bim = flat(tA[:, 12:16, :])
            T = [flat(tA[:, 4 * i:4 * i + 4, :]) for i in range(3)] + [flat(zr[:, 4 * i:4 * i + 4, :]) for i in range(4)]
            tmpi = flat(zi[:, 0:4, :]).bitcast(I32)
            bb = [bA, bBt, bzr, bzi]

            def dv(fn):
                P.op('dve', fn, reads=bb, writes=bb)

            def ac(fn):
                P.op('act', fn, reads=bb, writes=bb)
            ac(lambda e: e.activation(out=ld, in_=ld, func=AF.Exp))
            dv(lambda e: e.tensor_tensor(out=T[0], in0=ai, in1=ld, op=ALU.mult))
            dv(lambda e: e.tensor_tensor(out=T[1], in0=ar, in1=ld, op=ALU.mult))
            ac(lambda e: e.activation(out=T[1], in_=T[1], func=AF.Exp))
            sin_reduce(T[3], T[0], T[5], tmpi, bb, bb)
            dv(lambda e: e.tensor_scalar(out=T[0], in0=T[0], scalar1=float(np.pi / 2), scalar2=None, op0=ALU.add))
            sin_reduce(T[4], T[0], T[5], tmpi, bb, bb)
            dv(lambda e: e.tensor_tensor(out=T[4], in0=T[1], in1=T[4], op=ALU.mult))
            dv(lambda e: e.tensor_scalar(out=T[4], in0=T[4], scalar1=-1.0, scalar2=None, op0=ALU.add))
            dv(lambda e: e.tensor_tensor(out=T[3], in0=T[1], in1=T[3], op=ALU.mult))
            dv(lambda e: e.tensor_tensor(out=T[5], in0=ar, in1=ar, op=ALU.mult))
            dv(lambda e: e.tensor_tensor(out=T[6], in0=ai, in1=ai, op=ALU.mult))
            dv(lambda e: e.tensor_tensor(out=T[5], in0=T[5], in1=T[6], op=ALU.add))
            dv(lambda e: e.reciprocal(out=T[5], in_=T[5]))
            dv(lambda e: e.tensor_tensor(out=T[6], in0=T[4], in1=ar, op=ALU.mult))
            dv(lambda e: e.tensor_tensor(out=T[2], in0=T[3], in1=ai, op=ALU.mult))
            dv(lambda e: e.tensor_tensor(out=T[6], in0=T[6], in1=T[2], op=ALU.add))
            dv(lambda e: e.tensor_tensor(out=T[6], in0=T[6], in1=T[5], op=ALU.mult))
            dv(lambda e: e.tensor_tensor(out=T[2], in0=T[3], in1=ar, op=ALU.mult))
            dv(lambda e: e.tensor_tensor(out=T[0], in0=T[4], in1=ai, op=ALU.mult))
            dv(lambda e: e.tensor_tensor(out=T[2], in0=T[2], in1=T[0], op=ALU.subtract))
            dv(lambda e: e.tensor_tensor(out=T[2], in0=T[2], in1=T[5], op=ALU.mult))
            dv(lambda e: e.tensor_tensor(out=T[0], in0=T[6], in1=bre, op=ALU.mult))
            dv(lambda e: e.tensor_tensor(out=T[1], in0=T[2], in1=bim, op=ALU.mult))
            dv(lambda e: e.tensor_tensor(out=T[0], in0=T[0], in1=T[1], op=ALU.subtract))
            for sl in range(2):
                P.op('dve', lambda e, sl=sl: e.tensor_scalar(out=flat(Bp[:, 0, sl]), in0=T[0], scalar1=rmask2[:, sl:sl + 1], scalar2=None, op0=ALU.mult),
                     reads=bb + [bconst], writes=[bBbar])
            dv(lambda e: e.tensor_tensor(out=T[0], in0=T[6], in1=bim, op=ALU.mult))
            dv(lambda e: e.tensor_tensor(out=T[1], in0=T[2], in1=bre, op=ALU.mult))
            dv(lambda e: e.tensor_tensor(out=T[0], in0=T[0], in1=T[1], op=ALU.add))
            for sl in range(2):
                P.op('dve', lambda e, sl=sl: e.tensor_scalar(out=flat(Bp[:, 1, sl]), in0=T[0], scalar1=rmask2[:, sl:sl + 1], scalar2=None, op0=ALU.mult),
                     reads=bb + [bconst], writes=[bBbar])

        def sin_reduce_cos(l):
            sin_reduce(flat(cosT), flat(zr), flat(tA), flat(zi).bitcast(I32), [bzr], [bTab, bzi, bA])
            P.dma('sp', lambda e: e.dma_start(out=tA[:, 12:16, :], in_=s5b_d[:, l, 1]), writes=[bA])

        pieces128 = [(i * 128, 128, False) for i in range(cfg.tp // 128)] + [(cfg.tp, 128, True)]

        def s5(l):
            s5_setup(l)
            if cfg.stop <= 1:
                return
            cflat, sflat = flat(cosT), flat(sinT)
            PRv = psall[:, 0:4, :].rearrange("p b c -> p (b c)")
            PIv = psall[:, 4:8, :].rearrange("p b c -> p (b c)")
            bPR, bPI = bps[0:4], bps[4:8]
            SK = int(os.environ.get('S5SKIP', 0))
            if not SK & 1:
                P.op('dve', lambda e: e.memset(xc[:].rearrange("p a b c -> p (a b c)"), 0.0), writes=[bxc])
            for (t0, n, samp) in pieces128:
                bi = min(t0 // TB, len(cfg.blocks) - 1)
                P.dma('sp', lambda e, t0=t0: e.dma_start(out=ut[:], in_=psc[0:512, t0:t0 + 128].rearrange("(q p) t -> p q t", p=128)),
                      reads=[bpsc_all[bi]], writes=[but])
                if samp:
                    P.dma('sp', lambda e: e.dma_start(out=xc[:], in_=s5x0_d[:, l]), writes=[bxc])
                if not SK & 2:
                    P.op('act', lambda e: e.activation(out=utb[:], in_=ut[:], func=AF.Copy), reads=[but], writes=[butb])
                for j in range(16):
                    q, r, sl = j // 4, 64 * ((j % 4) // 2), j % 2
                    psR, bpR = ps_next()
                    psI, bpI = ps_next()
                    for c_, pso_, bpo_ in ((0, psR, bpR), (1, psI, bpI)):
                        P.op('pe', lambda e, q=q, r=r, sl=sl, c_=c_, pso_=pso_: e.matmul(
                            pso_[:, 0:128], lhsT=Bp[r:r + 64, c_, sl, q, :], rhs=utb[r:r + 64, q, :],
                            start=True, stop=True), reads=[bBbar, butb], writes=[bpo_])
                    PRb, PIb = psR[:, 0:128], psI[:, 0:128]
                    cf_, sf_ = cosT[:, j, :], sinT[:, j, :]
                    fA, fB, fzr, fzi = tA[:, j, :], tBt[:, j, :], zr[:, j, :], zi[:, j, :]
                    P.op('dve', lambda e, PRb=PRb, cf_=cf_, fA=fA: e.tensor_tensor(out=fA, in0=PRb, in1=cf_, op=ALU.mult), reads=[bpR, bTab], writes=[bA])
                    P.op('dve', lambda e, PIb=PIb, sf_=sf_, fB=fB: e.tensor_tensor(out=fB, in0=PIb, in1=sf_, op=ALU.mult), reads=[bpI, bTab], writes=[bBt])
                    P.op('dve', lambda e, fA=fA, fB=fB, fzr=fzr: e.tensor_tensor(out=fzr, in0=fA, in1=fB, op=ALU.add), reads=[bA, bBt], writes=[bzr])
                    P.op('dve', lambda e, PIb=PIb, cf_=cf_, fA=fA: e.tensor_tensor(out=fA, in0=PIb, in1=cf_, op=ALU.mult), reads=[bpI, bTab, bzr], writes=[bA])
                    P.op('dve', lambda e, PRb=PRb, sf_=sf_, fB=fB: e.tensor_tensor(out=fB, in0=PRb, in1=sf_, op=ALU.mult), reads=[bpR, bTab, bzr], writes=[bBt])
                    P.op('dve', lambda e, fA=fA, fB=fB, fzi=fzi: e.tensor_tensor(out=fzi, in0=fA, in1=fB, op=ALU.subtract), reads=[bA, bBt], writes=[bzi])
                if cfg.stop <= 2:
                    return
                nc_ = NS if samp else 1
                Xr, Xi = xc[:, 0, :, 0:nc_], xc[:, 1, :, 0:nc_]
                w = [wk[:, i, :, 0:nc_] for i in range(6)]
                if samp:
                    abr_b = fmw[:, 4, :].unsqueeze(2).to_broadcast([128, 16, NS])
                    abi_b = fmw[:, 5, :].unsqueeze(2).to_broadcast([128, 16, NS])
                    csel = lambda tbl: tbl[:, :, 0:128:8]
                else:
                    abr_b = fmw[:, 4, :].unsqueeze(2)
                    abi_b = fmw[:, 5, :].unsqueeze(2)
                    csel = lambda tbl: tbl[:, :, 0:1]
                rb_ = [bxc, bwk, bfm, bTab]

                def dw(fn):
                    P.op('dve', fn, reads=rb_, writes=[bwk])
                dw(lambda e, Xr=Xr, abr_b=abr_b, w=w: e.tensor_tensor(out=w[0], in0=Xr, in1=abr_b, op=ALU.mult))
                dw(lambda e, Xi=Xi, abi_b=abi_b, w=w: e.tensor_tensor(out=w[1], in0=Xi, in1=abi_b, op=ALU.mult))
                dw(lambda e, w=w: e.tensor_tensor(out=w[0], in0=w[0], in1=w[1], op=ALU.subtract))
                dw(lambda e, Xi=Xi, abr_b=abr_b, w=w: e.tensor_tensor(out=w[1], in0=Xi, in1=abr_b, op=ALU.mult))
                dw(lambda e, Xr=Xr, abi_b=abi_b, w=w: e.tensor_tensor(out=w[2], in0=Xr, in1=abi_b, op=ALU.mult))
                dw(lambda e, w=w: e.tensor_tensor(out=w[1], in0=w[1], in1=w[2], op=ALU.add))
                cs, ss = csel(cosT), csel(sinT)
                dw(lambda e, w=w, cs=cs: e.tensor_tensor(out=w[2], in0=w[0], in1=cs, op=ALU.mult))
                dw(lambda e, w=w, ss=ss: e.tensor_tensor(out=w[3], in0=w[1], in1=ss, op=ALU.mult))
                dw(lambda e, w=w: e.tensor_tensor(out=w[2], in0=w[2], in1=w[3], op=ALU.add))
                dw(lambda e, w=w, cs=cs: e.tensor_tensor(out=w[3], in0=w[1], in1=cs, op=ALU.mult))
                dw(lambda e, w=w, ss=ss: e.tensor_tensor(out=w[4], in0=w[0], in1=ss, op=ALU.mult))
                dw(lambda e, w=w: e.tensor_tensor(out=w[3], in0=w[3], in1=w[4], op=ALU.subtract))
                zrc, zic = csel(zr), csel(zi)
                P.op('dve', lambda e, zrc=zrc, w=w: e.tensor_tensor(out=zrc, in0=zrc, in1=w[2], op=ALU.add), reads=[bwk, bzr], writes=[bzr])
                P.op('dve', lambda e, zic=zic, w=w: e.tensor_tensor(out=zic, in0=zic, in1=w[3], op=ALU.add), reads=[bwk, bzi], writes=[bzi])
                if cfg.stop <= 3:
                    return
                RM = flat(RMs) if samp else flat(RMp)
                P.op('dve', lambda e, RM=RM: e.tensor_tensor_scan(out=flat(tA), data0=RM, data1=flat(zr), initial=0.0, op0=ALU.mult, op1=ALU.add),
                     reads=[bTab, bzr], writes=[bA])
                P.op('dve', lambda e, RM=RM: e.tensor_tensor_scan(out=flat(tBt), data0=RM, data1=flat(zi), initial=0.0, op0=ALU.mult, op1=ALU.add),
                     reads=[bTab, bzi], writes=[bBt])
                P.op('dve', lambda e: e.tensor_tensor(out=flat(zr), in0=flat(tA), in1=cflat, op=ALU.mult), reads=[bA, bTab], writes=[bzr])
                P.op('dve', lambda e: e.tensor_tensor(out=flat(zi), in0=flat(tBt), in1=sflat, op=ALU.mult), reads=[bBt, bTab], writes=[bzi])
                P.op('dve', lambda e: e.tensor_tensor(out=flat(zr), in0=flat(zr), in1=flat(zi), op=ALU.subtract), reads=[bzr, bzi], writes=[bzr])
                P.op('dve', lambda e: e.tensor_tensor(out=flat(zi), in0=flat(tA), in1=sflat, op=ALU.mult), reads=[bA, bTab, bzr], writes=[bzi])
                P.op('dve', lambda e: e.tensor_tensor(out=flat(tA), in0=flat(tBt), in1=cflat, op=ALU.mult), reads=[bBt, bTab, bzi], writes=[bA])
                P.op('dve', lambda e: e.tensor_tensor(out=flat(zi), in0=flat(zi), in1=flat(tA), op=ALU.add), reads=[bzi, bA], writes=[bzi])
                if cfg.stop <= 4:
                    return
                if samp:
                    P.op('act', lambda e: e.activation(out=xc[:, 0], in_=zr[:, :, 7:128:8], func=AF.Copy), reads=[bzr, bwk], writes=[bxc])
                    P.op('act', lambda e: e.activation(out=xc[:, 1], in_=zi[:, :, 7:128:8], func=AF.Copy), reads=[bzi, bwk], writes=[bxc])
                    P.dma('sp', lambda e: e.dma_start(out=s5s_d[:, l], in_=xc[:]), reads=[bxc])
                else:
                    P.op('act', lambda e: e.activation(out=xc[:, 0, :, 0:1], in_=zr[:, :, 127:128], func=AF.Copy), reads=[bzr, bwk], writes=[bxc])
                    P.op('act', lambda e: e.activation(out=xc[:, 1, :, 0:1], in_=zi[:, :, 127:128], func=AF.Copy), reads=[bzi, bwk], writes=[bxc])
                    if t0 + 128 == cfg.tp:
                        P.dma('sp', lambda e: e.dma_start(out=s5p_d[:, l], in_=xc[:, :, :, 0], allow_slow_non_contiguous=True), reads=[bxc])
                for q in range(4):
                    psA, bpA = ps_next()
                    psB, bpB = ps_next()
                    for jj in range(4):
                        j = 4 * q + jj
                        P.op('pe', lambda e, psA=psA, j=j, jj=jj: e.matmul(psA[:, 0:128], lhsT=cblk[:, j, :], rhs=zr[:, j, :], start=(jj == 0), stop=(jj == 3)),
                             reads=[bcblk, bzr], writes=[bpA])
                        P.op('pe', lambda e, psB=psB, j=j, jj=jj: e.matmul(psB[:, 0:128], lhsT=cblk[:, 16 + j, :], rhs=zi[:, j, :], start=(jj == 0), stop=(jj == 3)),
                             reads=[bcblk, bzi], writes=[bpB])
                    P.op('act', lambda e, psB=psB: e.activation(out=yt[0][:], in_=psB[:, 0:128], func=AF.Copy), reads=[bpB], writes=[byt[0]])
                    P.op('dve', lambda e, psA=psA: e.tensor_tensor(out=yt[0][:], in0=psA[:, 0:128], in1=yt[0][:], op=ALU.subtract), reads=[bpA, byt[0]], writes=[byt[0]])
                    P.op('dve', lambda e, q=q: e.scalar_tensor_tensor(out=yt[0][:], in0=ut[:, q, :], scalar=s5db[:, 0, q:q + 1], in1=yt[0][:], op0=ALU.mult, op1=ALU.add),
                         reads=[but, bs5p, byt[0]], writes=[byt[0]])
                    P.op('dve', lambda e: e.tensor_tensor(out=yt[1][:], in0=yt[0][:], in1=yt[0][:], op=ALU.mult), reads=[byt[0]], writes=[byt[1]])
                    P.op('dve', lambda e: e.tensor_scalar(out=yt[1][:], in0=yt[1][:], scalar1=0.044715, scalar2=1.0, op0=ALU.mult, op1=ALU.add), reads=[byt[1]], writes=[byt[1]])
                    P.op('dve', lambda e: e.tensor_tensor(out=yt[1][:], in0=yt[1][:], in1=yt[0][:], op=ALU.mult), reads=[byt[0], byt[1]], writes=[byt[1]])
                    P.op('act', lambda e: e.activation(out=yt[1][:], in_=yt[1][:], func=AF.Sigmoid, scale=1.5957691216057308), reads=[byt[1]], writes=[byt[1]])
                    P.op('dve', lambda e, q=q: e.tensor_tensor(out=ysb[:, q, :], in0=yt[0][:], in1=yt[1][:], op=ALU.mult), reads=[byt[0], byt[1]], writes=[bysb])
                    P.op('act', lambda e, q=q: e.activation(out=yb[:, q, :], in_=ysb[:, q, :], func=AF.Copy), reads=[bysb], writes=[byb])
                for q2 in range(4):
                    ps, bp = ps_next()
                    for q in range(4):
                        P.op('pe', lambda e, ps=ps, q=q, q2=q2: e.matmul(ps[:, 0:128], lhsT=wglu[:, q, q2 * 128:(q2 + 1) * 128], rhs=yb[:, q, :],
                                                                         start=(q == 0), stop=(q == 3)), reads=[bs5p, byb], writes=[bp])
                    P.op('act', lambda e, ps=ps, q2=q2: e.activation(out=yt[1][:], in_=ps[:, 0:128], func=AF.Sigmoid, bias=s5db[:, 1, q2:q2 + 1]),
                         reads=[bp, bs5p], writes=[byt[1]])
                    P.op('dve', lambda e, q2=q2: e.tensor_tensor(out=mixo[:, 0:128], in0=ysb[:, q2, :], in1=yt[1][:], op=ALU.mult), reads=[bysb, byt[1]], writes=[bmixo])
                    P.dma('sp', lambda e, q2=q2, t0=t0: e.dma_start(out=msc[q2 * 128:(q2 + 1) * 128, t0:t0 + 128], in_=mixo[:, 0:128]),
                          reads=[bmixo], writes=[bmsc_all[bi]])


        a_reset()
        zt = a_alloc([128, 20, 128]); zp = a_alloc([128, 20, 128]); bzt, bzp = Buf('zt'), Buf('zp')
        F6 = [a_alloc([128, 6, 128]) for _ in range(11)]; bF6 = [Buf('f6_%d' % i) for i in range(11)]
        F6.append(F6[1]); bF6.append(bF6[1])
        ARt = a_alloc([128, 6, 256], BF16); bAR = Buf('AR')
        ktl = a_alloc([128, 6, 128], BF16); btl = a_alloc([128, 6, 128], BF16); bktl, bbtl = Buf('ktl'), Buf('btl')
        twa = a_alloc([128, 128], BF16); sgi = a_alloc([128, 128], BF16); btwa, bsgi = Buf(), Buf()
        def rw_set():
            d = {}
            d['vA'] = a_alloc([128, 128], BF16); d['vB'] = a_alloc([128, 128], BF16); d['bvv'] = Buf('vAB')
            d['ktm_'] = a_alloc([128, 128], BF16); d['btm_'] = a_alloc([128, 128], BF16); d['bktm_'] = Buf(); d['bbtm_'] = Buf()
            d['NBm'] = [a_alloc([128, 256], BF16) for _ in range(2)]; d['KAm'] = [a_alloc([128, 256], BF16) for _ in range(2)]
            d['bNB'] = [Buf() for _ in range(2)]; d['bKA'] = [Buf() for _ in range(2)]
            d['Xm'] = [[a_alloc([128, 128], BF16) for _ in range(2)] for _ in range(2)]
            d['Ym'] = [[a_alloc([128, 128], BF16) for _ in range(2)] for _ in range(2)]
            d['Ttm'] = [a_alloc([128, 128], BF16) for _ in range(2)]; d['Tnm'] = [a_alloc([128, 128], BF16) for _ in range(2)]
            d['bXm'] = [[Buf() for _ in range(2)] for _ in range(2)]; d['bYm'] = [[Buf() for _ in range(2)] for _ in range(2)]
            d['bTt'] = [Buf() for _ in range(2)]; d['bTn'] = [Buf() for _ in range(2)]
            d['Ttf'] = [a_alloc([128, 128]) for _ in range(2)]; d['Tnf'] = [a_alloc([128, 128]) for _ in range(2)]
            d['WpA'] = a_alloc([128, 128], BF16); d['WpB'] = a_alloc([128, 128], BF16); d['bWp'] = Buf('Wp')
            d['UpA'] = a_alloc([128, 128], BF16); d['UpB'] = a_alloc([128, 128], BF16); d['bUp'] = Buf('Up')
            d['t128'] = [a_alloc([128, 128]) for _ in range(3)]; d['bt128'] = [Buf() for _ in range(3)]
            d['mixr'] = a_alloc([128, 128], BF16); d['bmixr'] = Buf()
            return d
        RWS = [rw_set(), rw_set()]
        Hblk = a_alloc([128, 6, 128]); Hbb = a_alloc([128, 6, 128], BF16); bHc = [Buf('H%d' % i) for i in range(6)]; bHbc = [Buf('Hb%d' % i) for i in range(6)]
        Hs = a_alloc([128, NS, 128]); Hsb = a_alloc([128, NS, 128], BF16); bHs, bHsb = Buf('Hs'), Buf('Hsb')
        ztf = zt.rearrange("p a b -> p (a b)")
        UmA = ztf[:, 0:1024].bitcast(BF16).rearrange("p (a b) -> p a b", a=NS); bUm = bzt
        VmA = ztf[:, 1024:2048].bitcast(BF16).rearrange("p (a b) -> p a b", a=NS); bVm = bzt
        yfm = a_alloc([128, 6, 128]); byfmc = [Buf('yfm%d' % i) for i in range(6)]
        shc = a_alloc([128, 20, NS]); bshc = Buf('shc')
        rwp = sb("rwp_t", [128, 62]); brwp = Buf('rwp')
        rwl = sb("rwl_t", [128, 2, 768], BF16); brwl = Buf('rwl')
        rwm = sb("rwm_t", [128, 2, 384], BF16); blk64 = sb("blk64_t", [128, 128]); cm6 = sb("cm6_t", [128, 2, 768], BF16)
        for t_, d_ in ((rwm, rwm_d), (blk64, blk64_d), (cm6, cm6_d)):
            P.dma('sp', lambda e, t_=t_, d_=d_: e.dma_start(out=t_[:], in_=d_), writes=[bconst])

        def rwkv(l):
            Dv = lambda fn, r, w: P.op('dve', fn, reads=r, writes=w)
            Ac = lambda fn, r, w: P.op('act', fn, reads=r, writes=w)
            Pe = lambda fn, r, w: P.op('pe', fn, reads=r, writes=w)
            P.dma('sp', lambda e: e.dma_start(out=rwp[:], in_=rwp_d[:, l]), writes=[brwp])
            P.dma('pool', lambda e: e.dma_start(out=rwl[:], in_=rwl_d[:, l]), writes=[brwl])
            mu = rwp[:, 0:20]
            w0, a0, k_k, k_a, r_k, ln_w, ln_b = (rwp[:, 20 + 6 * i:26 + 6 * i] for i in range(7))
            sg_a, lw, cc, ec, enc, ecm, kk, kp, gg, tm1, tm2, tm3 = F6
            b_a, b_lw, b_cc, b_ec, b_enc, b_ecm, b_kk, b_kp, b_gg, b_tm1, b_tm2, b_tm3 = bF6
            Dv(lambda e: e.memset(Hblk[:].rearrange("p a b -> p (a b)"), 0.0), [], bHc)
            Dv(lambda e: e.memset(Hbb[:].rearrange("p a b -> p (a b)"), 0.0), [], bHbc)
            for S_ in RWS:
                for z_ in (S_['WpA'], S_['WpB'], S_['UpA'], S_['UpB'], S_['vA'], S_['vB']):
                    Dv(lambda e, z_=z_: e.memset(z_[:], 0.0), [], [S_['bWp'], S_['bUp'], S_['bvv']])
            for (t0, n, samp) in pieces128:
                bi = min(t0 // TB, len(cfg.blocks) - 1)
                mi = 1 if samp else 0
                NLEV = 2 if samp else 6
                rd = [bpsc_all[bi]] + ([bpsc_all[bi - 1]] if bi > 0 else [])
                P.dma('sp', lambda e, t0=t0: e.dma_start(out=zt[:], in_=psc[512:3072, t0:t0 + 128].rearrange("(c p) t -> p c t", p=128)),
                      reads=rd, writes=[bzt])
                if t0 == 0:
                    Dv(lambda e: e.memset(zp[:, :, 0:1], 0.0), [], [bzp])
                    P.dma('sp', lambda e: e.dma_start(out=zp[:, :, 1:128], in_=psc[512:3072, 0:127].rearrange("(c p) t -> p c t", p=128)),
                          reads=rd, writes=[bzp])
                else:
                    P.dma('sp', lambda e, t0=t0: e.dma_start(out=zp[:], in_=psc[512:3072, t0 - 1:t0 + 127].rearrange("(c p) t -> p c t", p=128)),
                          reads=rd, writes=[bzp])
                if samp:
                    P.dma('sp', lambda e: e.dma_start(out=shc[:], in_=rwsh0_d[:, l]), writes=[bshc])
                    Dv(lambda e: e.tensor_copy(out=zp[:, :, 0:128:8], in_=shc[:]), [bshc, bzp], [bzp])
                    Ac(lambda e: e.activation(out=shc[:], in_=zt[:, :, 7:128:8], func=AF.Copy), [bzt, bzp], [bshc])
                    P.dma('sp', lambda e: e.dma_start(out=rwshs_d[:, l], in_=shc[:]), reads=[bshc])
                elif t0 + 128 == cfg.tp:
                    P.dma('sp', lambda e: e.dma_start(out=rwshp_d[:, l], in_=zt[:, :, 127], allow_slow_non_contiguous=True), reads=[bzt])
                Dv(lambda e: e.tensor_tensor(out=zp[:], in0=zp[:], in1=zt[:], op=ALU.subtract), [bzp, bzt], [bzp])
                Dv(lambda e: e.tensor_tensor(out=zp[:], in0=zp[:], in1=mu.unsqueeze(2).to_broadcast([128, 20, 128]), op=ALU.mult), [bzp, brwp], [bzp])
                Dv(lambda e: e.tensor_tensor(out=zp[:], in0=zp[:], in1=zt[:], op=ALU.add), [bzp, bzt], [bzp])
                zr_, zk_, zv_ = zp[:, 0:6, :], zp[:, 6:12, :], zp[:, 12:18, :]
                Ac(lambda e: e.activation(out=twa[0:64, :], in_=zp[0:64, 18, :], func=AF.Tanh), [bzp], [btwa])
                Ac(lambda e: e.activation(out=twa[64:128, :], in_=zp[64:128, 18, :], func=AF.Copy), [bzp], [btwa])
                Ac(lambda e: e.activation(out=sgi[:], in_=zp[:, 19, :], func=AF.Sigmoid), [bzp], [bsgi])
                for c in range(6):
                    ps, bp = ps_next()
                    Pe(lambda e, ps=ps, c=c: e.matmul(ps[:, 0:128], lhsT=rwl[0:64, 0, c * 128:(c + 1) * 128], rhs=twa[0:64, :], start=True, stop=True),
                       [brwl, btwa], [bp])
                    Ac(lambda e, ps=ps, c=c: e.activation(out=lw[:, c, :], in_=ps[:, 0:128], func=AF.Sigmoid, bias=w0[:, c:c + 1]), [bp, brwp], [b_lw])
                    ps, bp = ps_next()
                    Pe(lambda e, ps=ps, c=c: e.matmul(ps[:, 0:128], lhsT=rwl[64:128, 0, c * 128:(c + 1) * 128], rhs=twa[64:128, :], start=True, stop=True),
                       [brwl, btwa], [bp])
                    Ac(lambda e, ps=ps, c=c: e.activation(out=sg_a[:, c, :], in_=ps[:, 0:128], func=AF.Sigmoid, bias=a0[:, c:c + 1]), [bp, brwp], [b_a])
                    ps, bp = ps_next()
                    Pe(lambda e, ps=ps, c=c: e.matmul(ps[:, 0:128], lhsT=rwl[:, 1, c * 128:(c + 1) * 128], rhs=sgi[:], start=True, stop=True),
                       [brwl, bsgi], [bp])
                    Ac(lambda e, ps=ps, c=c: e.activation(out=gg[:, c, :], in_=ps[:, 0:128], func=AF.Copy), [bp], [b_gg])
                Dv(lambda e: e.tensor_scalar(out=lw[:], in0=lw[:], scalar1=-0.6065306597126334, scalar2=None, op0=ALU.mult), [b_lw], [b_lw])
                Dv(lambda e: e.tensor_tensor(out=kk[:], in0=zk_, in1=k_k.unsqueeze(2).to_broadcast([128, 6, 128]), op=ALU.mult), [bzp, brwp], [b_kk])
                Dv(lambda e: e.tensor_tensor(out=tm1[:], in0=kk[:], in1=kk[:], op=ALU.mult), [b_kk], [b_tm1])
                for c in range(6):
                    ps, bp = ps_next()
                    Pe(lambda e, ps=ps, c=c: e.matmul(ps[:, 0:128], lhsT=blk64[:], rhs=tm1[:, c, :], start=True, stop=True), [bconst, b_tm1], [bp])
                    Dv(lambda e, ps=ps, c=c: e.tensor_scalar(out=tm2[:, c, :], in0=ps[:, 0:128], scalar1=1e-24, scalar2=None, op0=ALU.max), [bp], [b_tm2])
                Ac(lambda e: e.activation(out=tm2[:], in_=tm2[:], func=AF.Sqrt), [b_tm2], [b_tm2])
                Dv(lambda e: e.reciprocal(out=tm2[:], in_=tm2[:]), [b_tm2], [b_tm2])
                Dv(lambda e: e.tensor_tensor(out=kk[:], in0=kk[:], in1=tm2[:], op=ALU.mult), [b_kk, b_tm2], [b_kk])
                Dv(lambda e: e.tensor_scalar(out=tm1[:], in0=sg_a[:], scalar1=-1.0, scalar2=None, op0=ALU.add), [b_a, b_tm1], [b_tm1])
                Dv(lambda e: e.tensor_tensor(out=tm1[:], in0=tm1[:], in1=k_a.unsqueeze(2).to_broadcast([128, 6, 128]), op=ALU.mult), [b_tm1, brwp], [b_tm1])
                Dv(lambda e: e.tensor_scalar(out=tm1[:], in0=tm1[:], scalar1=1.0, scalar2=None, op0=ALU.add), [b_tm1], [b_tm1])
                Dv(lambda e: e.tensor_tensor(out=kp[:], in0=zk_, in1=tm1[:], op=ALU.mult), [bzp, b_tm1], [b_kp])
                Dv(lambda e, mi=mi: e.tensor_tensor_scan(out=cc[:].rearrange("p a b -> p (a b)"), data0=cm6[:, mi, :], data1=lw[:].rearrange("p a b -> p (a b)"),
                                                         initial=0.0, op0=ALU.mult, op1=ALU.add), [b_lw, bconst], [b_cc])
                Ac(lambda e: e.activation(out=ec[:], in_=cc[:], func=AF.Exp), [b_cc], [b_ec])
                Ac(lambda e: e.activation(out=enc[:], in_=cc[:], func=AF.Exp, scale=-1.0), [b_cc], [b_enc])
                Dv(lambda e: e.tensor_tensor(out=ecm[:], in0=cc[:], in1=lw[:], op=ALU.subtract), [b_cc, b_lw], [b_ecm])
                Ac(lambda e: e.activation(out=ecm[:], in_=ecm[:], func=AF.Exp), [b_ecm], [b_ecm])
                Dv(lambda e: e.tensor_tensor(out=tm1[:], in0=kk[:], in1=ecm[:], op=ALU.mult), [b_kk, b_ecm, b_tm1], [b_tm1])
                Dv(lambda e: e.tensor_scalar(out=ARt[:, :, 0:128], in0=tm1[:], scalar1=-1.0, scalar2=None, op0=ALU.mult), [b_tm1], [bAR])
                Dv(lambda e: e.tensor_tensor(out=ARt[:, :, 128:256], in0=zr_, in1=ec[:], op=ALU.mult), [bzp, b_ec], [bAR])
                Dv(lambda e: e.tensor_tensor(out=tm2[:], in0=kp[:], in1=enc[:], op=ALU.mult), [b_kp, b_enc, b_tm2], [b_tm2])
                Dv(lambda e: e.tensor_tensor(out=tm3[:], in0=kk[:], in1=sg_a[:], op=ALU.mult), [b_kk, b_a], [b_tm3])
                Dv(lambda e: e.tensor_tensor(out=tm3[:], in0=tm3[:], in1=enc[:], op=ALU.mult), [b_tm3, b_enc], [b_tm3])
                Ac(lambda e: e.activation(out=ktl[:], in_=tm2[:], func=AF.Copy), [b_tm2], [bktl])
                Ac(lambda e: e.activation(out=btl[:], in_=tm3[:], func=AF.Copy), [b_tm3], [bbtl])
                def c_body(c, S_, R_):
                    vA, vB, bvv, ktm_, btm_, bktm_, bbtm_ = (S_[k] for k in ('vA', 'vB', 'bvv', 'ktm_', 'btm_', 'bktm_', 'bbtm_'))
                    NBm, KAm, bNB, bKA, Xm, Ym, bXm, bYm = (S_[k] for k in ('NBm', 'KAm', 'bNB', 'bKA', 'Xm', 'Ym', 'bXm', 'bYm'))
                    Ttm, Tnm, bTt, bTn, Ttf, Tnf = (S_[k] for k in ('Ttm', 'Tnm', 'bTt', 'bTn', 'Ttf', 'Tnf'))
                    WpA, WpB, bWp, UpA, UpB, bUp, t128, bt128 = (S_[k] for k in ('WpA', 'WpB', 'bWp', 'UpA', 'UpB', 'bUp', 't128', 'bt128'))
                    Dv = lambda fn, r, w: R_.op('dve', fn, reads=r, writes=w)
                    Ac = lambda fn, r, w: R_.op('act', fn, reads=r, writes=w)
                    Pe = lambda fn, r, w: R_.op('pe', fn, reads=r, writes=w)
                    bH, bHb, byfm = bHc[c], bHbc[c], byfmc[c]
                    ps, bp = R_.ps_next()
                    Pe(lambda e, ps=ps, c=c: e.transpose(ps[:, 0:128], zp[:, 12 + c, :], ident[:]), [bzp, bconst], [bp])
                    Ac(lambda e, ps=ps: e.activation(out=vA[:, 0:64], in_=ps[:, 0:64], func=AF.Copy), [bp], [bvv])
                    Ac(lambda e, ps=ps: e.activation(out=vB[:, 64:128], in_=ps[:, 64:128], func=AF.Copy), [bp], [bvv])
                    ps, bp = R_.ps_next()
                    Pe(lambda e, ps=ps, c=c: e.transpose(ps[:, 0:128], tm2[:, c, :], ident[:]), [b_tm2, bconst], [bp])
                    Ac(lambda e, ps=ps: e.activation(out=ktm_[:], in_=ps[:, 0:128], func=AF.Copy), [bp], [bktm_])
                    ps, bp = R_.ps_next()
                    Pe(lambda e, ps=ps, c=c: e.transpose(ps[:, 0:128], tm3[:, c, :], ident[:]), [b_tm3, bconst], [bp])
                    Ac(lambda e, ps=ps: e.activation(out=btm_[:], in_=ps[:, 0:128], func=AF.Copy), [bp], [bbtm_])
                    def hp_chain(hp, R_, c=c, mi=mi, NLEV=NLEV):
                        Dv = lambda fn, r, w: R_.op('dve', fn, reads=r, writes=w)
                        Ac = lambda fn, r, w: R_.op('act', fn, reads=r, writes=w)
                        Pe = lambda fn, r, w: R_.op('pe', fn, reads=r, writes=w)
                        r0 = 64 * hp
                        ps, bp = R_.ps_next()
                        Pe(lambda e, ps=ps, c=c, r0=r0: e.matmul(ps[:, 0:256], lhsT=btl[r0:r0 + 64, c, :], rhs=ARt[r0:r0 + 64, c, :], start=True, stop=True),
                           [bbtl, bAR], [bp])
                        Dv(lambda e, ps=ps, hp=hp, mi=mi: e.tensor_tensor(out=NBm[hp][:], in0=ps[:, 0:256], in1=rwm[:, mi, 0:256], op=ALU.mult), [bp, bconst], [bNB[hp]])
                        ps, bp = R_.ps_next()
                        Pe(lambda e, ps=ps, c=c, r0=r0: e.matmul(ps[:, 0:256], lhsT=ktl[r0:r0 + 64, c, :], rhs=ARt[r0:r0 + 64, c, :], start=True, stop=True),
                           [bktl, bAR], [bp])
                        Dv(lambda e, ps=ps, hp=hp, mi=mi: e.tensor_tensor(out=KAm[hp][:], in0=ps[:, 0:256], in1=rwm[:, mi, 0:256], op=ALU.mult), [bp, bconst], [bKA[hp]])
                        ps, bp = R_.ps_next()
                        Pe(lambda e, ps=ps, c=c, r0=r0: e.matmul(ps[:, 0:128], lhsT=ARt[r0:r0 + 64, c, 0:128], rhs=btl[r0:r0 + 64, c, :], start=True, stop=True),
                           [bbtl, bAR], [bp])
                        Dv(lambda e, ps=ps, hp=hp, mi=mi: e.tensor_tensor(out=Ym[hp][0][:], in0=ps[:, 0:128], in1=rwm[:, mi, 256:384], op=ALU.mult), [bp, bconst], [bYm[hp][0]])
                        X0 = NBm[hp][:, 0:128]
                        Dv(lambda e, hp=hp, X0=X0: e.tensor_tensor(out=Ttf[hp][:], in0=X0, in1=ident[:], op=ALU.add), [bNB[hp], bconst], [bTt[hp]])
                        Dv(lambda e, hp=hp: e.tensor_tensor(out=Tnf[hp][:], in0=Ym[hp][0][:], in1=ident[:], op=ALU.add), [bYm[hp][0], bconst], [bTn[hp]])
                        Ac(lambda e, hp=hp: e.activation(out=Ttm[hp][:], in_=Ttf[hp][:], func=AF.Copy), [bTt[hp]], [bTt[hp]])
                        Ac(lambda e, hp=hp: e.activation(out=Tnm[hp][:], in_=Tnf[hp][:], func=AF.Copy), [bTn[hp]], [bTn[hp]])
                        Xp, bXp = X0, bNB[hp]
                        Yp, bYp = Ym[hp][0][:], bYm[hp][0]
                        for lev in range(1, NLEV + 1):
                            cur = lev % 2
                            last = (lev == NLEV)
                            ps, bp = R_.ps_next()
                            Pe(lambda e, ps=ps, Xp=Xp, Yp=Yp: e.matmul(ps[:, 0:128], lhsT=Yp, rhs=Xp, start=True, stop=True), [bXp, bYp], [bp])
                            Xc, bXc = Xm[hp][cur][:], bXm[hp][cur]
                            Ac(lambda e, ps=ps, Xc=Xc: e.activation(out=Xc, in_=ps[:, 0:128], func=AF.Copy), [bp], [bXc])
                            if not last:
                                ps2, bp2 = R_.ps_next()
                                Pe(lambda e, ps2=ps2, Xp=Xp, Yp=Yp: e.matmul(ps2[:, 0:128], lhsT=Xp, rhs=Yp, start=True, stop=True), [bXp, bYp], [bp2])
                                Yc, bYc = Ym[hp][cur][:], bYm[hp][cur]
                                Ac(lambda e, ps2=ps2, Yc=Yc: e.activation(out=Yc, in_=ps2[:, 0:128], func=AF.Copy), [bp2], [bYc])
                            ps3, bp3 = R_.ps_next()
                            Pe(lambda e, ps3=ps3, hp=hp, Xc=Xc: e.matmul(ps3[:, 0:128], lhsT=Tnm[hp][:], rhs=Xc, start=True, stop=True), [bTn[hp], bXc], [bp3])
                            Dv(lambda e, ps3=ps3, hp=hp: e.tensor_tensor(out=Ttf[hp][:], in0=ps3[:, 0:128], in1=Ttf[hp][:], op=ALU.add), [bp3, bTt[hp]], [bTt[hp]])
                            Ac(lambda e, hp=hp: e.activation(out=Ttm[hp][:], in_=Ttf[hp][:], func=AF.Copy), [bTt[hp]], [bTt[hp]])
                            if not last:
                                ps4, bp4 = R_.ps_next()
                                Pe(lambda e, ps4=ps4, hp=hp, Xc=Xc: e.matmul(ps4[:, 0:128], lhsT=Xc, rhs=Tnm[hp][:], start=True, stop=True), [bTn[hp], bXc], [bp4])
                                Dv(lambda e, ps4=ps4, hp=hp: e.tensor_tensor(out=Tnf[hp][:], in0=ps4[:, 0:128], in1=Tnf[hp][:], op=ALU.add), [bp4, bTn[hp]], [bTn[hp]])
                                Ac(lambda e, hp=hp: e.activation(out=Tnm[hp][:], in_=Tnf[hp][:], func=AF.Copy), [bTn[hp]], [bTn[hp]])
                                Xp, bXp, Yp, bYp = Xc, bXc, Yc, bYc

                    nbk = len(R_.banks) // 2
                    rq0, rq1 = Rec(R_.banks[:nbk]), Rec(R_.banks[nbk:])
                    hp_chain(0, rq0)
                    hp_chain(1, rq1)
                    merge([rq0, rq1], R_)
                    if samp:
                        R_.dma('sp', lambda e, c=c: e.dma_start(out=Hs[:], in_=rwh0_d[:, l, c]), writes=[bHs])
                        Ac(lambda e: e.activation(out=Hsb[:], in_=Hs[:], func=AF.Copy), [bHs], [bHsb])
                    psW, bpW = R_.ps_next()
                    if not samp:
                        Pe(lambda e, psW=psW, c=c: e.matmul(psW[:, 0:128], lhsT=ARt[:, c, 0:128], rhs=Hbb[:, c, :], start=True, stop=False), [bAR, bHb], [bpW])
                    else:
                        wsel = t128[0]
                        for g4 in range(4):
                            psq, bpq = R_.ps_next()
                            Pe(lambda e, psq=psq, c=c, g4=g4: e.matmul(psq[:, :], lhsT=ARt[:, c, 0:128], rhs=Hsb[:, 4 * g4:4 * g4 + 4, :].rearrange("p a b -> p (a b)"),
                                                                      start=True, stop=True), [bAR, bHsb], [bpq])
                            Dv(lambda e, psq=psq, g4=g4: e.tensor_tensor(out=UmA[:, 4 * g4:4 * g4 + 4, :].rearrange("p a b -> p (a b)"), in0=psq[:, :],
                                                                       in1=bm16[:, 4 * g4:4 * g4 + 4, :].rearrange("p a b -> p (a b)"), op=ALU.mult), [bpq, bconst], [bUm])
                        Dv(lambda e: e.tensor_reduce(out=t128[0][:], in_=UmA[:].rearrange("p n v -> p v n"), axis=mybir.AxisListType.X, op=ALU.add), [bUm], [bt128[0]])
                        Ac(lambda e: e.activation(out=WpA[:, 0:64], in_=t128[0][:, 0:64], func=AF.Copy), [bt128[0]], [bWp])
                        Ac(lambda e: e.activation(out=WpB[:, 64:128], in_=t128[0][:, 64:128], func=AF.Copy), [bt128[0]], [bWp])
                    for hp, vv in ((0, vA), (1, vB)):
                        Pe(lambda e, psW=psW, hp=hp, vv=vv, st_=(samp and hp == 0): e.matmul(psW[:, 0:128], lhsT=KAm[hp][:, 0:128], rhs=vv[:], start=st_, stop=(hp == 1)),
                           [bKA[hp], bvv], [bpW])
                    if not samp:
                        Ac(lambda e, psW=psW: e.activation(out=WpA[:, 0:64], in_=psW[:, 0:64], func=AF.Copy), [bpW], [bWp])
                        Ac(lambda e, psW=psW: e.activation(out=WpB[:, 64:128], in_=psW[:, 64:128], func=AF.Copy), [bpW], [bWp])
                    else:
                        Dv(lambda e, psW=psW: e.tensor_tensor(out=t128[0][:], in0=psW[:, 0:128], in1=t128[0][:], op=ALU.add), [bpW, bt128[0], bWp], [bt128[0]])
                        Ac(lambda e: e.activation(out=WpA[:, 0:64], in_=t128[0][:, 0:64], func=AF.Copy), [bt128[0]], [bWp])
                        Ac(lambda e: e.activation(out=WpB[:, 64:128], in_=t128[0][:, 64:128], func=AF.Copy), [bt128[0]], [bWp])
                    psU, bpU = R_.ps_next()
                    for hp, ww in ((0, WpA), (1, WpB)):
                        Pe(lambda e, psU=psU, hp=hp, ww=ww: e.matmul(psU[:, 0:128], lhsT=Ttm[hp][:], rhs=ww[:], start=(hp == 0), stop=(hp == 1)), [bTt[hp], bWp], [bpU])
                    Ac(lambda e, psU=psU: e.activation(out=UpA[:, 0:64], in_=psU[:, 0:64], func=AF.Copy), [bpU], [bUp])
                    Ac(lambda e, psU=psU: e.activation(out=UpB[:, 64:128], in_=psU[:, 64:128], func=AF.Copy), [bpU], [bUp])
                    psY, bpY = R_.ps_next()
                    first = True
                    if not samp:
                        Pe(lambda e, psY=psY, c=c: e.matmul(psY[:, 0:128], lhsT=Hbb[:, c, :], rhs=ARt[:, c, 128:256], start=True, stop=False), [bHb, bAR], [bpY])
                        first = False
                    for hp, uu, vv in ((0, UpA, vA), (1, UpB, vB)):
                        Pe(lambda e, psY=psY, hp=hp, uu=uu, first=first: e.matmul(psY[:, 0:128], lhsT=uu[:], rhs=NBm[hp][:, 128:256], start=first, stop=False), [bUp, bNB[hp]], [bpY])
                        first = False
                        Pe(lambda e, psY=psY, hp=hp, vv=vv, sp_=(hp == 1 and not samp): e.matmul(psY[:, 0:128], lhsT=vv[:], rhs=KAm[hp][:, 128:256], start=False, stop=sp_), [bvv, bKA[hp]], [bpY])
                    if samp:
                        for sn in range(NS):
                            Pe(lambda e, psY=psY, sn=sn, c=c: e.matmul(psY[:, sn * 8:sn * 8 + 8], lhsT=Hsb[:, sn, :], rhs=ARt[:, c, 128 + sn * 8:128 + sn * 8 + 8],
                                                                      start=False, stop=(sn == NS - 1)), [bHsb, bAR], [bpY])
                    Ac(lambda e, psY=psY, c=c: e.activation(out=yfm[:, c, :], in_=psY[:, 0:128], func=AF.Copy), [bpY], [byfm])
                    if not samp:
                        psD, bpD = R_.ps_next()
                        Pe(lambda e, psD=psD: e.matmul(psD[:, 0:128], lhsT=btm_[:], rhs=UpA[:], start=True, stop=False), [bbtm_, bUp], [bpD])
                        Pe(lambda e, psD=psD: e.matmul(psD[:, 0:128], lhsT=btm_[:], rhs=UpB[:], start=False, stop=False), [bbtm_, bUp], [bpD])
                        Pe(lambda e, psD=psD: e.matmul(psD[:, 0:128], lhsT=ktm_[:], rhs=vA[:], start=False, stop=False), [bktm_, bvv], [bpD])
                        Pe(lambda e, psD=psD: e.matmul(psD[:, 0:128], lhsT=ktm_[:], rhs=vB[:], start=False, stop=True), [bktm_, bvv], [bpD])
                        Dv(lambda e, psD=psD: e.tensor_tensor(out=t128[1][:], in0=psD[:, 0:128], in1=blk64[:], op=ALU.mult), [bpD, bconst], [bt128[1]])
                        Dv(lambda e, c=c: e.tensor_tensor(out=t128[1][:], in0=t128[1][:], in1=Hblk[:, c, :], op=ALU.add), [bt128[1], bH], [bt128[1]])
                        Dv(lambda e, c=c: e.tensor_scalar(out=Hblk[:, c, :], in0=t128[1][:], scalar1=ec[:, c, 127:128], scalar2=None, op0=ALU.mult), [bt128[1], b_ec], [bH])
                        Ac(lambda e, c=c: e.activation(out=Hbb[:, c, :], in_=Hblk[:, c, :], func=AF.Copy), [bH], [bHb])
                        if t0 + 128 == cfg.tp:
                            R_.dma('sp', lambda e, c=c: e.dma_start(out=rwhp_d[:, l, c], in_=Hblk[:, c, :]), reads=[bH])
                    else:
                        for sn in range(NS):
                            Dv(lambda e, sn=sn: e.tensor_tensor(out=UmA[:, sn, :], in0=UpA[:], in1=bm16[:, sn, :], op=ALU.mult), [bUp, bconst, bUm], [bUm])
                            Dv(lambda e, sn=sn: e.tensor_tensor(out=VmA[:, sn, :], in0=UpB[:], in1=bm16[:, sn, :], op=ALU.mult), [bUp, bconst, bVm], [bVm])
                        Dv(lambda e: e.tensor_tensor(out=UmA[:], in0=UmA[:], in1=VmA[:], op=ALU.add), [bUm, bVm], [bUm])
                        for sn in range(NS):
                            Dv(lambda e, sn=sn: e.tensor_tensor(out=VmA[:, sn, :], in0=vA[:], in1=bm16[:, sn, :], op=ALU.mult), [bvv, bconst, bVm], [bVm])
                        for sn in range(NS):
                            Dv(lambda e, sn=sn: e.tensor_tensor(out=Hsb[:, sn, :], in0=vB[:], in1=bm16[:, sn, :], op=ALU.mult), [bvv, bconst, bHsb], [bHsb])
                        Dv(lambda e: e.tensor_tensor(out=VmA[:], in0=VmA[:], in1=Hsb[:], op=ALU.add), [bVm, bHsb], [bVm])
                        for g4 in range(4):
                            psD, bpD = R_.ps_next()
                            Pe(lambda e, psD=psD, g4=g4: e.matmul(psD[:, :], lhsT=btm_[:], rhs=UmA[:, 4 * g4:4 * g4 + 4, :].rearrange("p a b -> p (a b)"), start=True, stop=False),
                               [bbtm_, bUm], [bpD])
                            Pe(lambda e, psD=psD, g4=g4: e.matmul(psD[:, :], lhsT=ktm_[:], rhs=VmA[:, 4 * g4:4 * g4 + 4, :].rearrange("p a b -> p (a b)"), start=False, stop=True),
                               [bktm_, bVm], [bpD])
                            for j4 in range(4):
                                sn = 4 * g4 + j4
                                Dv(lambda e, psD=psD, j4=j4: e.tensor_tensor(out=t128[1][:], in0=psD[:, j4 * 128:(j4 + 1) * 128], in1=blk64[:], op=ALU.mult), [bpD, bconst], [bt128[1]])
                                Dv(lambda e, sn=sn: e.tensor_tensor(out=t128[1][:], in0=t128[1][:], in1=Hs[:, sn, :], op=ALU.add), [bt128[1], bHs], [bt128[1]])
                                Dv(lambda e, sn=sn, c=c: e.tensor_scalar(out=Hs[:, sn, :], in0=t128[1][:], scalar1=ec[:, c, sn * 8 + 7:sn * 8 + 8], scalar2=None, op0=ALU.mult),
                                   [bt128[1], b_ec], [bHs])
                        R_.dma('sp', lambda e, c=c: e.dma_start(out=rwhs_d[:, l, c], in_=Hs[:]), reads=[bHs])
                def n_body(c, S_, R_):
                    t128, bt128, mixo, bmixo = S_['t128'], S_['bt128'], S_['mixr'], S_['bmixr']
                    Dv = lambda fn, r, w: R_.op('dve', fn, reads=r, writes=w)
                    Ac = lambda fn, r, w: R_.op('act', fn, reads=r, writes=w)
                    Pe = lambda fn, r, w: R_.op('pe', fn, reads=r, writes=w)
                    byfm = byfmc[c]
                    psm, bpm = R_.ps_next()
                    Pe(lambda e, psm=psm, c=c: e.matmul(psm[:, 0:128], lhsT=blk64[:], rhs=yfm[:, c, :], start=True, stop=True), [bconst, byfm], [bpm])
                    Dv(lambda e, psm=psm, c=c: e.scalar_tensor_tensor(out=t128[0][:], in0=psm[:, 0:128], scalar=-1.0 / 64, in1=yfm[:, c, :], op0=ALU.mult, op1=ALU.add),
                       [bpm, byfm], [bt128[0]])
                    Dv(lambda e: e.tensor_tensor(out=t128[1][:], in0=t128[0][:], in1=t128[0][:], op=ALU.mult), [bt128[0]], [bt128[1]])
                    psv, bpv = R_.ps_next()
                    Pe(lambda e, psv=psv: e.matmul(psv[:, 0:128], lhsT=blk64[:], rhs=t128[1][:], start=True, stop=True), [bconst, bt128[1]], [bpv])
                    Dv(lambda e, psv=psv: e.tensor_scalar(out=t128[1][:], in0=psv[:, 0:128], scalar1=1.0 / 64, scalar2=64e-5, op0=ALU.mult, op1=ALU.add), [bpv], [bt128[1]])
                    Ac(lambda e: e.activation(out=t128[1][:], in_=t128[1][:], func=AF.Sqrt), [bt128[1]], [bt128[1]])
                    Dv(lambda e: e.reciprocal(out=t128[1][:], in_=t128[1][:]), [bt128[1]], [bt128[1]])
                    Dv(lambda e, c=c: e.scalar_tensor_tensor(out=t128[0][:], in0=t128[0][:], scalar=ln_w[:, c:c + 1], in1=t128[1][:], op0=ALU.mult, op1=ALU.mult),
                       [bt128[0], bt128[1], brwp], [bt128[0]])
                    Dv(lambda e, c=c: e.tensor_scalar(out=t128[0][:], in0=t128[0][:], scalar1=ln_b[:, c:c + 1], scalar2=None, op0=ALU.add), [bt128[0], brwp], [bt128[0]])
                    Dv(lambda e, c=c: e.scalar_tensor_tensor(out=t128[1][:], in0=zp[:, c, :], scalar=r_k[:, c:c + 1], in1=kp[:, c, :], op0=ALU.mult, op1=ALU.mult),
                       [bzp, b_kp, brwp, bt128[1]], [bt128[1]])
                    psb_, bpb = R_.ps_next()
                    Pe(lambda e, psb_=psb_: e.matmul(psb_[:, 0:128], lhsT=blk64[:], rhs=t128[1][:], start=True, stop=True), [bconst, bt128[1]], [bpb])
                    Dv(lambda e, psb_=psb_, c=c: e.tensor_tensor(out=t128[1][:], in0=psb_[:, 0:128], in1=zp[:, 12 + c, :], op=ALU.mult), [bpb, bzp, bt128[1]], [bt128[1]])
                    Dv(lambda e: e.tensor_tensor(out=t128[0][:], in0=t128[0][:], in1=t128[1][:], op=ALU.add), [bt128[0], bt128[1]], [bt128[0]])
                    Dv(lambda e, c=c: e.tensor_tensor(out=mixo[:, 0:128], in0=t128[0][:], in1=gg[:, c, :], op=ALU.mult), [bt128[0], b_gg], [bmixo])
                    R_.dma('sp', lambda e, c=c, t0=t0: e.dma_start(out=msc[512 + 128 * c:512 + 128 * c + 128, t0:t0 + 128], in_=mixo[:, 0:128]),
                          reads=[bmixo], writes=[bmsc_all[bi]])


                if samp:
                    for c in range(6):
                        r_ = Rec(list(range(8)))
                        c_body(c, RWS[0], r_)
                        n_body(c, RWS[0], r_)
                        merge([r_])
                else:
                    for c in range(0, 6, 2):
                        ra_, rb_ = Rec([0, 1, 2, 3]), Rec([4, 5, 6, 7])
                        c_body(c, RWS[0], ra_)
                        n_body(c, RWS[0], ra_)
                        c_body(c + 1, RWS[1], rb_)
                        n_body(c + 1, RWS[1], rb_)
                        merge([ra_, rb_])

        def zero_mix(l, c0, c1):
            P.op('dve', lambda e: e.memset(mixo[:], 0.0), writes=[bmixo])
            for pi, (t0, n) in enumerate(pieces):
                for c in range(c0, c1, 128):
                    P.dma('sp', lambda e, c=c, t0=t0, n=n: e.dma_start(out=msc[c:c + 128, t0:t0 + n], in_=mixo[:, :n]),
                          reads=[bmixo], writes=[bmsc_all[pi]])

        def mixers(l):
            P.barrier()
            if cfg.only is not None:
                zero_mix(l, 0, 2048)
            if cfg.only in (None, 's5'):
                s5(l)
            P.barrier()
            if cfg.only in (None, 'hgrn'):
                hgrn(l)
            P.barrier()
            if cfg.only in (None, 'rwkv'):
                rwkv(l)
            P.barrier()

        nblk = len(cfg.blocks)
        if cfg.mode == 'mix':
            mixers(cfg.mixl)
            nblk = 0
            wst['used'] = len(WSEQ)
        for bi in range(nblk):
            load_x0(bi)
            seg_front(0, bi)
        if cfg.mode != 'mix':
            mixers(0)
        for l in range(1, L if cfg.mode != 'mix' else 1):
            for bi in range(nblk):
                seg_back(l - 1, bi)
                seg_front(l, bi)
            mixers(l)
        for bi in range(nblk):
            seg_back(L - 1, bi)
            final(bi)
        assert wst['used'] == len(WSEQ)
        P.run(es)
    return nc


def host_consts():
    s = np.arange(128)
    m64 = ((s[:, None] // 64 == s[None, :] // 64) & (s[:, None] <= s[None, :])).astype(np.float32)
    m8 = ((s[:, None] // 8 == s[None, :] // 8) & (s[:, None] <= s[None, :])).astype(np.float32)
    bm16 = np.zeros((128, NS, 128), np.float32)
    for n in range(NS):
        bm16[n * 8:(n + 1) * 8, n, :] = 1
    t = np.arange(512)
    cm = np.ones((128, 3, 512), np.float32)
    cm[:, 0, t % 64 == 0] = 0
    cm[:, 1, t % 8 == 0] = 0
    cm[:, 2, t % 128 == 0] = 0
    iota = np.broadcast_to(np.arange(128, dtype=np.float32)[None, :], (128, 128)).copy()
    rwm = np.zeros((128, 2, 384), np.float32)
    for mi, blk in ((0, 128), (1, 8)):
        same = (s[:, None] // blk == s[None, :] // blk)
        rwm[:, mi, 0:128] = same & (s[:, None] < s[None, :])
        rwm[:, mi, 128:256] = same & (s[:, None] <= s[None, :])
        rwm[:, mi, 256:384] = same & (s[:, None] > s[None, :])
    blk64 = (s[:, None] // 64 == s[None, :] // 64).astype(np.float32)
    i6 = np.arange(768)
    cm6 = np.ones((128, 2, 768), np.float32)
    cm6[:, 0, i6 % 128 == 0] = 0
    cm6[:, 1, i6 % 8 == 0] = 0
    rmask2 = np.zeros((128, 2), np.float32)
    pp = np.arange(128)
    rmask2[:, 0] = ((pp % 64) // 32 == 0)
    rmask2[:, 1] = ((pp % 64) // 32 == 1)
    import ml_dtypes
    bf = ml_dtypes.bfloat16
    return dict(ident=np.eye(128, dtype=np.float32), m64=m64, m8=m8, bm16=bm16.astype(bf), cm=cm.astype(bf), iota=iota, rmask2=rmask2, rwm=rwm.astype(bf), blk64=blk64, cm6=cm6.astype(bf))


def prep_hgrn(lb_raw, norm_w):
    L = lb_raw.shape[0]
    return dict(lbraw=np.ascontiguousarray(lb_raw.reshape(L, 6, 128).transpose(2, 1, 0)),
                hnw=np.ascontiguousarray(norm_w.reshape(L, 6, 128).transpose(2, 0, 1)))


def prep_s5(a_re, a_im, log_dt, b_re, b_im, c_re, c_im, d, w_glu, b_glu):
    L = a_re.shape[0]
    f32 = np.float32
    ldt_full = np.repeat(log_dt, 64, axis=1)
    fm = np.stack([a_re.reshape(L, 2048), a_im.reshape(L, 2048), ldt_full], axis=1)
    s5fm = np.ascontiguousarray(fm.reshape(L, 3, 16, 128).transpose(3, 0, 1, 2)).astype(f32)
    p = np.arange(128)
    row = np.zeros((128, L, 3, 4, 128), f32)
    for jq in range(4):
        j = 4 * jq + p // 32
        idx = j[:, None] * 128 + np.arange(128)[None, :]
        for i in range(3):
            row[:, :, i, jq, :] = fm[:, i, :][:, idx].transpose(1, 0, 2)
    s5row = row.reshape(128, L, 12, 128)
    s5b = np.zeros((128, L, 2, 4, 128), f32)
    m = np.arange(128)
    for c, bb in enumerate((b_re, b_im)):
        for jq in range(4):
            for pp_ in range(128):
                j = 4 * jq + pp_ // 32
                gl = (pp_ % 32) // 16
                h = pp_ % 16
                s5b[pp_, :, c, jq, gl * 64:(gl + 1) * 64] = bb[:, 2 * j + gl, :, h]
    s5c = np.zeros((128, L, 2, 16, 128), f32)
    for c, cc in enumerate((c_re, c_im)):
        for j in range(16):
            for gl in range(2):
                m0 = 32 * (j % 4) + 16 * gl
                s5c[gl * 64:(gl + 1) * 64, :, c, j, m0:m0 + 16] = cc[:, 2 * j + gl, :, :].transpose(2, 0, 1)
    s5db = np.ascontiguousarray(np.stack([d, b_glu], axis=1).reshape(L, 2, 4, 128).transpose(3, 0, 1, 2)).astype(f32)
    s5w = np.ascontiguousarray(w_glu.reshape(L, 4, 128, 512).transpose(2, 0, 1, 3)).astype(f32)
    return dict(s5fm=s5fm, s5row=np.ascontiguousarray(s5row), s5b=s5b, s5c=s5c, s5db=s5db, s5w=s5w)


def fm_state_s5(re, im):
    L, N = re.shape[:2]
    x = np.stack([re.reshape(L, N, 16, 128), im.reshape(L, N, 16, 128)], axis=1)
    return np.ascontiguousarray(x.transpose(4, 0, 1, 3, 2)).astype(np.float32)


def unfm_state_s5(x):
    L, N = x.shape[1], x.shape[4]
    y = x.transpose(1, 2, 4, 3, 0).reshape(L, 2, N, 32, 64)
    return np.ascontiguousarray(y[:, 0]), np.ascontiguousarray(y[:, 1])


def prep_rwkv(mu, w0, w2, a0, a2, g2, k_k, k_a, r_k, ln_w, ln_b):
    L = mu.shape[0]
    f32 = np.float32
    cols = [mu.reshape(L, 20, 128)] + [x.reshape(L, 6, 128) for x in (w0, a0, k_k, k_a, r_k.reshape(L, 768), ln_w, ln_b)]
    rwp = np.ascontiguousarray(np.concatenate(cols, axis=1).transpose(2, 0, 1)).astype(f32)
    rwl = np.zeros((128, L, 2, 768), f32)
    rwl[0:64, :, 0, :] = w2.transpose(1, 0, 2)
    rwl[64:128, :, 0, :] = a2.transpose(1, 0, 2)
    rwl[:, :, 1, :] = g2.transpose(1, 0, 2)
    return dict(rwp=rwp, rwl=rwl)


def fm_shift(sh):
    L, N = sh.shape[:2]
    return np.ascontiguousarray(sh.reshape(L, N, 20, 128).transpose(3, 0, 2, 1)).astype(np.float32)


def unfm_shift(x):
    L, N = x.shape[1], x.shape[3]
    return np.ascontiguousarray(x.transpose(1, 3, 2, 0).reshape(L, N, 2560))


def fm_wkv(wkv):
    L, N = wkv.shape[:2]
    out = np.zeros((128, L, 6, N, 128), np.float32)
    w = wkv.reshape(L, N, 6, 2, 64, 64)
    for hp in range(2):
        out[hp * 64:(hp + 1) * 64, :, :, :, hp * 64:(hp + 1) * 64] = w[:, :, :, hp].transpose(4, 0, 2, 1, 3)
    return out


def unfm_wkv(x):
    L, N = x.shape[1], x.shape[3]
    out = np.zeros((L, N, 6, 2, 64, 64), np.float32)
    for hp in range(2):
        blk = x[hp * 64:(hp + 1) * 64, :, :, :, hp * 64:(hp + 1) * 64]
        out[:, :, :, hp] = blk.transpose(1, 3, 2, 4, 0)
    return out.reshape(L, N, 12, 64, 64)


def tile_w(w):
    L, K, N = w.shape
    return np.ascontiguousarray(w.reshape(L, K // 128, 128, N // 512, 512).transpose(0, 3, 2, 1, 4))


_WNAMES = (("wg1", "ffn1_w_gate"), ("wu1", "ffn1_w_up"), ("wd1", "ffn1_w_down"), ("win", "w_in"), ("wout", "w_out"),
           ("wg2", "ffn2_w_gate"), ("wu2", "ffn2_w_up"), ("wd2", "ffn2_w_down"))


def make_inmaps(inp, cfg, ncores, nprompt):
    f32 = np.float32
    L = cfg.depth
    shared = host_consts()
    for k, nm in _WNAMES:
        shared[k] = tile_w(np.asarray(inp[nm], f32)[:L])
    norms = np.concatenate([np.stack([inp["norm_ffn1"][l], inp["norm_mix"][l], inp["norm_ffn2"][l]]) for l in range(L)] + [inp["norm_final"][None]], axis=0)
    shared["nrm"] = np.ascontiguousarray(np.asarray(norms, f32).reshape(3 * L + 1, KT, 128).transpose(2, 0, 1))
    shared.update(prep_hgrn(np.asarray(inp["hgrn_lb_raw"], f32)[:L], np.asarray(inp["hgrn_norm_w"], f32)[:L]))
    shared.update(prep_s5(*(np.asarray(inp[k], f32)[:L] for k in ("s5_a_re", "s5_a_im", "s5_log_dt", "s5_b_re", "s5_b_im", "s5_c_re", "s5_c_im",
                                                                   "s5_d", "s5_w_glu", "s5_b_glu"))))
    shared.update(prep_rwkv(*(np.asarray(inp[k], f32)[:L] for k in ("rwkv_mu", "rwkv_w0", "rwkv_w2", "rwkv_a0", "rwkv_a2", "rwkv_g2", "rwkv_k_k",
                                                                     "rwkv_k_a", "rwkv_r_k", "rwkv_ln_w", "rwkv_ln_b"))))
    maps = []
    for c in range(ncores):
        m = dict(shared)
        sl = slice(NS * c, NS * (c + 1))
        xp = np.asarray(inp["x_prompt"][c % nprompt], f32)
        xs = np.asarray(inp["x_sample"][sl], f32).reshape(NS * TS, D)
        m["xT"] = np.ascontiguousarray(np.concatenate([xp, xs], axis=0).T)
        m["hst"] = np.ascontiguousarray(np.asarray(inp["state_hgrn"], f32)[:L, sl])
        m["s5x0"] = fm_state_s5(np.asarray(inp["state_s5_re"], f32)[:L, sl], np.asarray(inp["state_s5_im"], f32)[:L, sl])
        m["rwsh0"] = fm_shift(np.asarray(inp["state_rwkv_shift"], f32)[:L, sl])
        m["rwh0"] = fm_wkv(np.asarray(inp["state_rwkv_wkv"], f32)[:L, sl])
        maps.append(m)
    return maps


def gather(results, cfg, ncores, nprompt):
    L = cfg.depth
    TP = cfg.tp
    f32 = np.float32
    y_p = np.stack([results[c]["yT"][:, :TP].T for c in range(nprompt)]).astype(f32)
    y_s = np.concatenate([results[c]["yT"][:, TP:].T.reshape(NS, TS, D) for c in range(ncores)]).astype(f32)
    s5p = [unfm_state_s5(results[c]["s5p"][..., None]) for c in range(nprompt)]
    s5s = [unfm_state_s5(results[c]["s5s"]) for c in range(ncores)]
    s5re_p = np.concatenate([a[0] for a in s5p], axis=1)
    s5im_p = np.concatenate([a[1] for a in s5p], axis=1)
    s5re_s = np.concatenate([a[0] for a in s5s], axis=1)
    s5im_s = np.concatenate([a[1] for a in s5s], axis=1)
    sh_p = np.concatenate([unfm_shift(results[c]["rwshp"][..., None]) for c in range(nprompt)], axis=1)
    sh_s = np.concatenate([unfm_shift(results[c]["rwshs"]) for c in range(ncores)], axis=1)
    wkv_p = np.concatenate([unfm_wkv(results[c]["rwhp"][:, :, :, None, :]) for c in range(nprompt)], axis=1)
    wkv_s = np.concatenate([unfm_wkv(results[c]["rwhs"]) for c in range(ncores)], axis=1)
    hg_p = np.stack([results[c]["hgp"] for c in range(nprompt)], axis=1).astype(f32)
    hg_s = np.concatenate([results[c]["hgs"] for c in range(ncores)], axis=1).astype(f32)
    outs = (y_p, y_s, s5re_p, s5im_p, sh_p, wkv_p, hg_p, s5re_s, s5im_s, sh_s, wkv_s, hg_s)
    return tuple(np.ascontiguousarray(o, dtype=f32) for o in outs)


def kernel(**inp):
    cfg = Cfg(depth=4, tp=2048)
    nc = build(cfg)
    maps = make_inmaps(inp, cfg, 8, 4)
    res = run_bass_kernel_spmd(nc, maps, core_ids=list(range(8)))
    return gather(res.results, cfg, 8, 4)
```

```python
import os
import numpy as np
import concourse.bass as bass
import concourse.mybir as mybir
from concourse.bass_utils import run_bass_kernel_spmd
from contextlib import ExitStack

F32 = mybir.dt.float32
BF16 = mybir.dt.bfloat16
I32 = mybir.dt.int32
AF = mybir.ActivationFunctionType
ALU = mybir.AluOpType

D = 2048
DFF = 5632
NIN = 6144
KT = D // 128
FT = DFF // 128
NS = 16
TS = 8
NORM_EPS = 1e-6

EPOCH = 8192
COMPUTE = ('pe', 'act', 'dve', 'pool')
NDMASEM = 24
NSWSEM = 8


class Buf:
    __slots__ = ('name', 'w', 'r')

    def __init__(self, name=''):
        self.name = name
        self.w = None
        self.r = {}


class Prog:
    def __init__(self, nc):
        self.nc = nc
        self.q = {e: [] for e in ('pe', 'act', 'dve', 'pool', 'sp')}
        self.src_n = {}
        self.signalled = {}
        self.seen = {e: {} for e in self.q}
        self.dma_rr = 0
        self.dma_rr2 = 0

    def _new_event(self, src):
        i = self.src_n.get(src, 0)
        self.src_n[src] = i + 1
        return i

    def _deps(self, eng, src, reads, writes):
        deps = {}

        def add(d, raw):
            if d is None:
                return
            s, i = d
            if s == src and src == 'pe':
                return
            if deps.get(s, -1) < i:
                deps[s] = i
        for b in reads:
            add(b.w, True)
        for b in writes:
            add(b.w, False)
            for s, i in b.r.items():
                if (s != src or src != 'pe') and deps.get(s, -1) < i:
                    deps[s] = i
        out = []
        seen = self.seen[eng]
        for s, i in deps.items():
            if seen.get(s, -1) >= i:
                continue
            seen[s] = i
            self.signalled[(s, i)] = True
            out.append((s, i))
        return out

    def op(self, eng, fn, reads=(), writes=()):
        waits = self._deps(eng, eng, reads, writes)
        idx = self._new_event(eng)
        for b in reads:
            b.r[eng] = idx
        for b in writes:
            b.w = (eng, idx)
            b.r = {}
        self.q[eng].append(('op', waits, fn, (eng, idx)))

    def dma(self, eng, fn, reads=(), writes=()):
        if eng == 'pool':
            src = 'dma%d' % (NDMASEM + self.dma_rr2)
            self.dma_rr2 = (self.dma_rr2 + 1) % NSWSEM
        else:
            src = 'dma%d' % self.dma_rr
            self.dma_rr = (self.dma_rr + 1) % NDMASEM
        waits = self._deps(eng, src, reads, writes)
        idx = self._new_event(src)
        if idx > 0 and self.seen[eng].get(src, -1) < idx - 1:
            self.seen[eng][src] = idx - 1
            self.signalled[(src, idx - 1)] = True
            waits.append((src, idx - 1))
        for b in reads:
            b.r[src] = idx
        for b in writes:
            b.w = (src, idx)
            b.r = {}
        self.q[eng].append(('dma', waits, fn, (src, idx)))

    def barrier(self):
        for eng in self.q:
            waits = []
            seen = self.seen[eng]
            for src, n in self.src_n.items():
                if src == eng or n == 0:
                    continue
                i = n - 1
                if seen.get(src, -1) >= i:
                    continue
                seen[src] = i
                self.signalled[(src, i)] = True
                waits.append((src, i))
            self.q[eng].append(('bar', waits, None, None))

    def run(self, es):
        nc = self.nc
        sems = {}
        val = {}
        for e in COMPUTE:
            n = self.src_n.get(e, 0)
            c = 0
            ep = 0
            for i in range(n):
                if self.signalled.get((e, i)):
                    if c == EPOCH:
                        ep += 1
                        c = 0
                    c += 1
                    val[(e, i)] = (e, ep, c)
                    if (e, ep) not in sems:
                        sems[(e, ep)] = es.enter_context(nc.semaphore('s_%s_%d' % (e, ep)))
        for k in range(NDMASEM + NSWSEM):
            s = 'dma%d' % k
            if self.src_n.get(s, 0):
                sems[(s, 0)] = es.enter_context(nc.semaphore('s_' + s))
        final_dma = {('dma%d' % k): 16 * self.src_n.get('dma%d' % k, 0) for k in range(NDMASEM + NSWSEM)}

        def waitspec(s, i):
            if s.startswith('dma'):
                return sems[(s, 0)], 16 * (i + 1)
            _, ep, c = val[(s, i)]
            return sems[(s, ep)], c

        block = es.enter_context(nc.Block())
        prog = self

        def replay(engname):
            def body(eng):
                for kind, waits, fn, ev in prog.q[engname]:
                    for (s, i) in waits:
                        sem, v = waitspec(s, i)
                        eng.wait_ge(sem, v)
                    if fn is None:
                        continue
                    ins = fn(eng)
                    if kind == 'dma':
                        ins.then_inc(sems[(ev[0], 0)], 16)
                    elif ev in val:
                        _, ep, c = val[ev]
                        ins.then_inc(sems[(ev[0], ep)], 1)
                if engname == 'sp':
                    for s, v in final_dma.items():
                        if v:
                            eng.wait_ge(sems[(s, 0)], v)
            return body

        block.tensor(replay('pe'))
        block.scalar(replay('act'))
        block.vector(replay('dve'))
        block.gpsimd(replay('pool'))
        block.sync(replay('sp'))


class Cfg:
    def __init__(self, depth=4, tp=2048, tb=512, mixers=True, debug=False, mode='full'):
        self.mode = mode
        self.mixl = 1
        self.stop = 99
        self.only = None
        self.depth = depth
        self.debug = debug
        self.tp = tp
        self.tb = tb
        self.tok = tp + NS * TS
        self.mixers = mixers
        self.blocks = [(i * tb, tb) for i in range(tp // tb)] + [(tp, NS * TS)]


NB = 5


def build(cfg):
    nc = bass.Bass("TRN2", target_bir_lowering=False)
    L = cfg.depth
    TOK = cfg.tok
    TB = cfg.tb

    def din(name, shape, dt=F32):
        return nc.dram_tensor(name, list(shape), dt, kind="ExternalInput").ap()

    def dout(name, shape, dt=F32):
        return nc.dram_tensor(name, list(shape), dt, kind="ExternalOutput").ap()

    def dscr(name, shape, dt=F32):
        return nc.dram_tensor(name, list(shape), dt, kind="Internal").ap()

    xT = din("xT", [D, TOK])
    wts = {}
    for nm, ncb, nkt in (("wg1", 11, 16), ("wu1", 11, 16), ("wd1", 4, 44), ("win", 12, 16),
                         ("wout", 4, 16), ("wg2", 11, 16), ("wu2", 11, 16), ("wd2", 4, 44)):
        wts[nm] = din(nm, [L if cfg.mode != 'mix' else 1, ncb if cfg.mode != 'mix' else 1, 128, nkt, 512])
    nrm = din("nrm", [128, 3 * L + 1, KT])
    yT = dout("yT", [D, TOK])
    xsc = dscr("xsc", [D, TOK])
    psc = (dout if cfg.debug else dscr)("psc", [NIN, TOK])
    msc = dscr("msc", [D, TOK], BF16)
    if cfg.debug:
        dbg_h = dout("dbg_h", [D, TOK], BF16)
        dbg_x = dout("dbg_x", [D, TOK])
    if cfg.mode == 'mix':
        psc = din("psc_in", [NIN, TOK])
        msc = dout("msc_out", [D, TOK], BF16)
    ident_d = din("ident", [128, 128])
    m64_d = din("m64", [128, 128])
    m8_d = din("m8", [128, 128])
    bm16_d = din("bm16", [128, NS, 128], BF16)
    cm_d = din("cm", [128, 3, 512], BF16)
    lbraw_d = din("lbraw", [128, 6, L])
    hnw_d = din("hnw", [128, L, 6])
    hst_d = din("hst", [L, NS, 6, 128, 128])
    s5fm_d = din("s5fm", [128, L, 3, 16])
    s5row_d = din("s5row", [128, L, 12, 128])
    s5b_d = din("s5b", [128, L, 2, 4, 128])
    s5c_d = din("s5c", [128, L, 2, 16, 128])
    s5db_d = din("s5db", [128, L, 2, 4])
    s5w_d = din("s5w", [128, L, 4, 512])
    s5x0_d = din("s5x0", [128, L, 2, 16, NS])
    s5p_d = dout("s5p", [128, L, 2, 16])
    s5s_d = dout("s5s", [128, L, 2, 16, NS])
    iota_d = din("iota", [128, 128])
    rwp_d = din("rwp", [128, L, 62])
    rwl_d = din("rwl", [128, L, 2, 768])
    rwsh0_d = din("rwsh0", [128, L, 20, NS])
    rwshp_d = dout("rwshp", [128, L, 20])
    rwshs_d = dout("rwshs", [128, L, 20, NS])
    rwh0_d = din("rwh0", [128, L, 6, NS, 128])
    rwhp_d = dout("rwhp", [128, L, 6, 128])
    rwhs_d = dout("rwhs", [128, L, 6, NS, 128])
    rwm_d = din("rwm", [128, 2, 384], BF16)
    blk64_d = din("blk64", [128, 128])
    cm6_d = din("cm6", [128, 2, 768], BF16)
    rmask2_d = din("rmask2", [128, 2])
    hgp_d = dout("hgp", [L, 6, 128, 128])
    hgs_d = dout("hgs", [L, NS, 6, 128, 128])
    bxsc = [Buf() for _ in cfg.blocks]
    bpsc = [Buf() for _ in cfg.blocks]
    bmsc = [Buf() for _ in cfg.blocks]

    es = ExitStack()
    with es:
        P = Prog(nc)

        def sb(name, shape, dt=F32):
            return es.enter_context(nc.sbuf_tensor(name, list(shape), dt))

        AW = KT * TB + KT * TB // 2 + FT * TB // 2 + 7 * TB
        arena = sb("arena", [128, AW])
        ast = {'p': 0}

        def a_reset():
            ast['p'] = 0

        def a_alloc(shape, dt=F32):
            n = 1
            for d_ in shape[1:]:
                n *= d_
            words = n if dt == F32 else (n + 1) // 2
            p0 = ast['p']
            ast['p'] = p0 + words
            assert ast['p'] <= AW, ("arena overflow", ast['p'], AW)
            v = arena[:, p0:p0 + words]
            if dt != F32:
                v = v.bitcast(dt)
                if dt == I32:
                    pass
            if len(shape) == 3:
                v = v.rearrange("p (a b) -> p a b", a=shape[1])
            return v

        xs = a_alloc([128, KT, TB])
        bx = [Buf('x%d' % i) for i in range(KT)]
        hb = a_alloc([128, KT, TB], BF16)
        bh = Buf('h')
        act = a_alloc([128, FT, TB], BF16)
        bact = [Buf('act%d' % i) for i in range(FT)]
        sq = [a_alloc([128, TB]) for i in range(2)]
        bsq = [Buf() for _ in range(2)]
        rstd = a_alloc([128, TB])
        brstd = Buf('rstd')
        sg = [a_alloc([128, TB]) for i in range(2)]
        bsg = [Buf() for _ in range(2)]
        stg = [a_alloc([128, TB]) for i in range(2)]
        bstg = [Buf() for _ in range(2)]
        ones = sb("ones", [128, 128])
        bones = Buf('ones')
        nrm_t = sb("nrm_t", [128, 3 * L + 1, KT])
        bnrm = Buf('nrm')
        wring = [sb("wr%d" % i, [128, 8, 512], BF16) for i in range(NB)]
        bwr = [Buf('wr%d' % i) for i in range(NB)]
        psall = es.enter_context(nc.psum_tensor("psall", [128, 8, 512], F32))
        psb = [psall[:, i, :] for i in range(8)]
        bps = [Buf('ps%d' % i) for i in range(8)]
        st = {'ps': 0, 'sq': 0, 'sg': 0, 'stg': 0}

        def ps_next():
            i = st['ps']
            st['ps'] = (i + 1) % 8
            return psb[i], bps[i]

        def rr(key, n):
            i = st[key]
            st[key] = (i + 1) % n
            return i


        class Rec:
            def __init__(self, banks):
                self.ops = []
                self.banks = banks
                self.i = 0

            def op(self, eng, fn, reads=(), writes=()):
                self.ops.append(('op', eng, fn, list(reads), list(writes)))

            def dma(self, eng, fn, reads=(), writes=()):
                self.ops.append(('dma', eng, fn, list(reads), list(writes)))

            def ps_next(self):
                b = self.banks[self.i % len(self.banks)]
                self.i += 1
                return psb[b], bps[b]

        def merge(recs, target=None):
            tgt = target if target is not None else P
            idx = [0] * len(recs)
            live = True
            while live:
                live = False
                for i, r in enumerate(recs):
                    if idx[i] < len(r.ops):
                        kind, eng, fn, rd, wr = r.ops[idx[i]]
                        idx[i] += 1
                        (tgt.op if kind == 'op' else tgt.dma)(eng, fn, reads=rd, writes=wr)
                        live = True

        def wseq():
            seq = []

            def ffn(l, g, u, d):
                for cb in range(11):
                    for nm in (g, u):
                        for kb in range(2):
                            seq.append((nm, l, cb, kb * 8, 8))
                for cb in range(4):
                    for rb in range(6):
                        seq.append((d, l, cb, rb * 8, 8 if rb < 5 else 4))

            def front(l):
                ffn(l, "wg1", "wu1", "wd1")
                for cb in range(12):
                    for kb in range(2):
                        seq.append(("win", l, cb, kb * 8, 8))

            def back(l):
                for cb in range(4):
                    for kb in range(2):
                        seq.append(("wout", l, cb, kb * 8, 8))
                ffn(l, "wg2", "wu2", "wd2")

            for _ in cfg.blocks:
                front(0)
            for l in range(1, L):
                for _ in cfg.blocks:
                    back(l - 1)
                    front(l)
            for _ in cfg.blocks:
                back(L - 1)
            return seq

        WSEQ = wseq()
        wst = {'issued': 0, 'used': 0}

        def w_issue():
            j = wst['issued']
            nm, l, cb, k0, n = WSEQ[j]
            slot = j % NB
            src = wts[nm][l, cb, :, k0:k0 + n, :]
            P.dma('pool', lambda e, slot=slot, src=src, n=n: e.dma_start(out=wring[slot][:, 0:n, :], in_=src),
                  writes=[bwr[slot]])
            wst['issued'] = j + 1

        def w_get(expect):
            j = wst['used']
            assert WSEQ[j] == expect, (WSEQ[j], expect)
            while wst['issued'] < min(len(WSEQ), j + NB - 1):
                w_issue()
            wst['used'] = j + 1
            slot = j % NB
            return wring[slot], bwr[slot]

        P.op('dve', lambda e: e.memset(ones[:], 1.0), writes=[bones])
        P.dma('sp', lambda e: e.dma_start(out=nrm_t[:], in_=nrm), writes=[bnrm])

        def rmsnorm(n, widx, out_fn):
            ps, bp = ps_next()
            for kt in range(KT):
                i = rr('sq', 2)
                P.op('act', lambda e, kt=kt, i=i: e.activation(out=sq[i][:, :n], in_=xs[:, kt, :n], func=AF.Square),
                     reads=[bx[kt]], writes=[bsq[i]])
                P.op('pe', lambda e, kt=kt, i=i, ps=ps: e.matmul(ps[:, :n], lhsT=ones[:], rhs=sq[i][:, :n],
                                                                   start=(kt == 0), stop=(kt == KT - 1)),
                     reads=[bones, bsq[i]], writes=[bp])
            P.op('dve', lambda e, ps=ps: e.tensor_scalar(out=rstd[:, :n], in0=ps[:, :n], scalar1=1.0 / D, scalar2=NORM_EPS,
                                                          op0=ALU.mult, op1=ALU.add), reads=[bp], writes=[brstd])
            P.op('act', lambda e: e.activation(out=rstd[:, :n], in_=rstd[:, :n], func=AF.Sqrt), reads=[brstd], writes=[brstd])
            P.op('dve', lambda e: e.reciprocal(out=rstd[:, :n], in_=rstd[:, :n]), reads=[brstd], writes=[brstd])
            for kt in range(KT):
                o, wb = out_fn(kt)
                P.op('dve', lambda e, kt=kt, o=o: e.scalar_tensor_tensor(out=o, in0=xs[:, kt, :n],
                                                                         scalar=nrm_t[:, widx, kt:kt + 1],
                                                                         in1=rstd[:, :n], op0=ALU.mult, op1=ALU.mult),
                     reads=[bx[kt], bnrm, brstd], writes=wb)

        def norm_to_h(n, widx):
            rmsnorm(n, widx, lambda kt: (hb[:, kt, :n], [bh]))

        def proj16(n, l, nm, cb, rhs_t, rhs_b):
            w0, b0 = w_get((nm, l, cb, 0, 8))
            w1, b1 = w_get((nm, l, cb, 8, 8))
            outs = []
            for ft in range(4):
                ps, bp = ps_next()
                for kt in range(KT):
                    w, bw_ = (w0, b0) if kt < 8 else (w1, b1)
                    P.op('pe', lambda e, ps=ps, w=w, kt=kt, ft=ft: e.matmul(
                        ps[:, :n], lhsT=w[:, kt % 8, ft * 128:(ft + 1) * 128], rhs=rhs_t[:, kt, :n],
                        start=(kt == 0), stop=(kt == KT - 1)), reads=[bw_, rhs_b], writes=[bp])
                outs.append((ps, bp))
            return outs

        def ffn(n, l, g, u, d):
            for cb in range(11):
                go = proj16(n, l, g, cb, hb, bh)
                uo = proj16(n, l, u, cb, hb, bh)
                for ft in range(4):
                    f = cb * 4 + ft
                    i = rr('sg', 2)
                    gps, gb = go[ft]
                    ups, ub = uo[ft]
                    P.op('act', lambda e, i=i, gps=gps: e.activation(out=sg[i][:, :n], in_=gps[:, :n], func=AF.Silu),
                         reads=[gb], writes=[bsg[i]])
                    P.op('dve', lambda e, i=i, ups=ups, f=f: e.tensor_tensor(out=act[:, f, :n], in0=sg[i][:, :n],
                                                                            in1=ups[:, :n], op=ALU.mult),
                         reads=[bsg[i], ub], writes=[bact[f]])
            for cb in range(4):
                pss = [ps_next() for _ in range(4)]
                for rb in range(6):
                    nk = 8 if rb < 5 else 4
                    w, bw_ = w_get((d, l, cb, rb * 8, nk))
                    for dt in range(4):
                        ps, bp = pss[dt]
                        for k in range(nk):
                            f = rb * 8 + k
                            P.op('pe', lambda e, ps=ps, w=w, k=k, dt=dt, f=f: e.matmul(
                                ps[:, :n], lhsT=w[:, k, dt * 128:(dt + 1) * 128], rhs=act[:, f, :n],
                                start=(f == 0), stop=(f == FT - 1)), reads=[bw_, bact[f]], writes=[bp])
                for dt in range(4):
                    ps, bp = pss[dt]
                    kt = cb * 4 + dt
                    P.op('dve', lambda e, ps=ps, kt=kt: e.scalar_tensor_tensor(
                        out=xs[:, kt, :n], in0=ps[:, :n], scalar=0.5, in1=xs[:, kt, :n], op0=ALU.mult, op1=ALU.add),
                        reads=[bp, bx[kt]], writes=[bx[kt]])

        def seg_front(l, bi):
            t0, n = cfg.blocks[bi]
            norm_to_h(n, 3 * l + 0)
            if cfg.debug and l == 0:
                P.dma('sp', lambda e: e.dma_start(out=dbg_h[:, t0:t0 + n].rearrange("(kt p) t -> p kt t", p=128), in_=hb[:, :, :n]),
                      reads=[bh])
            ffn(n, l, "wg1", "wu1", "wd1")
            if cfg.debug and l == 0:
                P.dma('sp', lambda e: e.dma_start(out=dbg_x[:, t0:t0 + n].rearrange("(kt p) t -> p kt t", p=128), in_=xs[:, :, :n]),
                      reads=bx)
            P.dma('sp', lambda e: e.dma_start(out=xsc[:, t0:t0 + n].rearrange("(kt p) t -> p kt t", p=128), in_=xs[:, :, :n]),
                  reads=bx, writes=[bxsc[bi]])
            norm_to_h(n, 3 * l + 1)
            for cb in range(12):
                po = proj16(n, l, "win", cb, hb, bh)
                for ft in range(4):
                    ps, bp = po[ft]
                    i = rr('stg', 2)
                    c0 = cb * 512 + ft * 128
                    P.op('act', lambda e, i=i, ps=ps: e.activation(out=stg[i][:, :n], in_=ps[:, :n], func=AF.Copy),
                         reads=[bp], writes=[bstg[i]])
                    P.dma('sp', lambda e, i=i, c0=c0: e.dma_start(out=psc[c0:c0 + 128, t0:t0 + n], in_=stg[i][:, :n]),
                          reads=[bstg[i]], writes=[bpsc[bi]])

        def seg_back(l, bi):
            t0, n = cfg.blocks[bi]
            P.dma('sp', lambda e: e.dma_start(out=xs[:, :, :n], in_=xsc[:, t0:t0 + n].rearrange("(kt p) t -> p kt t", p=128)),
                  reads=[bxsc[bi]], writes=bx)
            P.dma('sp', lambda e: e.dma_start(out=hb[:, :, :n], in_=msc[:, t0:t0 + n].rearrange("(kt p) t -> p kt t", p=128)),
                  reads=[bmsc[bi]], writes=[bh])
            for cb in range(4):
                po = proj16(n, l, "wout", cb, hb, bh)
                for ft in range(4):
                    ps, bp = po[ft]
                    kt = cb * 4 + ft
                    P.op('dve', lambda e, ps=ps, kt=kt: e.tensor_tensor(out=xs[:, kt, :n], in0=ps[:, :n], in1=xs[:, kt, :n],
                                                                       op=ALU.add), reads=[bp, bx[kt]], writes=[bx[kt]])
            norm_to_h(n, 3 * l + 2)
            ffn(n, l, "wg2", "wu2", "wd2")

        def load_x0(bi):
            t0, n = cfg.blocks[bi]
            P.dma('sp', lambda e: e.dma_start(out=xs[:, :, :n], in_=xT[:, t0:t0 + n].rearrange("(kt p) t -> p kt t", p=128)),
                  writes=bx)

        def final(bi):
            t0, n = cfg.blocks[bi]

            def of(kt):
                return act[:, 0:2, :].bitcast(F32)[:, 0, :n] if False else None
            ps, bp = ps_next()
            for kt in range(KT):
                i = rr('sq', 2)
                P.op('act', lambda e, kt=kt, i=i: e.activation(out=sq[i][:, :n], in_=xs[:, kt, :n], func=AF.Square),
                     reads=[bx[kt]], writes=[bsq[i]])
                P.op('pe', lambda e, kt=kt, i=i, ps=ps: e.matmul(ps[:, :n], lhsT=ones[:], rhs=sq[i][:, :n],
                                                                   start=(kt == 0), stop=(kt == KT - 1)),
                     reads=[bones, bsq[i]], writes=[bp])
            P.op('dve', lambda e, ps=ps: e.tensor_scalar(out=rstd[:, :n], in0=ps[:, :n], scalar1=1.0 / D, scalar2=NORM_EPS,
                                                          op0=ALU.mult, op1=ALU.add), reads=[bp], writes=[brstd])
            P.op('act', lambda e: e.activation(out=rstd[:, :n], in_=rstd[:, :n], func=AF.Sqrt), reads=[brstd], writes=[brstd])
            P.op('dve', lambda e: e.reciprocal(out=rstd[:, :n], in_=rstd[:, :n]), reads=[brstd], writes=[brstd])
            for kt in range(KT):
                i = rr('stg', 2)
                P.op('dve', lambda e, kt=kt, i=i: e.scalar_tensor_tensor(out=stg[i][:, :n], in0=xs[:, kt, :n],
                                                                         scalar=nrm_t[:, 3 * L, kt:kt + 1],
                                                                         in1=rstd[:, :n], op0=ALU.mult, op1=ALU.mult),
                     reads=[bx[kt], bnrm, brstd], writes=[bstg[i]])
                P.dma('sp', lambda e, kt=kt, i=i: e.dma_start(out=yT[kt * 128:(kt + 1) * 128, t0:t0 + n], in_=stg[i][:, :n]),
                      reads=[bstg[i]])


        a_reset()

        def hg_set():
            d = {}
            d['MT'] = [a_alloc([128, 512]) for i in range(10)]
            d['bMT'] = [Buf('mt%d' % i) for i in range(10)]
            for nm, shp, dt_ in (('qtb', [128, 512], BF16), ('ktb', [128, 512], BF16), ('vtm', [128, 128], BF16), ('ktm', [128, 128], BF16),
                                 ('attm', [128, 128], BF16), ('Sst', [128, 128], F32), ('Stmp', [128, 128], F32), ('Sbf', [128, 128], BF16),
                                 ('S0', [128, NS, 128], F32), ('S0b', [128, NS, 128], BF16), ('vmask', [128, NS, 128], BF16), ('mixh', [128, 512], BF16)):
                d[nm] = a_alloc(shp, dt_)
                d['b' + nm] = Buf(nm)
            return d
        HGS = [hg_set(), hg_set()]
        mixo = sb("mixo", [128, 512], BF16); bmixo = Buf()
        ident = sb("identt", [128, 128]); m64 = sb("m64t", [128, 128]); m8 = sb("m8t", [128, 128])
        bm16 = sb("bm16t", [128, NS, 128], BF16); cm = sb("cmt", [128, 3, 512], BF16); iota = sb("iotat", [128, 128])
        bconst = Buf('const')
        lbraw = sb("lbraw_t", [128, 6, L]); lbt = sb("lbt", [128, 6, L]); omlt = sb("omlt", [128, 6, L])
        lsum = sb("lsum", [128, 6]); hnw = sb("hnw_t", [128, L, 6])
        blb = Buf('lb')
        for t_, d_ in ((ident, ident_d), (m64, m64_d), (m8, m8_d), (bm16, bm16_d), (cm, cm_d), (hnw, hnw_d), (iota, iota_d)):
            P.dma('sp', lambda e, t_=t_, d_=d_: e.dma_start(out=t_[:], in_=d_), writes=[bconst])
        P.dma('sp', lambda e: e.dma_start(out=lbraw[:], in_=lbraw_d), writes=[blb])
        P.op('act', lambda e: e.activation(out=lbraw[:], in_=lbraw[:], func=AF.Exp), reads=[blb], writes=[blb])
        P.op('dve', lambda e: e.tensor_reduce(out=lsum[:], in_=lbraw[:], axis=mybir.AxisListType.X, op=ALU.add), reads=[blb], writes=[blb])
        P.op('dve', lambda e: e.reciprocal(out=lsum[:], in_=lsum[:]), reads=[blb], writes=[blb])
        P.op('dve', lambda e: e.memset(lbt[:], 0.0), reads=[blb], writes=[blb])
        for l_ in range(1, L):
            P.op('dve', lambda e, l_=l_: e.tensor_tensor(out=lbt[:, :, l_], in0=lbraw[:, :, l_], in1=lsum[:], op=ALU.mult), reads=[blb], writes=[blb])
            if l_ > 1:
                P.op('dve', lambda e, l_=l_: e.tensor_tensor(out=lbt[:, :, l_], in0=lbt[:, :, l_], in1=lbt[:, :, l_ - 1], op=ALU.add), reads=[blb], writes=[blb])
        P.op('dve', lambda e: e.tensor_scalar(out=omlt[:], in0=lbt[:], scalar1=-1.0, scalar2=1.0, op0=ALU.mult, op1=ALU.add), reads=[blb], writes=[blb])

        pieces = list(cfg.blocks)
        NPB = len(pieces) - 1
        bpsc_all = bpsc
        bmsc_all = bmsc

        def hgrn(l):
            for hA in range(0, 6, 2):
                ra, rb_ = Rec([0, 1, 2, 3]), Rec([4, 5, 6, 7])
                hgrn_head(l, hA, HGS[0], ra)
                hgrn_head(l, hA + 1, HGS[1], rb_)
                merge([ra, rb_])

        def hgrn_head(l, h, S_, R_):
            O_Q, O_F, O_I, O_G = 3072, 3840, 4608, 5376
            tF, tQ, tG, tI, tK, tB, tE, tEn, oacc, rst = S_['MT']
            bF, bQ, bG, bI, bK, bB, bE, bEn, boacc, brst = S_['bMT']
            qtb, ktb, vtm, ktm, attm, Sst, Stmp, Sbf, S0, S0b, vmask, mixo = (S_[k] for k in ('qtb', 'ktb', 'vtm', 'ktm', 'attm', 'Sst', 'Stmp', 'Sbf', 'S0', 'S0b', 'vmask', 'mixh'))
            bqtb, bktb, bvtm, bktm, battm, bSst, bStmp, bSbf, bS0, bS0b, bvmask, bmixo = (S_['b' + k] for k in ('qtb', 'ktb', 'vtm', 'ktm', 'attm', 'Sst', 'Stmp', 'Sbf', 'S0', 'S0b', 'vmask', 'mixh'))
            if True:
                R_.op('dve', lambda e: e.memset(Sst[:], 0.0), writes=[bSst])
                R_.op('dve', lambda e: e.memset(Sbf[:], 0.0), writes=[bSbf])
                for pi, (t0, n) in enumerate(pieces):
                    samp = (pi == NPB)
                    for tt_, bb_, off in ((tF, bF, O_F), (tQ, bQ, O_Q), (tI, bI, O_I), (tG, bG, O_G)):
                        R_.dma('sp', lambda e, tt_=tt_, off=off, t0=t0, n=n, h=h: e.dma_start(
                            out=tt_[:, :n], in_=psc[off + 128 * h: off + 128 * h + 128, t0:t0 + n]),
                            reads=[bpsc_all[pi]], writes=[bb_])
                    R_.op('act', lambda e, n=n: e.activation(out=tF[:, :n], in_=tF[:, :n], func=AF.Sigmoid), reads=[bF], writes=[bF])
                    R_.op('dve', lambda e, n=n, h=h: e.tensor_scalar(out=tF[:, :n], in0=tF[:, :n], scalar1=omlt[:, h, l:l + 1],
                                                                      scalar2=lbt[:, h, l:l + 1], op0=ALU.mult, op1=ALU.add),
                         reads=[bF, blb], writes=[bF])
                    R_.op('dve', lambda e, n=n: e.tensor_scalar(out=tK[:, :n], in0=tF[:, :n], scalar1=-1.0, scalar2=1.0,
                                                                op0=ALU.mult, op1=ALU.add), reads=[bF], writes=[bK])
                    R_.op('act', lambda e, n=n: e.activation(out=tF[:, :n], in_=tF[:, :n], func=AF.Ln), reads=[bF], writes=[bF])
                    ci = 1 if samp else 0
                    R_.op('dve', lambda e, n=n, ci=ci: e.tensor_tensor_scan(out=tB[:, :n], data0=cm[:, ci, :n], data1=tF[:, :n],
                                                                           initial=0.0, op0=ALU.mult, op1=ALU.add),
                         reads=[bF, bconst], writes=[bB])
                    R_.op('act', lambda e, n=n: e.activation(out=tE[:, :n], in_=tB[:, :n], func=AF.Exp), reads=[bB], writes=[bE])
                    R_.op('act', lambda e, n=n: e.activation(out=tEn[:, :n], in_=tB[:, :n], func=AF.Exp, scale=-1.0), reads=[bB], writes=[bEn])
                    R_.op('act', lambda e, n=n: e.activation(out=tQ[:, :n], in_=tQ[:, :n], func=AF.Silu), reads=[bQ], writes=[bQ])
                    R_.op('dve', lambda e, n=n: e.tensor_tensor(out=qtb[:, :n], in0=tQ[:, :n], in1=tE[:, :n], op=ALU.mult),
                         reads=[bQ, bE], writes=[bqtb])
                    R_.op('dve', lambda e, n=n: e.tensor_tensor(out=tK[:, :n], in0=tK[:, :n], in1=tEn[:, :n], op=ALU.mult),
                         reads=[bK, bEn], writes=[bK])
                    R_.op('act', lambda e, n=n: e.activation(out=ktb[:, :n], in_=tK[:, :n], func=AF.Copy), reads=[bK], writes=[bktb])
                    if samp:
                        R_.dma('sp', lambda e, h=h: e.dma_start(out=S0[:], in_=hst_d[l, :, h, :, :].rearrange("n k v -> k n v")),
                              writes=[bS0])
                        R_.op('act', lambda e: e.activation(out=S0b[:], in_=S0[:], func=AF.Copy), reads=[bS0], writes=[bS0b])
                    for tt in range(n // 128):
                        c0 = tt * 128
                        ps1, bp1 = R_.ps_next()
                        R_.op('pe', lambda e, ps1=ps1, c0=c0: e.transpose(ps1[:, 0:128], tI[:, c0:c0 + 128], ident[:]),
                             reads=[bI, bconst], writes=[bp1])
                        R_.op('act', lambda e, ps1=ps1: e.activation(out=vtm[:], in_=ps1[:, 0:128], func=AF.Copy), reads=[bp1], writes=[bvtm])
                        ps2, bp2 = R_.ps_next()
                        R_.op('pe', lambda e, ps2=ps2, c0=c0: e.transpose(ps2[:, 0:128], tK[:, c0:c0 + 128], ident[:]),
                             reads=[bK, bconst], writes=[bp2])
                        R_.op('act', lambda e, ps2=ps2: e.activation(out=ktm[:], in_=ps2[:, 0:128], func=AF.Copy), reads=[bp2], writes=[bktm])
                        ps3, bp3 = R_.ps_next()
                        R_.op('pe', lambda e, ps3=ps3, c0=c0: e.matmul(ps3[:, 0:128], lhsT=ktb[:, c0:c0 + 128], rhs=qtb[:, c0:c0 + 128],
                                                                      start=True, stop=True), reads=[bktb, bqtb], writes=[bp3])
                        msk = m8 if samp else m64
                        R_.op('dve', lambda e, ps3=ps3, msk=msk: e.tensor_tensor(out=attm[:], in0=ps3[:, 0:128], in1=msk[:], op=ALU.mult),
                             reads=[bp3, bconst], writes=[battm])
                        pso, bpo = R_.ps_next()
                        R_.op('pe', lambda e, pso=pso: e.matmul(pso[:, 0:128], lhsT=vtm[:], rhs=attm[:], start=True, stop=False),
                             reads=[bvtm, battm], writes=[bpo])
                        if not samp:
                            for cc in range(2):
                                a0 = cc * 64
                                R_.op('pe', lambda e, pso=pso, a0=a0, c0=c0, cc=cc: e.matmul(
                                    pso[:, a0:a0 + 64], lhsT=Sbf[:], rhs=qtb[:, c0 + a0:c0 + a0 + 64], start=False, stop=(cc == 1)),
                                    reads=[bSbf, bqtb], writes=[bpo])
                                psd, bpd = R_.ps_next()
                                R_.op('pe', lambda e, psd=psd, a0=a0: e.matmul(psd[:, 0:128], lhsT=ktm[a0:a0 + 64, :], rhs=vtm[a0:a0 + 64, :],
                                                                                start=True, stop=True), reads=[bktm, bvtm], writes=[bpd])
                                ecol = c0 + a0 + 63
                                R_.op('dve', lambda e, ecol=ecol: e.tensor_scalar(out=Stmp[:], in0=Sst[:], scalar1=tE[:, ecol:ecol + 1],
                                                                                 scalar2=None, op0=ALU.mult), reads=[bSst, bE], writes=[bStmp])
                                R_.op('dve', lambda e, psd=psd, ecol=ecol: e.scalar_tensor_tensor(
                                    out=Sst[:], in0=psd[:, 0:128], scalar=tE[:, ecol:ecol + 1], in1=Stmp[:], op0=ALU.mult, op1=ALU.add),
                                    reads=[bpd, bE, bStmp], writes=[bSst])
                                R_.op('act', lambda e: e.activation(out=Sbf[:], in_=Sst[:], func=AF.Copy), reads=[bSst], writes=[bSbf])
                        else:
                            for sn in range(NS):
                                R_.op('pe', lambda e, pso=pso, sn=sn: e.matmul(
                                    pso[:, sn * 8:sn * 8 + 8], lhsT=S0b[:, sn, :], rhs=qtb[:, sn * 8:sn * 8 + 8], start=False, stop=(sn == NS - 1)),
                                    reads=[bS0b, bqtb], writes=[bpo])
                            R_.op('act', lambda e, pso=pso, c0=c0: e.activation(out=oacc[:, c0:c0 + 128], in_=pso[:, 0:128], func=AF.Copy),
                                 reads=[bpo], writes=[boacc])
                            for sn in range(NS):
                                R_.op('dve', lambda e, sn=sn: e.tensor_tensor(out=vmask[:, sn, :], in0=vtm[:], in1=bm16[:, sn, :], op=ALU.mult),
                                     reads=[bvtm, bconst], writes=[bvmask])
                            for g4 in range(4):
                                psd, bpd = R_.ps_next()
                                R_.op('pe', lambda e, psd=psd, g4=g4: e.matmul(
                                    psd[:, :], lhsT=ktm[:], rhs=vmask[:, g4 * 4:g4 * 4 + 4, :].rearrange("p a b -> p (a b)"), start=True, stop=True),
                                    reads=[bktm, bvmask], writes=[bpd])
                                for j in range(4):
                                    sn = g4 * 4 + j
                                    ecol = sn * 8 + 7
                                    R_.op('dve', lambda e, psd=psd, j=j, sn=sn: e.tensor_tensor(
                                        out=S0[:, sn, :], in0=psd[:, j * 128:(j + 1) * 128], in1=S0[:, sn, :], op=ALU.add),
                                        reads=[bpd, bS0, bS0b], writes=[bS0])
                                    R_.op('dve', lambda e, sn=sn, ecol=ecol: e.tensor_scalar(
                                        out=S0[:, sn, :], in0=S0[:, sn, :], scalar1=tE[:, ecol:ecol + 1], scalar2=None, op0=ALU.mult),
                                        reads=[bS0, bE], writes=[bS0])
                            R_.dma('sp', lambda e, h=h: e.dma_start(out=hgs_d[l, :, h, :, :].rearrange("n k v -> k n v"), in_=S0[:]),
                                  reads=[bS0])
                        if not samp:
                            R_.op('act', lambda e, pso=pso, c0=c0: e.activation(out=oacc[:, c0:c0 + 128], in_=pso[:, 0:128], func=AF.Copy),
                                 reads=[bpo], writes=[boacc])
                    if pi == NPB - 1:
                        R_.dma('sp', lambda e, h=h: e.dma_start(out=hgp_d[l, h, :, :], in_=Sst[:]), reads=[bSst])
                    R_.op('act', lambda e, n=n: e.activation(out=rst[:, :n], in_=oacc[:, :n], func=AF.Square), reads=[boacc], writes=[brst])
                    psn, bpn = R_.ps_next()
                    R_.op('pe', lambda e, psn=psn, n=n: e.matmul(psn[:, :n], lhsT=ones[:], rhs=rst[:, :n], start=True, stop=True),
                         reads=[bones, brst], writes=[bpn])
                    R_.op('dve', lambda e, psn=psn, n=n: e.tensor_scalar(out=rst[:, :n], in0=psn[:, :n], scalar1=1.0 / 128, scalar2=1e-5,
                                                                        op0=ALU.mult, op1=ALU.add), reads=[bpn], writes=[brst])
                    R_.op('act', lambda e, n=n: e.activation(out=rst[:, :n], in_=rst[:, :n], func=AF.Sqrt), reads=[brst], writes=[brst])
                    R_.op('dve', lambda e, n=n: e.reciprocal(out=rst[:, :n], in_=rst[:, :n]), reads=[brst], writes=[brst])
                    R_.op('dve', lambda e, n=n, h=h: e.scalar_tensor_tensor(out=oacc[:, :n], in0=oacc[:, :n], scalar=hnw[:, l, h:h + 1],
                                                                            in1=rst[:, :n], op0=ALU.mult, op1=ALU.mult),
                         reads=[boacc, brst, bconst], writes=[boacc])
                    R_.op('act', lambda e, n=n: e.activation(out=tG[:, :n], in_=tG[:, :n], func=AF.Silu), reads=[bG], writes=[bG])
                    R_.op('dve', lambda e, n=n: e.tensor_tensor(out=mixo[:, :n], in0=oacc[:, :n], in1=tG[:, :n], op=ALU.mult),
                         reads=[boacc, bG], writes=[bmixo])
                    R_.dma('sp', lambda e, h=h, t0=t0, n=n: e.dma_start(out=msc[1280 + 128 * h:1280 + 128 * h + 128, t0:t0 + n], in_=mixo[:, :n]),
                          reads=[bmixo], writes=[bmsc_all[pi]])


        a_reset()
        cosT = a_alloc([128, 16, 128]); sinT = a_alloc([128, 16, 128]); RMp = a_alloc([128, 16, 128]); RMs = a_alloc([128, 16, 128])
        bTab = Buf('s5tab')
        tA = a_alloc([128, 16, 128]); tBt = a_alloc([128, 16, 128]); zr = a_alloc([128, 16, 128]); zi = a_alloc([128, 16, 128])
        bA, bBt, bzr, bzi = Buf('A'), Buf('Bt'), Buf('zr'), Buf('zi')
        cblk = a_alloc([128, 32, 128]); bcblk = Buf('cblk')
        s5fm = sb("s5fm_t", [128, 3, 16]); fmw = sb("fmw", [128, 6, 16]); bfm = Buf('fm')
        Bp = sb("Bp", [128, 2, 2, 4, 128], BF16); bBbar = Buf('Bbar')
        utb = sb("utb", [128, 4, 128], BF16); butb = Buf('utb')
        rmask2 = sb("rmask2_t", [128, 2])
        zer128 = sb("zer128", [128, 128], BF16)
        P.op('dve', lambda e: e.memset(zer128[:], 0.0), writes=[bconst])
        P.dma('sp', lambda e: e.dma_start(out=rmask2[:], in_=rmask2_d), writes=[bconst])
        s5db = sb("s5db_t", [128, 2, 4]); wglu = sb("wglu", [128, 4, 512], BF16); bs5p = Buf('s5p')
        ut = sb("ut", [128, 4, 128]); but = Buf('u')
        xc = sb("xc", [128, 2, 16, NS]); bxc = Buf('xc')
        wk = sb("wk", [128, 6, 16, NS]); bwk = Buf('wk')
        ysb = sb("ysb", [128, 4, 128]); bysb = Buf('ysb')
        yb = sb("yb", [128, 4, 128], BF16); byb = Buf('yb')
        yt = [sb("yt%d" % i, [128, 128]) for i in range(2)]; byt = [Buf() for _ in range(2)]
        TWO_PI = 2.0 * np.pi

        def flat(v):
            return v.rearrange("p a b -> p (a b)")

        def sin_reduce(out, src, tmpf, tmpi, bufs_r, bufs_w):
            rw = list(bufs_r) + list(bufs_w)
            P.op('dve', lambda e: e.tensor_scalar(out=tmpf, in0=src, scalar1=1.0 / TWO_PI, scalar2=None, op0=ALU.mult), reads=rw, writes=bufs_w)
            P.op('dve', lambda e: e.tensor_copy(out=tmpi, in_=tmpf), reads=rw, writes=bufs_w)
            P.op('dve', lambda e: e.tensor_copy(out=tmpf, in_=tmpi), reads=rw, writes=bufs_w)
            P.op('dve', lambda e: e.scalar_tensor_tensor(out=out, in0=tmpf, scalar=-TWO_PI, in1=src, op0=ALU.mult, op1=ALU.add), reads=rw, writes=bufs_w)
            P.op('dve', lambda e: e.tensor_scalar(out=tmpf, in0=out, scalar1=float(np.pi), scalar2=-TWO_PI, op0=ALU.is_gt, op1=ALU.mult), reads=rw, writes=bufs_w)
            P.op('dve', lambda e: e.tensor_tensor(out=out, in0=out, in1=tmpf, op=ALU.add), reads=rw, writes=bufs_w)
            P.op('dve', lambda e: e.tensor_scalar(out=tmpf, in0=out, scalar1=-float(np.pi), scalar2=TWO_PI, op0=ALU.is_lt, op1=ALU.mult), reads=rw, writes=bufs_w)
            P.op('dve', lambda e: e.tensor_tensor(out=out, in0=out, in1=tmpf, op=ALU.add), reads=rw, writes=bufs_w)
            P.op('dve', lambda e: e.tensor_scalar(out=out, in0=out, scalar1=3.1415925, scalar2=-3.1415925, op0=ALU.min, op1=ALU.max), reads=rw, writes=bufs_w)
            P.op('act', lambda e: e.activation(out=out, in_=out, func=AF.Sin), reads=rw, writes=bufs_w)

        def s5_setup(l):
            allb = [bTab, bA, bBt, bzr, bzi]
            P.dma('sp', lambda e: e.dma_start(out=s5fm[:], in_=s5fm_d[:, l]), writes=[bfm])
            P.dma('sp', lambda e: e.dma_start(out=s5db[:], in_=s5db_d[:, l]), writes=[bs5p])
            P.dma('pool', lambda e: e.dma_start(out=wglu[:], in_=s5w_d[:, l]), writes=[bs5p])
            P.dma('sp', lambda e: e.dma_start(out=cblk[:], in_=s5c_d[:, l].rearrange("p a j m -> p (a j) m")), writes=[bcblk])
            P.dma('sp', lambda e: e.dma_start(out=tBt[:, 0:12, :], in_=s5row_d[:, l]), writes=[bBt])
            P.dma('sp', lambda e: e.dma_start(out=tBt[:, 12:16, :], in_=s5b_d[:, l, 0]), writes=[bBt])
            P.dma('sp', lambda e: e.dma_start(out=tA[:, 12:16, :], in_=s5b_d[:, l, 1]), writes=[bA])
            are, aim, ldt = s5fm[:, 0, :], s5fm[:, 1, :], s5fm[:, 2, :]
            dt_, th, lr, rho, abr, abi = (fmw[:, i, :] for i in range(6))
            P.op('act', lambda e: e.activation(out=dt_, in_=ldt, func=AF.Exp), reads=[bfm], writes=[bfm])
            P.op('dve', lambda e: e.tensor_tensor(out=th, in0=aim, in1=dt_, op=ALU.mult), reads=[bfm], writes=[bfm])
            P.op('dve', lambda e: e.tensor_tensor(out=lr, in0=are, in1=dt_, op=ALU.mult), reads=[bfm], writes=[bfm])
            P.op('act', lambda e: e.activation(out=rho, in_=lr, func=AF.Exp), reads=[bfm], writes=[bfm])
            for j in range(16):
                P.op('dve', lambda e, j=j: e.tensor_scalar(out=zr[:, j, :], in0=iota[:], scalar1=fmw[:, 1, j:j + 1], scalar2=None, op0=ALU.mult),
                     reads=[bfm, bconst], writes=[bzr])
                P.op('dve', lambda e, j=j: e.tensor_scalar(out=RMp[:, j, :], in0=cm[:, 2, 0:128], scalar1=fmw[:, 3, j:j + 1], scalar2=None, op0=ALU.mult),
                     reads=[bfm, bconst], writes=[bTab])
                P.op('dve', lambda e, j=j: e.tensor_tensor(out=RMs[:, j, :], in0=RMp[:, j, :], in1=cm[:, 1, 0:128], op=ALU.mult),
                     reads=[bTab, bconst], writes=[bTab])
            tAf = flat(tA[:, 0:12, :])
            zif = flat(zi)
            sin_reduce(flat(sinT), flat(zr), flat(cosT), flat(zi).bitcast(I32), [bzr], [bTab, bzi])
            P.op('dve', lambda e: e.tensor_scalar(out=flat(zr), in0=flat(zr), scalar1=float(np.pi / 2), scalar2=None, op0=ALU.add), reads=[bzr, bTab], writes=[bzr])
            sin_reduce_cos(l)
            P.op('dve', lambda e: e.tensor_tensor(out=abr, in0=rho, in1=cosT[:, :, 1], op=ALU.mult), reads=[bfm, bTab], writes=[bfm])
            P.op('dve', lambda e: e.tensor_tensor(out=abi, in0=rho, in1=sinT[:, :, 1], op=ALU.mult), reads=[bfm, bTab], writes=[bfm])
            R = [flat(tBt[:, 4 * i:4 * i + 4, :]) for i in range(4)]
            ar, ai, ld, bre = R
            bim = flat(tA[:, 12:16, :])
            T = [flat(tA[:, 4 * i:4 * i + 4, :]) for i in range(3)] + [flat(zr[:, 4 * i:4 * i + 4, :]) for i in range(4)]
            tmpi = flat(zi[:, 0:4, :]).bitcast(I32)
            bb = [bA, bBt, bzr, bzi]

            def dv(fn):
                P.op('dve', fn, reads=bb, writes=bb)

            def ac(fn):
                P.op('act', fn, reads=bb, writes=bb)
            ac(lambda e: e.activation(out=ld, in_=ld, func=AF.Exp))
            dv(lambda e: e.tensor_tensor(out=T[0], in0=ai, in1=ld, op=ALU.mult))
            dv(lambda e: e.tensor_tensor(out=T[1], in0=ar, in1=ld, op=ALU.mult))
            ac(lambda e: e.activation(out=T[1], in_=T[1], func=AF.Exp))
            sin_reduce(T[3], T[0], T[5], tmpi, bb, bb)
            dv(lambda e: e.tensor_scalar(out=T[0], in0=T[0], scalar1=float(np.pi / 2), scalar2=None, op0=ALU.add))
            sin_reduce(T[4], T[0], T[5], tmpi, bb, bb)
            dv(lambda e: e.tensor_tensor(out=T[4], in0=T[1], in1=T[4], op=ALU.mult))
            dv(lambda e: e.tensor_scalar(out=T[4], in0=T[4], scalar1=-1.0, scalar2=None, op0=ALU.add))
            dv(lambda e: e.tensor_tensor(out=T[3], in0=T[1], in1=T[3], op=ALU.mult))
            dv(lambda e: e.tensor_tensor(out=T[5], in0=ar, in1=ar, op=ALU.mult))
            dv(lambda e: e.tensor_tensor(out=T[6], in0=ai, in1=ai, op=ALU.mult))
            dv(lambda e: e.tensor_tensor(out=T[5], in0=T[5], in1=T[6], op=ALU.add))
            dv(lambda e: e.reciprocal(out=T[5], in_=T[5]))
            dv(lambda e: e.tensor_tensor(out=T[6], in0=T[4], in1=ar, op=ALU.mult))
            dv(lambda e: e.tensor_tensor(out=T[2], in0=T[3], in1=ai, op=ALU.mult))
            dv(lambda e: e.tensor_tensor(out=T[6], in0=T[6], in1=T[2], op=ALU.add))
            dv(lambda e: e.tensor_tensor(out=T[6], in0=T[6], in1=T[5], op=ALU.mult))
            dv(lambda e: e.tensor_tensor(out=T[2], in0=T[3], in1=ar, op=ALU.mult))
            dv(lambda e: e.tensor_tensor(out=T[0], in0=T[4], in1=ai, op=ALU.mult))
            dv(lambda e: e.tensor_tensor(out=T[2], in0=T[2], in1=T[0], op=ALU.subtract))
            dv(lambda e: e.tensor_tensor(out=T[2], in0=T[2], in1=T[5], op=ALU.mult))
            dv(lambda e: e.tensor_tensor(out=T[0], in0=T[6], in1=bre, op=ALU.mult))
            dv(lambda e: e.tensor_tensor(out=T[1], in0=T[2], in1=bim, op=ALU.mult))
            dv(lambda e: e.tensor_tensor(out=T[0], in0=T[0], in1=T[1], op=ALU.subtract))
            for sl in range(2):
                P.op('dve', lambda e, sl=sl: e.tensor_scalar(out=flat(Bp[:, 0, sl]), in0=T[0], scalar1=rmask2[:, sl:sl + 1], scalar2=None, op0=ALU.mult),
                     reads=bb + [bconst], writes=[bBbar])
            dv(lambda e: e.tensor_tensor(out=T[0], in0=T[6], in1=bim, op=ALU.mult))
            dv(lambda e: e.tensor_tensor(out=T[1], in0=T[2], in1=bre, op=ALU.mult))
            dv(lambda e: e.tensor_tensor(out=T[0], in0=T[0], in1=T[1], op=ALU.add))
            for sl in range(2):
                P.op('dve', lambda e, sl=sl: e.tensor_scalar(out=flat(Bp[:, 1, sl]), in0=T[0], scalar1=rmask2[:, sl:sl + 1], scalar2=None, op0=ALU.mult),
                     reads=bb + [bconst], writes=[bBbar])

        def sin_reduce_cos(l):
            sin_reduce(flat(cosT), flat(zr), flat(tA), flat(zi).bitcast(I32), [bzr], [bTab, bzi, bA])
            P.dma('sp', lambda e: e.dma_start(out=tA[:, 12:16, :], in_=s5b_d[:, l, 1]), writes=[bA])

        pieces128 = [(i * 128, 128, False) for i in range(cfg.tp // 128)] + [(cfg.tp, 128, True)]

        def s5(l):
            s5_setup(l)
            if cfg.stop <= 1:
                return
            cflat, sflat = flat(cosT), flat(sinT)
            PRv = psall[:, 0:4, :].rearrange("p b c -> p (b c)")
            PIv = psall[:, 4:8, :].rearrange("p b c -> p (b c)")
            bPR, bPI = bps[0:4], bps[4:8]
            SK = int(os.environ.get('S5SKIP', 0))
            if not SK & 1:
                P.op('dve', lambda e: e.memset(xc[:].rearrange("p a b c -> p (a b c)"), 0.0), writes=[bxc])
            for (t0, n, samp) in pieces128:
                bi = min(t0 // TB, len(cfg.blocks) - 1)
                P.dma('sp', lambda e, t0=t0: e.dma_start(out=ut[:], in_=psc[0:512, t0:t0 + 128].rearrange("(q p) t -> p q t", p=128)),
                      reads=[bpsc_all[bi]], writes=[but])
                if samp:
                    P.dma('sp', lambda e: e.dma_start(out=xc[:], in_=s5x0_d[:, l]), writes=[bxc])
                if not SK & 2:
                    P.op('act', lambda e: e.activation(out=utb[:], in_=ut[:], func=AF.Copy), reads=[but], writes=[butb])
                for g in range(4):
                    for jj in range(4):
                        j = 4 * g + jj
                        q, r, sl = j // 4, 64 * ((j % 4) // 2), j % 2
                        for c_, bk in ((0, jj), (1, 4 + jj)):
                            P.op('pe', lambda e, q=q, r=r, sl=sl, c_=c_, bk=bk: e.matmul(
                                psb[bk][:, 0:128], lhsT=Bp[r:r + 64, c_, sl, q, :], rhs=utb[r:r + 64, q, :],
                                start=True, stop=True), reads=[bBbar, butb], writes=[bps[bk]])
                    PRb = psall[:, 0:4, 0:128]
                    PIb = psall[:, 4:8, 0:128]
                    bPR4, bPI4 = bps[0:4], bps[4:8]
                    gs = slice(4 * g, 4 * g + 4)
                    cf_, sf_ = cosT[:, gs, :], sinT[:, gs, :]
                    fA, fB, fzr, fzi = tA[:, gs, :], tBt[:, gs, :], zr[:, gs, :], zi[:, gs, :]
                    P.op('dve', lambda e, PRb=PRb, cf_=cf_, fA=fA: e.tensor_tensor(out=fA, in0=PRb, in1=cf_, op=ALU.mult), reads=bPR4 + [bTab], writes=[bA])
                    P.op('dve', lambda e, PIb=PIb, sf_=sf_, fB=fB: e.tensor_tensor(out=fB, in0=PIb, in1=sf_, op=ALU.mult), reads=bPI4 + [bTab], writes=[bBt])
                    P.op('dve', lambda e, fA=fA, fB=fB, fzr=fzr: e.tensor_tensor(out=fzr, in0=fA, in1=fB, op=ALU.add), reads=[bA, bBt], writes=[bzr])
                    P.op('dve', lambda e, PIb=PIb, cf_=cf_, fA=fA: e.tensor_tensor(out=fA, in0=PIb, in1=cf_, op=ALU.mult), reads=bPI4 + [bTab, bzr], writes=[bA])
                    P.op('dve', lambda e, PRb=PRb, sf_=sf_, fB=fB: e.tensor_tensor(out=fB, in0=PRb, in1=sf_, op=ALU.mult), reads=bPR4 + [bTab, bzr], writes=[bBt])
                    P.op('dve', lambda e, fA=fA, fB=fB, fzi=fzi: e.tensor_tensor(out=fzi, in0=fA, in1=fB, op=ALU.subtract), reads=[bA, bBt], writes=[bzi])
                if cfg.stop <= 2:
                    return
                nc_ = NS if samp else 1
                Xr, Xi = xc[:, 0, :, 0:nc_], xc[:, 1, :, 0:nc_]
                w = [wk[:, i, :, 0:nc_] for i in range(6)]
                if samp:
                    abr_b = fmw[:, 4, :].unsqueeze(2).to_broadcast([128, 16, NS])
                    abi_b = fmw[:, 5, :].unsqueeze(2).to_broadcast([128, 16, NS])
                    csel = lambda tbl: tbl[:, :, 0:128:8]
                else:
                    abr_b = fmw[:, 4, :].unsqueeze(2)
                    abi_b = fmw[:, 5, :].unsqueeze(2)
                    csel = lambda tbl: tbl[:, :, 0:1]
                rb_ = [bxc, bwk, bfm, bTab]

                def dw(fn):
                    P.op('dve', fn, reads=rb_, writes=[bwk])
                dw(lambda e, Xr=Xr, abr_b=abr_b, w=w: e.tensor_tensor(out=w[0], in0=Xr, in1=abr_b, op=ALU.mult))
                dw(lambda e, Xi=Xi, abi_b=abi_b, w=w: e.tensor_tensor(out=w[1], in0=Xi, in1=abi_b, op=ALU.mult))
                dw(lambda e, w=w: e.tensor_tensor(out=w[0], in0=w[0], in1=w[1], op=ALU.subtract))
                dw(lambda e, Xi=Xi, abr_b=abr_b, w=w: e.tensor_tensor(out=w[1], in0=Xi, in1=abr_b, op=ALU.mult))
                dw(lambda e, Xr=Xr, abi_b=abi_b, w=w: e.tensor_tensor(out=w[2], in0=Xr, in1=abi_b, op=ALU.mult))
                dw(lambda e, w=w: e.tensor_tensor(out=w[1], in0=w[1], in1=w[2], op=ALU.add))
                cs, ss = csel(cosT), csel(sinT)
                dw(lambda e, w=w, cs=cs: e.tensor_tensor(out=w[2], in0=w[0], in1=cs, op=ALU.mult))
                dw(lambda e, w=w, ss=ss: e.tensor_tensor(out=w[3], in0=w[1], in1=ss, op=ALU.mult))
                dw(lambda e, w=w: e.tensor_tensor(out=w[2], in0=w[2], in1=w[3], op=ALU.add))
                dw(lambda e, w=w, cs=cs: e.tensor_tensor(out=w[3], in0=w[1], in1=cs, op=ALU.mult))
                dw(lambda e, w=w, ss=ss: e.tensor_tensor(out=w[4], in0=w[0], in1=ss, op=ALU.mult))
                dw(lambda e, w=w: e.tensor_tensor(out=w[3], in0=w[3], in1=w[4], op=ALU.subtract))
                zrc, zic = csel(zr), csel(zi)
                P.op('dve', lambda e, zrc=zrc, w=w: e.tensor_tensor(out=zrc, in0=zrc, in1=w[2], op=ALU.add), reads=[bwk, bzr], writes=[bzr])
                P.op('dve', lambda e, zic=zic, w=w: e.tensor_tensor(out=zic, in0=zic, in1=w[3], op=ALU.add), reads=[bwk, bzi], writes=[bzi])
                if cfg.stop <= 3:
                    return
                RM = flat(RMs) if samp else flat(RMp)
                P.op('dve', lambda e, RM=RM: e.tensor_tensor_scan(out=flat(tA), data0=RM, data1=flat(zr), initial=0.0, op0=ALU.mult, op1=ALU.add),
                     reads=[bTab, bzr], writes=[bA])
                P.op('dve', lambda e, RM=RM: e.tensor_tensor_scan(out=flat(tBt), data0=RM, data1=flat(zi), initial=0.0, op0=ALU.mult, op1=ALU.add),
                     reads=[bTab, bzi], writes=[bBt])
                P.op('dve', lambda e: e.tensor_tensor(out=flat(zr), in0=flat(tA), in1=cflat, op=ALU.mult), reads=[bA, bTab], writes=[bzr])
                P.op('dve', lambda e: e.tensor_tensor(out=flat(zi), in0=flat(tBt), in1=sflat, op=ALU.mult), reads=[bBt, bTab], writes=[bzi])
                P.op('dve', lambda e: e.tensor_tensor(out=flat(zr), in0=flat(zr), in1=flat(zi), op=ALU.subtract), reads=[bzr, bzi], writes=[bzr])
                P.op('dve', lambda e: e.tensor_tensor(out=flat(zi), in0=flat(tA), in1=sflat, op=ALU.mult), reads=[bA, bTab, bzr], writes=[bzi])
                P.op('dve', lambda e: e.tensor_tensor(out=flat(tA), in0=flat(tBt), in1=cflat, op=ALU.mult), reads=[bBt, bTab, bzi], writes=[bA])
                P.op('dve', lambda e: e.tensor_tensor(out=flat(zi), in0=flat(zi), in1=flat(tA), op=ALU.add), reads=[bzi, bA], writes=[bzi])
                if cfg.stop <= 4:
                    return
                if samp:
                    P.op('act', lambda e: e.activation(out=xc[:, 0], in_=zr[:, :, 7:128:8], func=AF.Copy), reads=[bzr, bwk], writes=[bxc])
                    P.op('act', lambda e: e.activation(out=xc[:, 1], in_=zi[:, :, 7:128:8], func=AF.Copy), reads=[bzi, bwk], writes=[bxc])
                    P.dma('sp', lambda e: e.dma_start(out=s5s_d[:, l], in_=xc[:]), reads=[bxc])
                else:
                    P.op('act', lambda e: e.activation(out=xc[:, 0, :, 0:1], in_=zr[:, :, 127:128], func=AF.Copy), reads=[bzr, bwk], writes=[bxc])
                    P.op('act', lambda e: e.activation(out=xc[:, 1, :, 0:1], in_=zi[:, :, 127:128], func=AF.Copy), reads=[bzi, bwk], writes=[bxc])
                    if t0 + 128 == cfg.tp:
                        P.dma('sp', lambda e: e.dma_start(out=s5p_d[:, l], in_=xc[:, :, :, 0], allow_slow_non_contiguous=True), reads=[bxc])
                for q in range(4):
                    psA, bpA = ps_next()
                    psB, bpB = ps_next()
                    for jj in range(4):
                        j = 4 * q + jj
                        P.op('pe', lambda e, psA=psA, j=j, jj=jj: e.matmul(psA[:, 0:128], lhsT=cblk[:, j, :], rhs=zr[:, j, :], start=(jj == 0), stop=(jj == 3)),
                             reads=[bcblk, bzr], writes=[bpA])
                        P.op('pe', lambda e, psB=psB, j=j, jj=jj: e.matmul(psB[:, 0:128], lhsT=cblk[:, 16 + j, :], rhs=zi[:, j, :], start=(jj == 0), stop=(jj == 3)),
                             reads=[bcblk, bzi], writes=[bpB])
                    P.op('act', lambda e, psB=psB: e.activation(out=yt[0][:], in_=psB[:, 0:128], func=AF.Copy), reads=[bpB], writes=[byt[0]])
                    P.op('dve', lambda e, psA=psA: e.tensor_tensor(out=yt[0][:], in0=psA[:, 0:128], in1=yt[0][:], op=ALU.subtract), reads=[bpA, byt[0]], writes=[byt[0]])
                    P.op('dve', lambda e, q=q: e.scalar_tensor_tensor(out=yt[0][:], in0=ut[:, q, :], scalar=s5db[:, 0, q:q + 1], in1=yt[0][:], op0=ALU.mult, op1=ALU.add),
                         reads=[but, bs5p, byt[0]], writes=[byt[0]])
                    P.op('dve', lambda e: e.tensor_tensor(out=yt[1][:], in0=yt[0][:], in1=yt[0][:], op=ALU.mult), reads=[byt[0]], writes=[byt[1]])
                    P.op('dve', lambda e: e.tensor_scalar(out=yt[1][:], in0=yt[1][:], scalar1=0.044715, scalar2=1.0, op0=ALU.mult, op1=ALU.add), reads=[byt[1]], writes=[byt[1]])
                    P.op('dve', lambda e: e.tensor_tensor(out=yt[1][:], in0=yt[1][:], in1=yt[0][:], op=ALU.mult), reads=[byt[0], byt[1]], writes=[byt[1]])
                    P.op('act', lambda e: e.activation(out=yt[1][:], in_=yt[1][:], func=AF.Sigmoid, scale=1.5957691216057308), reads=[byt[1]], writes=[byt[1]])
                    P.op('dve', lambda e, q=q: e.tensor_tensor(out=ysb[:, q, :], in0=yt[0][:], in1=yt[1][:], op=ALU.mult), reads=[byt[0], byt[1]], writes=[bysb])
                    P.op('act', lambda e, q=q: e.activation(out=yb[:, q, :], in_=ysb[:, q, :], func=AF.Copy), reads=[bysb], writes=[byb])
                for q2 in range(4):
                    ps, bp = ps_next()
                    for q in range(4):
                        P.op('pe', lambda e, ps=ps, q=q, q2=q2: e.matmul(ps[:, 0:128], lhsT=wglu[:, q, q2 * 128:(q2 + 1) * 128], rhs=yb[:, q, :],
                                                                         start=(q == 0), stop=(q == 3)), reads=[bs5p, byb], writes=[bp])
                    P.op('act', lambda e, ps=ps, q2=q2: e.activation(out=yt[1][:], in_=ps[:, 0:128], func=AF.Sigmoid, bias=s5db[:, 1, q2:q2 + 1]),
                         reads=[bp, bs5p], writes=[byt[1]])
                    P.op('dve', lambda e, q2=q2: e.tensor_tensor(out=mixo[:, 0:128], in0=ysb[:, q2, :], in1=yt[1][:], op=ALU.mult), reads=[bysb, byt[1]], writes=[bmixo])
                    P.dma('sp', lambda e, q2=q2, t0=t0: e.dma_start(out=msc[q2 * 128:(q2 + 1) * 128, t0:t0 + 128], in_=mixo[:, 0:128]),
                          reads=[bmixo], writes=[bmsc_all[bi]])


        a_reset()
        zt = a_alloc([128, 20, 128]); zp = a_alloc([128, 20, 128]); bzt, bzp = Buf('zt'), Buf('zp')
        F6 = [a_alloc([128, 6, 128]) for _ in range(11)]; bF6 = [Buf('f6_%d' % i) for i in range(11)]
        F6.append(F6[1]); bF6.append(bF6[1])
        ARt = a_alloc([128, 6, 256], BF16); bAR = Buf('AR')
        ktl = a_alloc([128, 6, 128], BF16); btl = a_alloc([128, 6, 128], BF16); bktl, bbtl = Buf('ktl'), Buf('btl')
        twa = a_alloc([128, 128], BF16); sgi = a_alloc([128, 128], BF16); btwa, bsgi = Buf(), Buf()
        def rw_set():
            d = {}
            d['vA'] = a_alloc([128, 128], BF16); d['vB'] = a_alloc([128, 128], BF16); d['bvv'] = Buf('vAB')
            d['ktm_'] = a_alloc([128, 128], BF16); d['btm_'] = a_alloc([128, 128], BF16); d['bktm_'] = Buf(); d['bbtm_'] = Buf()
            d['NBm'] = [a_alloc([128, 256], BF16) for _ in range(2)]; d['KAm'] = [a_alloc([128, 256], BF16) for _ in range(2)]
            d['bNB'] = [Buf() for _ in range(2)]; d['bKA'] = [Buf() for _ in range(2)]
            d['Xm'] = [[a_alloc([128, 128], BF16) for _ in range(2)] for _ in range(2)]
            d['Ym'] = [[a_alloc([128, 128], BF16) for _ in range(2)] for _ in range(2)]
            d['Ttm'] = [a_alloc([128, 128], BF16) for _ in range(2)]; d['Tnm'] = [a_alloc([128, 128], BF16) for _ in range(2)]
            d['bXm'] = [[Buf() for _ in range(2)] for _ in range(2)]; d['bYm'] = [[Buf() for _ in range(2)] for _ in range(2)]
            d['bTt'] = [Buf() for _ in range(2)]; d['bTn'] = [Buf() for _ in range(2)]
            d['Ttf'] = [a_alloc([128, 128]) for _ in range(2)]; d['Tnf'] = [a_alloc([128, 128]) for _ in range(2)]
            d['WpA'] = a_alloc([128, 128], BF16); d['WpB'] = a_alloc([128, 128], BF16); d['bWp'] = Buf('Wp')
            d['UpA'] = a_alloc([128, 128], BF16); d['UpB'] = a_alloc([128, 128], BF16); d['bUp'] = Buf('Up')
            d['t128'] = [a_alloc([128, 128]) for _ in range(3)]; d['bt128'] = [Buf() for _ in range(3)]
            d['mixr'] = a_alloc([128, 128], BF16); d['bmixr'] = Buf()
            return d
        RWS = [rw_set(), rw_set()]
        Hblk = a_alloc([128, 6, 128]); Hbb = a_alloc([128, 6, 128], BF16); bHc = [Buf('H%d' % i) for i in range(6)]; bHbc = [Buf('Hb%d' % i) for i in range(6)]
        Hs = a_alloc([128, NS, 128]); Hsb = a_alloc([128, NS, 128], BF16); bHs, bHsb = Buf('Hs'), Buf('Hsb')
        ztf = zt.rearrange("p a b -> p (a b)")
        UmA = ztf[:, 0:1024].bitcast(BF16).rearrange("p (a b) -> p a b", a=NS); bUm = bzt
        VmA = ztf[:, 1024:2048].bitcast(BF16).rearrange("p (a b) -> p a b", a=NS); bVm = bzt
        yfm = a_alloc([128, 6, 128]); byfmc = [Buf('yfm%d' % i) for i in range(6)]
        shc = a_alloc([128, 20, NS]); bshc = Buf('shc')
        rwp = sb("rwp_t", [128, 62]); brwp = Buf('rwp')
        rwl = sb("rwl_t", [128, 2, 768], BF16); brwl = Buf('rwl')
        rwm = sb("rwm_t", [128, 2, 384], BF16); blk64 = sb("blk64_t", [128, 128]); cm6 = sb("cm6_t", [128, 2, 768], BF16)
        for t_, d_ in ((rwm, rwm_d), (blk64, blk64_d), (cm6, cm6_d)):
            P.dma('sp', lambda e, t_=t_, d_=d_: e.dma_start(out=t_[:], in_=d_), writes=[bconst])

        def rwkv(l):
            Dv = lambda fn, r, w: P.op('dve', fn, reads=r, writes=w)
            Ac = lambda fn, r, w: P.op('act', fn, reads=r, writes=w)
            Pe = lambda fn, r, w: P.op('pe', fn, reads=r, writes=w)
            P.dma('sp', lambda e: e.dma_start(out=rwp[:], in_=rwp_d[:, l]), writes=[brwp])
            P.dma('pool', lambda e: e.dma_start(out=rwl[:], in_=rwl_d[:, l]), writes=[brwl])
            mu = rwp[:, 0:20]
            w0, a0, k_k, k_a, r_k, ln_w, ln_b = (rwp[:, 20 + 6 * i:26 + 6 * i] for i in range(7))
            sg_a, lw, cc, ec, enc, ecm, kk, kp, gg, tm1, tm2, tm3 = F6
            b_a, b_lw, b_cc, b_ec, b_enc, b_ecm, b_kk, b_kp, b_gg, b_tm1, b_tm2, b_tm3 = bF6
            Dv(lambda e: e.memset(Hblk[:].rearrange("p a b -> p (a b)"), 0.0), [], bHc)
            Dv(lambda e: e.memset(Hbb[:].rearrange("p a b -> p (a b)"), 0.0), [], bHbc)
            for S_ in RWS:
                for z_ in (S_['WpA'], S_['WpB'], S_['UpA'], S_['UpB'], S_['vA'], S_['vB']):
                    Dv(lambda e, z_=z_: e.memset(z_[:], 0.0), [], [S_['bWp'], S_['bUp'], S_['bvv']])
            for (t0, n, samp) in pieces128:
                bi = min(t0 // TB, len(cfg.blocks) - 1)
                mi = 1 if samp else 0
                NLEV = 2 if samp else 6
                rd = [bpsc_all[bi]] + ([bpsc_all[bi - 1]] if bi > 0 else [])
                P.dma('sp', lambda e, t0=t0: e.dma_start(out=zt[:], in_=psc[512:3072, t0:t0 + 128].rearrange("(c p) t -> p c t", p=128)),
                      reads=rd, writes=[bzt])
                if t0 == 0:
                    Dv(lambda e: e.memset(zp[:, :, 0:1], 0.0), [], [bzp])
                    P.dma('sp', lambda e: e.dma_start(out=zp[:, :, 1:128], in_=psc[512:3072, 0:127].rearrange("(c p) t -> p c t", p=128)),
                          reads=rd, writes=[bzp])
                else:
                    P.dma('sp', lambda e, t0=t0: e.dma_start(out=zp[:], in_=psc[512:3072, t0 - 1:t0 + 127].rearrange("(c p) t -> p c t", p=128)),
                          reads=rd, writes=[bzp])
                if samp:
                    P.dma('sp', lambda e: e.dma_start(out=shc[:], in_=rwsh0_d[:, l]), writes=[bshc])
                    Dv(lambda e: e.tensor_copy(out=zp[:, :, 0:128:8], in_=shc[:]), [bshc, bzp], [bzp])
                    Ac(lambda e: e.activation(out=shc[:], in_=zt[:, :, 7:128:8], func=AF.Copy), [bzt, bzp], [bshc])
                    P.dma('sp', lambda e: e.dma_start(out=rwshs_d[:, l], in_=shc[:]), reads=[bshc])
                elif t0 + 128 == cfg.tp:
                    P.dma('sp', lambda e: e.dma_start(out=rwshp_d[:, l], in_=zt[:, :, 127], allow_slow_non_contiguous=True), reads=[bzt])
                Dv(lambda e: e.tensor_tensor(out=zp[:], in0=zp[:], in1=zt[:], op=ALU.subtract), [bzp, bzt], [bzp])
                Dv(lambda e: e.tensor_tensor(out=zp[:], in0=zp[:], in1=mu.unsqueeze(2).to_broadcast([128, 20, 128]), op=ALU.mult), [bzp, brwp], [bzp])
                Dv(lambda e: e.tensor_tensor(out=zp[:], in0=zp[:], in1=zt[:], op=ALU.add), [bzp, bzt], [bzp])
                zr_, zk_, zv_ = zp[:, 0:6, :], zp[:, 6:12, :], zp[:, 12:18, :]
                Ac(lambda e: e.activation(out=twa[0:64, :], in_=zp[0:64, 18, :], func=AF.Tanh), [bzp], [btwa])
                Ac(lambda e: e.activation(out=twa[64:128, :], in_=zp[64:128, 18, :], func=AF.Copy), [bzp], [btwa])
                Ac(lambda e: e.activation(out=sgi[:], in_=zp[:, 19, :], func=AF.Sigmoid), [bzp], [bsgi])
                for c in range(6):
                    ps, bp = ps_next()
                    Pe(lambda e, ps=ps, c=c: e.matmul(ps[:, 0:128], lhsT=rwl[0:64, 0, c * 128:(c + 1) * 128], rhs=twa[0:64, :], start=True, stop=True),
                       [brwl, btwa], [bp])
                    Ac(lambda e, ps=ps, c=c: e.activation(out=lw[:, c, :], in_=ps[:, 0:128], func=AF.Sigmoid, bias=w0[:, c:c + 1]), [bp, brwp], [b_lw])
                    ps, bp = ps_next()
                    Pe(lambda e, ps=ps, c=c: e.matmul(ps[:, 0:128], lhsT=rwl[64:128, 0, c * 128:(c + 1) * 128], rhs=twa[64:128, :], start=True, stop=True),
                       [brwl, btwa], [bp])
                    Ac(lambda e, ps=ps, c=c: e.activation(out=sg_a[:, c, :], in_=ps[:, 0:128], func=AF.Sigmoid, bias=a0[:, c:c + 1]), [bp, brwp], [b_a])
                    ps, bp = ps_next()
                    Pe(lambda e, ps=ps, c=c: e.matmul(ps[:, 0:128], lhsT=rwl[:, 1, c * 128:(c + 1) * 128], rhs=sgi[:], start=True, stop=True),
                       [brwl, bsgi], [bp])
                    Ac(lambda e, ps=ps, c=c: e.activation(out=gg[:, c, :], in_=ps[:, 0:128], func=AF.Copy), [bp], [b_gg])
                Dv(lambda e: e.tensor_scalar(out=lw[:], in0=lw[:], scalar1=-0.6065306597126334, scalar2=None, op0=ALU.mult), [b_lw], [b_lw])
                Dv(lambda e: e.tensor_tensor(out=kk[:], in0=zk_, in1=k_k.unsqueeze(2).to_broadcast([128, 6, 128]), op=ALU.mult), [bzp, brwp], [b_kk])
                Dv(lambda e: e.tensor_tensor(out=tm1[:], in0=kk[:], in1=kk[:], op=ALU.mult), [b_kk], [b_tm1])
                for c in range(6):
                    ps, bp = ps_next()
                    Pe(lambda e, ps=ps, c=c: e.matmul(ps[:, 0:128], lhsT=blk64[:], rhs=tm1[:, c, :], start=True, stop=True), [bconst, b_tm1], [bp])
                    Dv(lambda e, ps=ps, c=c: e.tensor_scalar(out=tm2[:, c, :], in0=ps[:, 0:128], scalar1=1e-24, scalar2=None, op0=ALU.max), [bp], [b_tm2])
                Ac(lambda e: e.activation(out=tm2[:], in_=tm2[:], func=AF.Sqrt), [b_tm2], [b_tm2])
                Dv(lambda e: e.reciprocal(out=tm2[:], in_=tm2[:]), [b_tm2], [b_tm2])
                Dv(lambda e: e.tensor_tensor(out=kk[:], in0=kk[:], in1=tm2[:], op=ALU.mult), [b_kk, b_tm2], [b_kk])
                Dv(lambda e: e.tensor_scalar(out=tm1[:], in0=sg_a[:], scalar1=-1.0, scalar2=None, op0=ALU.add), [b_a, b_tm1], [b_tm1])
                Dv(lambda e: e.tensor_tensor(out=tm1[:], in0=tm1[:], in1=k_a.unsqueeze(2).to_broadcast([128, 6, 128]), op=ALU.mult), [b_tm1, brwp], [b_tm1])
                Dv(lambda e: e.tensor_scalar(out=tm1[:], in0=tm1[:], scalar1=1.0, scalar2=None, op0=ALU.add), [b_tm1], [b_tm1])
                Dv(lambda e: e.tensor_tensor(out=kp[:], in0=zk_, in1=tm1[:], op=ALU.mult), [bzp, b_tm1], [b_kp])
                Dv(lambda e, mi=mi: e.tensor_tensor_scan(out=cc[:].rearrange("p a b -> p (a b)"), data0=cm6[:, mi, :], data1=lw[:].rearrange("p a b -> p (a b)"),
                                                         initial=0.0, op0=ALU.mult, op1=ALU.add), [b_lw, bconst], [b_cc])
                Ac(lambda e: e.activation(out=ec[:], in_=cc[:], func=AF.Exp), [b_cc], [b_ec])
                Ac(lambda e: e.activation(out=enc[:], in_=cc[:], func=AF.Exp, scale=-1.0), [b_cc], [b_enc])
                Dv(lambda e: e.tensor_tensor(out=ecm[:], in0=cc[:], in1=lw[:], op=ALU.subtract), [b_cc, b_lw], [b_ecm])
                Ac(lambda e: e.activation(out=ecm[:], in_=ecm[:], func=AF.Exp), [b_ecm], [b_ecm])
                Dv(lambda e: e.tensor_tensor(out=tm1[:], in0=kk[:], in1=ecm[:], op=ALU.mult), [b_kk, b_ecm, b_tm1], [b_tm1])
                Dv(lambda e: e.tensor_scalar(out=ARt[:, :, 0:128], in0=tm1[:], scalar1=-1.0, scalar2=None, op0=ALU.mult), [b_tm1], [bAR])
                Dv(lambda e: e.tensor_tensor(out=ARt[:, :, 128:256], in0=zr_, in1=ec[:], op=ALU.mult), [bzp, b_ec], [bAR])
                Dv(lambda e: e.tensor_tensor(out=tm2[:], in0=kp[:], in1=enc[:], op=ALU.mult), [b_kp, b_enc, b_tm2], [b_tm2])
                Dv(lambda e: e.tensor_tensor(out=tm3[:], in0=kk[:], in1=sg_a[:], op=ALU.mult), [b_kk, b_a], [b_tm3])
                Dv(lambda e: e.tensor_tensor(out=tm3[:], in0=tm3[:], in1=enc[:], op=ALU.mult), [b_tm3, b_enc], [b_tm3])
                Ac(lambda e: e.activation(out=ktl[:], in_=tm2[:], func=AF.Copy), [b_tm2], [bktl])
                Ac(lambda e: e.activation(out=btl[:], in_=tm3[:], func=AF.Copy), [b_tm3], [bbtl])
                def c_body(c, S_, R_):
                    vA, vB, bvv, ktm_, btm_, bktm_, bbtm_ = (S_[k] for k in ('vA', 'vB', 'bvv', 'ktm_', 'btm_', 'bktm_', 'bbtm_'))
                    NBm, KAm, bNB, bKA, Xm, Ym, bXm, bYm = (S_[k] for k in ('NBm', 'KAm', 'bNB', 'bKA', 'Xm', 'Ym', 'bXm', 'bYm'))
                    Ttm, Tnm, bTt, bTn, Ttf, Tnf = (S_[k] for k in ('Ttm', 'Tnm', 'bTt', 'bTn', 'Ttf', 'Tnf'))
                    WpA, WpB, bWp, UpA, UpB, bUp, t128, bt128 = (S_[k] for k in ('WpA', 'WpB', 'bWp', 'UpA', 'UpB', 'bUp', 't128', 'bt128'))
                    Dv = lambda fn, r, w: R_.op('dve', fn, reads=r, writes=w)
                    Ac = lambda fn, r, w: R_.op('act', fn, reads=r, writes=w)
                    Pe = lambda fn, r, w: R_.op('pe', fn, reads=r, writes=w)
                    bH, bHb, byfm = bHc[c], bHbc[c], byfmc[c]
                    ps, bp = R_.ps_next()
                    Pe(lambda e, ps=ps, c=c: e.transpose(ps[:, 0:128], zp[:, 12 + c, :], ident[:]), [bzp, bconst], [bp])
                    Ac(lambda e, ps=ps: e.activation(out=vA[:, 0:64], in_=ps[:, 0:64], func=AF.Copy), [bp], [bvv])
                    Ac(lambda e, ps=ps: e.activation(out=vB[:, 64:128], in_=ps[:, 64:128], func=AF.Copy), [bp], [bvv])
                    ps, bp = R_.ps_next()
                    Pe(lambda e, ps=ps, c=c: e.transpose(ps[:, 0:128], tm2[:, c, :], ident[:]), [b_tm2, bconst], [bp])
                    Ac(lambda e, ps=ps: e.activation(out=ktm_[:], in_=ps[:, 0:128], func=AF.Copy), [bp], [bktm_])
                    ps, bp = R_.ps_next()
                    Pe(lambda e, ps=ps, c=c: e.transpose(ps[:, 0:128], tm3[:, c, :], ident[:]), [b_tm3, bconst], [bp])
                    Ac(lambda e, ps=ps: e.activation(out=btm_[:], in_=ps[:, 0:128], func=AF.Copy), [bp], [bbtm_])
                    def hp_chain(hp, R_, c=c, mi=mi, NLEV=NLEV):
                        Dv = lambda fn, r, w: R_.op('dve', fn, reads=r, writes=w)
                        Ac = lambda fn, r, w: R_.op('act', fn, reads=r, writes=w)
                        Pe = lambda fn, r, w: R_.op('pe', fn, reads=r, writes=w)
                        r0 = 64 * hp
                        ps, bp = R_.ps_next()
                        Pe(lambda e, ps=ps, c=c, r0=r0: e.matmul(ps[:, 0:256], lhsT=btl[r0:r0 + 64, c, :], rhs=ARt[r0:r0 + 64, c, :], start=True, stop=True),
                           [bbtl, bAR], [bp])
                        Dv(lambda e, ps=ps, hp=hp, mi=mi: e.tensor_tensor(out=NBm[hp][:], in0=ps[:, 0:256], in1=rwm[:, mi, 0:256], op=ALU.mult), [bp, bconst], [bNB[hp]])
                        ps, bp = R_.ps_next()
                        Pe(lambda e, ps=ps, c=c, r0=r0: e.matmul(ps[:, 0:256], lhsT=ktl[r0:r0 + 64, c, :], rhs=ARt[r0:r0 + 64, c, :], start=True, stop=True),
                           [bktl, bAR], [bp])
                        Dv(lambda e, ps=ps, hp=hp, mi=mi: e.tensor_tensor(out=KAm[hp][:], in0=ps[:, 0:256], in1=rwm[:, mi, 0:256], op=ALU.mult), [bp, bconst], [bKA[hp]])
                        ps, bp = R_.ps_next()
                        Pe(lambda e, ps=ps, c=c, r0=r0: e.matmul(ps[:, 0:128], lhsT=ARt[r0:r0 + 64, c, 0:128], rhs=btl[r0:r0 + 64, c, :], start=True, stop=True),
                           [bbtl, bAR], [bp])
                        Dv(lambda e, ps=ps, hp=hp, mi=mi: e.tensor_tensor(out=Ym[hp][0][:], in0=ps[:, 0:128], in1=rwm[:, mi, 256:384], op=ALU.mult), [bp, bconst], [bYm[hp][0]])
                        X0 = NBm[hp][:, 0:128]
                        Dv(lambda e, hp=hp, X0=X0: e.tensor_tensor(out=Ttf[hp][:], in0=X0, in1=ident[:], op=ALU.add), [bNB[hp], bconst], [bTt[hp]])
                        Dv(lambda e, hp=hp: e.tensor_tensor(out=Tnf[hp][:], in0=Ym[hp][0][:], in1=ident[:], op=ALU.add), [bYm[hp][0], bconst], [bTn[hp]])
                        Ac(lambda e, hp=hp: e.activation(out=Tnm[hp][:], in_=Tnf[hp][:], func=AF.Copy), [bTn[hp]], [bTn[hp]])
                        Xp, bXp = X0, bNB[hp]
                        Yp, bYp = Ym[hp][0][:], bYm[hp][0]
                        for lev in range(1, NLEV + 1):
                            cur = lev % 2
                            last = (lev == NLEV)
                            ps, bp = R_.ps_next()
                            Pe(lambda e, ps=ps, Xp=Xp, Yp=Yp: e.matmul(ps[:, 0:128], lhsT=Yp, rhs=Xp, start=True, stop=True), [bXp, bYp], [bp])
                            Xc, bXc = Xm[hp][cur][:], bXm[hp][cur]
                            Ac(lambda e, ps=ps, Xc=Xc: e.activation(out=Xc, in_=ps[:, 0:128], func=AF.Copy), [bp], [bXc])
                            if not last:
                                ps2, bp2 = R_.ps_next()
                                Pe(lambda e, ps2=ps2, Xp=Xp, Yp=Yp: e.matmul(ps2[:, 0:128], lhsT=Xp, rhs=Yp, start=True, stop=True), [bXp, bYp], [bp2])
                                Yc, bYc = Ym[hp][cur][:], bYm[hp][cur]
                                Ac(lambda e, ps2=ps2, Yc=Yc: e.activation(out=Yc, in_=ps2[:, 0:128], func=AF.Copy), [bp2], [bYc])
                            ps3, bp3 = R_.ps_next()
                            Pe(lambda e, ps3=ps3, hp=hp, Xc=Xc: e.matmul(ps3[:, 0:128], lhsT=Tnm[hp][:], rhs=Xc, start=True, stop=True), [bTn[hp], bXc], [bp3])
                            if last:
                                Dv(lambda e, ps3=ps3, hp=hp: e.tensor_tensor(out=Ttm[hp][:], in0=ps3[:, 0:128], in1=Ttf[hp][:], op=ALU.add), [bp3, bTt[hp]], [bTt[hp]])
                            else:
                                Dv(lambda e, ps3=ps3, hp=hp: e.tensor_tensor(out=Ttf[hp][:], in0=ps3[:, 0:128], in1=Ttf[hp][:], op=ALU.add), [bp3, bTt[hp]], [bTt[hp]])
                            if not last:
                                ps4, bp4 = R_.ps_next()
                                Pe(lambda e, ps4=ps4, hp=hp, Xc=Xc: e.matmul(ps4[:, 0:128], lhsT=Xc, rhs=Tnm[hp][:], start=True, stop=True), [bTn[hp], bXc], [bp4])
                                Dv(lambda e, ps4=ps4, hp=hp: e.tensor_tensor(out=Tnf[hp][:], in0=ps4[:, 0:128], in1=Tnf[hp][:], op=ALU.add), [bp4, bTn[hp]], [bTn[hp]])
                                Ac(lambda e, hp=hp: e.activation(out=Tnm[hp][:], in_=Tnf[hp][:], func=AF.Copy), [bTn[hp]], [bTn[hp]])
                                Xp, bXp, Yp, bYp = Xc, bXc, Yc, bYc

                    nbk = len(R_.banks) // 2
                    rq0, rq1 = Rec(R_.banks[:nbk]), Rec(R_.banks[nbk:])
                    hp_chain(0, rq0)
                    hp_chain(1, rq1)
                    merge([rq0, rq1], R_)
                    if samp:
                        R_.dma('sp', lambda e, c=c: e.dma_start(out=Hs[:], in_=rwh0_d[:, l, c]), writes=[bHs])
                        Ac(lambda e: e.activation(out=Hsb[:], in_=Hs[:], func=AF.Copy), [bHs], [bHsb])
                    psW, bpW = R_.ps_next()
                    if not samp:
                        Pe(lambda e, psW=psW, c=c: e.matmul(psW[:, 0:128], lhsT=ARt[:, c, 0:128], rhs=Hbb[:, c, :], start=True, stop=False), [bAR, bHb], [bpW])
                    else:
                        wsel = t128[0]
                        for g4 in range(4):
                            psq, bpq = R_.ps_next()
                            Pe(lambda e, psq=psq, c=c, g4=g4: e.matmul(psq[:, :], lhsT=ARt[:, c, 0:128], rhs=Hsb[:, 4 * g4:4 * g4 + 4, :].rearrange("p a b -> p (a b)"),
                                                                      start=True, stop=True), [bAR, bHsb], [bpq])
                            Dv(lambda e, psq=psq, g4=g4: e.tensor_tensor(out=UmA[:, 4 * g4:4 * g4 + 4, :].rearrange("p a b -> p (a b)"), in0=psq[:, :],
                                                                       in1=bm16[:, 4 * g4:4 * g4 + 4, :].rearrange("p a b -> p (a b)"), op=ALU.mult), [bpq, bconst], [bUm])
                        Dv(lambda e: e.tensor_reduce(out=t128[0][:], in_=UmA[:].rearrange("p n v -> p v n"), axis=mybir.AxisListType.X, op=ALU.add), [bUm], [bt128[0]])
                        Ac(lambda e: e.activation(out=WpA[:, 0:64], in_=t128[0][:, 0:64], func=AF.Copy), [bt128[0]], [bWp])
                        Ac(lambda e: e.activation(out=WpB[:, 64:128], in_=t128[0][:, 64:128], func=AF.Copy), [bt128[0]], [bWp])
                    for hp, vv in ((0, vA), (1, vB)):
                        Pe(lambda e, psW=psW, hp=hp, vv=vv, st_=(samp and hp == 0): e.matmul(psW[:, 0:128], lhsT=KAm[hp][:, 0:128], rhs=vv[:], start=st_, stop=(hp == 1)),
                           [bKA[hp], bvv], [bpW])
                    if not samp:
                        Ac(lambda e, psW=psW: e.activation(out=WpA[:, 0:64], in_=psW[:, 0:64], func=AF.Copy), [bpW], [bWp])
                        Ac(lambda e, psW=psW: e.activation(out=WpB[:, 64:128], in_=psW[:, 64:128], func=AF.Copy), [bpW], [bWp])
                    else:
                        Dv(lambda e, psW=psW: e.tensor_tensor(out=t128[0][:], in0=psW[:, 0:128], in1=t128[0][:], op=ALU.add), [bpW, bt128[0], bWp], [bt128[0]])
                        Ac(lambda e: e.activation(out=WpA[:, 0:64], in_=t128[0][:, 0:64], func=AF.Copy), [bt128[0]], [bWp])
                        Ac(lambda e: e.activation(out=WpB[:, 64:128], in_=t128[0][:, 64:128], func=AF.Copy), [bt128[0]], [bWp])
                    psU, bpU = R_.ps_next()
                    for hp, ww in ((0, WpA), (1, WpB)):
                        Pe(lambda e, psU=psU, hp=hp, ww=ww: e.matmul(psU[:, 0:128], lhsT=Ttm[hp][:], rhs=ww[:], start=(hp == 0), stop=(hp == 1)), [bTt[hp], bWp], [bpU])
                    Ac(lambda e, psU=psU: e.activation(out=UpA[:, 0:64], in_=psU[:, 0:64], func=AF.Copy), [bpU], [bUp])
                    Ac(lambda e, psU=psU: e.activation(out=UpB[:, 64:128], in_=psU[:, 64:128], func=AF.Copy), [bpU], [bUp])
                    psY, bpY = R_.ps_next()
                    first = True
                    if not samp:
                        Pe(lambda e, psY=psY, c=c: e.matmul(psY[:, 0:128], lhsT=Hbb[:, c, :], rhs=ARt[:, c, 128:256], start=True, stop=False), [bHb, bAR], [bpY])
                        first = False
                    for hp, uu, vv in ((0, UpA, vA), (1, UpB, vB)):
                        Pe(lambda e, psY=psY, hp=hp, uu=uu, first=first: e.matmul(psY[:, 0:128], lhsT=uu[:], rhs=NBm[hp][:, 128:256], start=first, stop=False), [bUp, bNB[hp]], [bpY])
                        first = False
                        Pe(lambda e, psY=psY, hp=hp, vv=vv, sp_=(hp == 1 and not samp): e.matmul(psY[:, 0:128], lhsT=vv[:], rhs=KAm[hp][:, 128:256], start=False, stop=sp_), [bvv, bKA[hp]], [bpY])
                    if samp:
                        for sn in range(NS):
                            Pe(lambda e, psY=psY, sn=sn, c=c: e.matmul(psY[:, sn * 8:sn * 8 + 8], lhsT=Hsb[:, sn, :], rhs=ARt[:, c, 128 + sn * 8:128 + sn * 8 + 8],
                                                                      start=False, stop=(sn == NS - 1)), [bHsb, bAR], [bpY])
                    Ac(lambda e, psY=psY, c=c: e.activation(out=yfm[:, c, :], in_=psY[:, 0:128], func=AF.Copy), [bpY], [byfm])
                    if not samp:
                        psD, bpD = R_.ps_next()
                        Pe(lambda e, psD=psD: e.matmul(psD[:, 0:128], lhsT=btm_[:], rhs=UpA[:], start=True, stop=False), [bbtm_, bUp], [bpD])
                        Pe(lambda e, psD=psD: e.matmul(psD[:, 0:128], lhsT=btm_[:], rhs=UpB[:], start=False, stop=False), [bbtm_, bUp], [bpD])
                        Pe(lambda e, psD=psD: e.matmul(psD[:, 0:128], lhsT=ktm_[:], rhs=vA[:], start=False, stop=False), [bktm_, bvv], [bpD])
                        Pe(lambda e, psD=psD: e.matmul(psD[:, 0:128], lhsT=ktm_[:], rhs=vB[:], start=False, stop=True), [bktm_, bvv], [bpD])
                        Dv(lambda e, psD=psD: e.tensor_tensor(out=t128[1][:], in0=psD[:, 0:128], in1=blk64[:], op=ALU.mult), [bpD, bconst], [bt128[1]])
                        Dv(lambda e, c=c: e.tensor_tensor(out=t128[1][:], in0=t128[1][:], in1=Hblk[:, c, :], op=ALU.add), [bt128[1], bH], [bt128[1]])
                        Dv(lambda e, c=c: e.tensor_scalar(out=Hblk[:, c, :], in0=t128[1][:], scalar1=ec[:, c, 127:128], scalar2=None, op0=ALU.mult), [bt128[1], b_ec], [bH])
                        Ac(lambda e, c=c: e.activation(out=Hbb[:, c, :], in_=Hblk[:, c, :], func=AF.Copy), [bH], [bHb])
                        if t0 + 128 == cfg.tp:
                            R_.dma('sp', lambda e, c=c: e.dma_start(out=rwhp_d[:, l, c], in_=Hblk[:, c, :]), reads=[bH])
                    else:
                        for sn in range(NS):
                            Dv(lambda e, sn=sn: e.tensor_tensor(out=UmA[:, sn, :], in0=UpA[:], in1=bm16[:, sn, :], op=ALU.mult), [bUp, bconst, bUm], [bUm])
                            Dv(lambda e, sn=sn: e.tensor_tensor(out=VmA[:, sn, :], in0=UpB[:], in1=bm16[:, sn, :], op=ALU.mult), [bUp, bconst, bVm], [bVm])
                        Dv(lambda e: e.tensor_tensor(out=UmA[:], in0=UmA[:], in1=VmA[:], op=ALU.add), [bUm, bVm], [bUm])
                        for sn in range(NS):
                            Dv(lambda e, sn=sn: e.tensor_tensor(out=VmA[:, sn, :], in0=vA[:], in1=bm16[:, sn, :], op=ALU.mult), [bvv, bconst, bVm], [bVm])
                        for sn in range(NS):
                            Dv(lambda e, sn=sn: e.tensor_tensor(out=Hsb[:, sn, :], in0=vB[:], in1=bm16[:, sn, :], op=ALU.mult), [bvv, bconst, bHsb], [bHsb])
                        Dv(lambda e: e.tensor_tensor(out=VmA[:], in0=VmA[:], in1=Hsb[:], op=ALU.add), [bVm, bHsb], [bVm])
                        for g4 in range(4):
                            psD, bpD = R_.ps_next()
                            Pe(lambda e, psD=psD, g4=g4: e.matmul(psD[:, :], lhsT=btm_[:], rhs=UmA[:, 4 * g4:4 * g4 + 4, :].rearrange("p a b -> p (a b)"), start=True, stop=False),
                               [bbtm_, bUm], [bpD])
                            Pe(lambda e, psD=psD, g4=g4: e.matmul(psD[:, :], lhsT=ktm_[:], rhs=VmA[:, 4 * g4:4 * g4 + 4, :].rearrange("p a b -> p (a b)"), start=False, stop=True),
                               [bktm_, bVm], [bpD])
                            for j4 in range(4):
                                sn = 4 * g4 + j4
                                Dv(lambda e, psD=psD, j4=j4: e.tensor_tensor(out=t128[1][:], in0=psD[:, j4 * 128:(j4 + 1) * 128], in1=blk64[:], op=ALU.mult), [bpD, bconst], [bt128[1]])
                                Dv(lambda e, sn=sn: e.tensor_tensor(out=t128[1][:], in0=t128[1][:], in1=Hs[:, sn, :], op=ALU.add), [bt128[1], bHs], [bt128[1]])
                                Dv(lambda e, sn=sn, c=c: e.tensor_scalar(out=Hs[:, sn, :], in0=t128[1][:], scalar1=ec[:, c, sn * 8 + 7:sn * 8 + 8], scalar2=None, op0=ALU.mult),
                                   [bt128[1], b_ec], [bHs])
                        R_.dma('sp', lambda e, c=c: e.dma_start(out=rwhs_d[:, l, c], in_=Hs[:]), reads=[bHs])
                def n_body(c, S_, R_):
                    t128, bt128, mixo, bmixo = S_['t128'], S_['bt128'], S_['mixr'], S_['bmixr']
                    Dv = lambda fn, r, w: R_.op('dve', fn, reads=r, writes=w)
                    Ac = lambda fn, r, w: R_.op('act', fn, reads=r, writes=w)
                    Pe = lambda fn, r, w: R_.op('pe', fn, reads=r, writes=w)
                    byfm = byfmc[c]
                    psm, bpm = R_.ps_next()
                    Pe(lambda e, psm=psm, c=c: e.matmul(psm[:, 0:128], lhsT=blk64[:], rhs=yfm[:, c, :], start=True, stop=True), [bconst, byfm], [bpm])
                    Dv(lambda e, psm=psm, c=c: e.scalar_tensor_tensor(out=t128[0][:], in0=psm[:, 0:128], scalar=-1.0 / 64, in1=yfm[:, c, :], op0=ALU.mult, op1=ALU.add),
                       [bpm, byfm], [bt128[0]])
                    Dv(lambda e: e.tensor_tensor(out=t128[1][:], in0=t128[0][:], in1=t128[0][:], op=ALU.mult), [bt128[0]], [bt128[1]])
                    psv, bpv = R_.ps_next()
                    Pe(lambda e, psv=psv: e.matmul(psv[:, 0:128], lhsT=blk64[:], rhs=t128[1][:], start=True, stop=True), [bconst, bt128[1]], [bpv])
                    Dv(lambda e, psv=psv: e.tensor_scalar(out=t128[1][:], in0=psv[:, 0:128], scalar1=1.0 / 64, scalar2=64e-5, op0=ALU.mult, op1=ALU.add), [bpv], [bt128[1]])
                    Ac(lambda e: e.activation(out=t128[1][:], in_=t128[1][:], func=AF.Sqrt), [bt128[1]], [bt128[1]])
                    Dv(lambda e: e.reciprocal(out=t128[1][:], in_=t128[1][:]), [bt128[1]], [bt128[1]])
                    Dv(lambda e, c=c: e.scalar_tensor_tensor(out=t128[0][:], in0=t128[0][:], scalar=ln_w[:, c:c + 1], in1=t128[1][:], op0=ALU.mult, op1=ALU.mult),
                       [bt128[0], bt128[1], brwp], [bt128[0]])
                    Dv(lambda e, c=c: e.tensor_scalar(out=t128[0][:], in0=t128[0][:], scalar1=ln_b[:, c:c + 1], scalar2=None, op0=ALU.add), [bt128[0], brwp], [bt128[0]])
                    Dv(lambda e, c=c: e.scalar_tensor_tensor(out=t128[1][:], in0=zp[:, c, :], scalar=r_k[:, c:c + 1], in1=kp[:, c, :], op0=ALU.mult, op1=ALU.mult),
                       [bzp, b_kp, brwp, bt128[1]], [bt128[1]])
                    psb_, bpb = R_.ps_next()
                    Pe(lambda e, psb_=psb_: e.matmul(psb_[:, 0:128], lhsT=blk64[:], rhs=t128[1][:], start=True, stop=True), [bconst, bt128[1]], [bpb])
                    Dv(lambda e, psb_=psb_, c=c: e.tensor_tensor(out=t128[1][:], in0=psb_[:, 0:128], in1=zp[:, 12 + c, :], op=ALU.mult), [bpb, bzp, bt128[1]], [bt128[1]])
                    Dv(lambda e: e.tensor_tensor(out=t128[0][:], in0=t128[0][:], in1=t128[1][:], op=ALU.add), [bt128[0], bt128[1]], [bt128[0]])
                    Dv(lambda e, c=c: e.tensor_tensor(out=mixo[:, 0:128], in0=t128[0][:], in1=gg[:, c, :], op=ALU.mult), [bt128[0], b_gg], [bmixo])
                    R_.dma('sp', lambda e, c=c, t0=t0: e.dma_start(out=msc[512 + 128 * c:512 + 128 * c + 128, t0:t0 + 128], in_=mixo[:, 0:128]),
                          reads=[bmixo], writes=[bmsc_all[bi]])


                if samp:
                    for c in range(6):
                        r_ = Rec(list(range(8)))
                        c_body(c, RWS[0], r_)
                        n_body(c, RWS[0], r_)
                        merge([r_])
                else:
                    for c in range(0, 6, 2):
                        ra_, rb_ = Rec([0, 1, 2, 3]), Rec([4, 5, 6, 7])
                        c_body(c, RWS[0], ra_)
                        n_body(c, RWS[0], ra_)
                        c_body(c + 1, RWS[1], rb_)
                        n_body(c + 1, RWS[1], rb_)
                        merge([ra_, rb_])

        def zero_mix(l, c0, c1):
            P.op('dve', lambda e: e.memset(mixo[:], 0.0), writes=[bmixo])
            for pi, (t0, n) in enumerate(pieces):
                for c in range(c0, c1, 128):
                    P.dma('sp', lambda e, c=c, t0=t0, n=n: e.dma_start(out=msc[c:c + 128, t0:t0 + n], in_=mixo[:, :n]),
                          reads=[bmixo], writes=[bmsc_all[pi]])

        def mixers(l):
            P.barrier()
            if cfg.only is not None:
                zero_mix(l, 0, 2048)
            if cfg.only in (None, 's5'):
                s5(l)
            P.barrier()
            if cfg.only in (None, 'hgrn'):
                hgrn(l)
            P.barrier()
            if cfg.only in (None, 'rwkv'):
                rwkv(l)
            P.barrier()

        nblk = len(cfg.blocks)
        if cfg.mode == 'mix':
            mixers(cfg.mixl)
            nblk = 0
            wst['used'] = len(WSEQ)
        for bi in range(nblk):
            load_x0(bi)
            seg_front(0, bi)
        if cfg.mode != 'mix':
            mixers(0)
        for l in range(1, L if cfg.mode != 'mix' else 1):
            for bi in range(nblk):
                seg_back(l - 1, bi)
                seg_front(l, bi)
            mixers(l)
        for bi in range(nblk):
            seg_back(L - 1, bi)
            final(bi)
        assert wst['used'] == len(WSEQ)
        P.run(es)
    return nc


def host_consts():
    s = np.arange(128)
    m64 = ((s[:, None] // 64 == s[None, :] // 64) & (s[:, None] <= s[None, :])).astype(np.float32)
    m8 = ((s[:, None] // 8 == s[None, :] // 8) & (s[:, None] <= s[None, :])).astype(np.float32)
    bm16 = np.zeros((128, NS, 128), np.float32)
    for n in range(NS):
        bm16[n * 8:(n + 1) * 8, n, :] = 1
    t = np.arange(512)
    cm = np.ones((128, 3, 512), np.float32)
    cm[:, 0, t % 64 == 0] = 0
    cm[:, 1, t % 8 == 0] = 0
    cm[:, 2, t % 128 == 0] = 0
    iota = np.broadcast_to(np.arange(128, dtype=np.float32)[None, :], (128, 128)).copy()
    rwm = np.zeros((128, 2, 384), np.float32)
    for mi, blk in ((0, 128), (1, 8)):
        same = (s[:, None] // blk == s[None, :] // blk)
        rwm[:, mi, 0:128] = same & (s[:, None] < s[None, :])
        rwm[:, mi, 128:256] = same & (s[:, None] <= s[None, :])
        rwm[:, mi, 256:384] = same & (s[:, None] > s[None, :])
    blk64 = (s[:, None] // 64 == s[None, :] // 64).astype(np.float32)
    i6 = np.arange(768)
    cm6 = np.ones((128, 2, 768), np.float32)
    cm6[:, 0, i6 % 128 == 0] = 0
    cm6[:, 1, i6 % 8 == 0] = 0
    rmask2 = np.zeros((128, 2), np.float32)
    pp = np.arange(128)
    rmask2[:, 0] = ((pp % 64) // 32 == 0)
    rmask2[:, 1] = ((pp % 64) // 32 == 1)
    import ml_dtypes
    bf = ml_dtypes.bfloat16
    return dict(ident=np.eye(128, dtype=np.float32), m64=m64, m8=m8, bm16=bm16.astype(bf), cm=cm.astype(bf), iota=iota, rmask2=rmask2, rwm=rwm.astype(bf), blk64=blk64, cm6=cm6.astype(bf))


def prep_hgrn(lb_raw, norm_w):
    L = lb_raw.shape[0]
    return dict(lbraw=np.ascontiguousarray(lb_raw.reshape(L, 6, 128).transpose(2, 1, 0)),
                hnw=np.ascontiguousarray(norm_w.reshape(L, 6, 128).transpose(2, 0, 1)))


def prep_s5(a_re, a_im, log_dt, b_re, b_im, c_re, c_im, d, w_glu, b_glu):
    L = a_re.shape[0]
    f32 = np.float32
    ldt_full = np.repeat(log_dt, 64, axis=1)
    fm = np.stack([a_re.reshape(L, 2048), a_im.reshape(L, 2048), ldt_full], axis=1)
    s5fm = np.ascontiguousarray(fm.reshape(L, 3, 16, 128).transpose(3, 0, 1, 2)).astype(f32)
    p = np.arange(128)
    row = np.zeros((128, L, 3, 4, 128), f32)
    for jq in range(4):
        j = 4 * jq + p // 32
        idx = j[:, None] * 128 + np.arange(128)[None, :]
        for i in range(3):
            row[:, :, i, jq, :] = fm[:, i, :][:, idx].transpose(1, 0, 2)
    s5row = row.reshape(128, L, 12, 128)
    s5b = np.zeros((128, L, 2, 4, 128), f32)
    m = np.arange(128)
    for c, bb in enumerate((b_re, b_im)):
        for jq in range(4):
            for pp_ in range(128):
                j = 4 * jq + pp_ // 32
                gl = (pp_ % 32) // 16
                h = pp_ % 16
                s5b[pp_, :, c, jq, gl * 64:(gl + 1) * 64] = bb[:, 2 * j + gl, :, h]
    s5c = np.zeros((128, L, 2, 16, 128), f32)
    for c, cc in enumerate((c_re, c_im)):
        for j in range(16):
            for gl in range(2):
                m0 = 32 * (j % 4) + 16 * gl
                s5c[gl * 64:(gl + 1) * 64, :, c, j, m0:m0 + 16] = cc[:, 2 * j + gl, :, :].transpose(2, 0, 1)
    s5db = np.ascontiguousarray(np.stack([d, b_glu], axis=1).reshape(L, 2, 4, 128).transpose(3, 0, 1, 2)).astype(f32)
    s5w = np.ascontiguousarray(w_glu.reshape(L, 4, 128, 512).transpose(2, 0, 1, 3)).astype(f32)
    return dict(s5fm=s5fm, s5row=np.ascontiguousarray(s5row), s5b=s5b, s5c=s5c, s5db=s5db, s5w=s5w)


def fm_state_s5(re, im):
    L, N = re.shape[:2]
    x = np.stack([re.reshape(L, N, 16, 128), im.reshape(L, N, 16, 128)], axis=1)
    return np.ascontiguousarray(x.transpose(4, 0, 1, 3, 2)).astype(np.float32)


def unfm_state_s5(x):
    L, N = x.shape[1], x.shape[4]
    y = x.transpose(1, 2, 4, 3, 0).reshape(L, 2, N, 32, 64)
    return np.ascontiguousarray(y[:, 0]), np.ascontiguousarray(y[:, 1])


def prep_rwkv(mu, w0, w2, a0, a2, g2, k_k, k_a, r_k, ln_w, ln_b):
    L = mu.shape[0]
    f32 = np.float32
    cols = [mu.reshape(L, 20, 128)] + [x.reshape(L, 6, 128) for x in (w0, a0, k_k, k_a, r_k.reshape(L, 768), ln_w, ln_b)]
    rwp = np.ascontiguousarray(np.concatenate(cols, axis=1).transpose(2, 0, 1)).astype(f32)
    rwl = np.zeros((128, L, 2, 768), f32)
    rwl[0:64, :, 0, :] = w2.transpose(1, 0, 2)
    rwl[64:128, :, 0, :] = a2.transpose(1, 0, 2)
    rwl[:, :, 1, :] = g2.transpose(1, 0, 2)
    return dict(rwp=rwp, rwl=rwl)


def fm_shift(sh):
    L, N = sh.shape[:2]
    return np.ascontiguousarray(sh.reshape(L, N, 20, 128).transpose(3, 0, 2, 1)).astype(np.float32)


def unfm_shift(x):
    L, N = x.shape[1], x.shape[3]
    return np.ascontiguousarray(x.transpose(1, 3, 2, 0).reshape(L, N, 2560))


def fm_wkv(wkv):
    L, N = wkv.shape[:2]
    out = np.zeros((128, L, 6, N, 128), np.float32)
    w = wkv.reshape(L, N, 6, 2, 64, 64)
    for hp in range(2):
        out[hp * 64:(hp + 1) * 64, :, :, :, hp * 64:(hp + 1) * 64] = w[:, :, :, hp].transpose(4, 0, 2, 1, 3)
    return out


def unfm_wkv(x):
    L, N = x.shape[1], x.shape[3]
    out = np.zeros((L, N, 6, 2, 64, 64), np.float32)
    for hp in range(2):
        blk = x[hp * 64:(hp + 1) * 64, :, :, :, hp * 64:(hp + 1) * 64]
        out[:, :, :, hp] = blk.transpose(1, 3, 2, 4, 0)
    return out.reshape(L, N, 12, 64, 64)


def tile_w(w):
    L, K, N = w.shape
    return np.ascontiguousarray(w.reshape(L, K // 128, 128, N // 512, 512).transpose(0, 3, 2, 1, 4))


_WNAMES = (("wg1", "ffn1_w_gate"), ("wu1", "ffn1_w_up"), ("wd1", "ffn1_w_down"), ("win", "w_in"), ("wout", "w_out"),
           ("wg2", "ffn2_w_gate"), ("wu2", "ffn2_w_up"), ("wd2", "ffn2_w_down"))


def make_inmaps(inp, cfg, ncores, nprompt):
    f32 = np.float32
    L = cfg.depth
    shared = host_consts()
    for k, nm in _WNAMES:
        shared[k] = tile_w(np.asarray(inp[nm], f32)[:L])
    norms = np.concatenate([np.stack([inp["norm_ffn1"][l], inp["norm_mix"][l], inp["norm_ffn2"][l]]) for l in range(L)] + [inp["norm_final"][None]], axis=0)
    shared["nrm"] = np.ascontiguousarray(np.asarray(norms, f32).reshape(3 * L + 1, KT, 128).transpose(2, 0, 1))
    shared.update(prep_hgrn(np.asarray(inp["hgrn_lb_raw"], f32)[:L], np.asarray(inp["hgrn_norm_w"], f32)[:L]))
    shared.update(prep_s5(*(np.asarray(inp[k], f32)[:L] for k in ("s5_a_re", "s5_a_im", "s5_log_dt", "s5_b_re", "s5_b_im", "s5_c_re", "s5_c_im",
                                                                   "s5_d", "s5_w_glu", "s5_b_glu"))))
    shared.update(prep_rwkv(*(np.asarray(inp[k], f32)[:L] for k in ("rwkv_mu", "rwkv_w0", "rwkv_w2", "rwkv_a0", "rwkv_a2", "rwkv_g2", "rwkv_k_k",
                                                                     "rwkv_k_a", "rwkv_r_k", "rwkv_ln_w", "rwkv_ln_b"))))
    maps = []
    for c in range(ncores):
        m = dict(shared)
        sl = slice(NS * c, NS * (c + 1))
        xp = np.asarray(inp["x_prompt"][c % nprompt], f32)
        xs = np.asarray(inp["x_sample"][sl], f32).reshape(NS * TS, D)
        m["xT"] = np.ascontiguousarray(np.concatenate([xp, xs], axis=0).T)
        m["hst"] = np.ascontiguousarray(np.asarray(inp["state_hgrn"], f32)[:L, sl])
        m["s5x0"] = fm_state_s5(np.asarray(inp["state_s5_re"], f32)[:L, sl], np.asarray(inp["state_s5_im"], f32)[:L, sl])
        m["rwsh0"] = fm_shift(np.asarray(inp["state_rwkv_shift"], f32)[:L, sl])
        m["rwh0"] = fm_wkv(np.asarray(inp["state_rwkv_wkv"], f32)[:L, sl])
        maps.append(m)
    return maps


def gather(results, cfg, ncores, nprompt):
    L = cfg.depth
    TP = cfg.tp
    f32 = np.float32
    y_p = np.stack([results[c]["yT"][:, :TP].T for c in range(nprompt)]).astype(f32)
    y_s = np.concatenate([results[c]["yT"][:, TP:].T.reshape(NS, TS, D) for c in range(ncores)]).astype(f32)
    s5p = [unfm_state_s5(results[c]["s5p"][..., None]) for c in range(nprompt)]
    s5s = [unfm_state_s5(results[c]["s5s"]) for c in range(ncores)]
    s5re_p = np.concatenate([a[0] for a in s5p], axis=1)
    s5im_p = np.concatenate([a[1] for a in s5p], axis=1)
    s5re_s = np.concatenate([a[0] for a in s5s], axis=1)
    s5im_s = np.concatenate([a[1] for a in s5s], axis=1)
    sh_p = np.concatenate([unfm_shift(results[c]["rwshp"][..., None]) for c in range(nprompt)], axis=1)
    sh_s = np.concatenate([unfm_shift(results[c]["rwshs"]) for c in range(ncores)], axis=1)
    wkv_p = np.concatenate([unfm_wkv(results[c]["rwhp"][:, :, :, None, :]) for c in range(nprompt)], axis=1)
    wkv_s = np.concatenate([unfm_wkv(results[c]["rwhs"]) for c in range(ncores)], axis=1)
    hg_p = np.stack([results[c]["hgp"] for c in range(nprompt)], axis=1).astype(f32)
    hg_s = np.concatenate([results[c]["hgs"] for c in range(ncores)], axis=1).astype(f32)
    outs = (y_p, y_s, s5re_p, s5im_p, sh_p, wkv_p, hg_p, s5re_s, s5im_s, sh_s, wkv_s, hg_s)
    return tuple(np.ascontiguousarray(o, dtype=f32) for o in outs)


def kernel(**inp):
    cfg = Cfg(depth=4, tp=2048)
    nc = build(cfg)
    maps = make_inmaps(inp, cfg, 8, 4)
    res = run_bass_kernel_spmd(nc, maps, core_ids=list(range(8)))
    return gather(res.results, cfg, 8, 4)
```
